# Optimizing a Trainium2 kernel written in Bass

```python
import math, functools
import jax, jax.numpy as jnp
from jax import lax
import numpy as np

D_MODEL = 1024
BATCH = 16
SEQ = 4096
DEPTH = 4
DEC_BATCH = 8
DEC_SEQ = 32
PAST_LEN = 1024

CHUNK = 64
N_MIXERS = 4
DEEPNORM_ALPHA = (2.0 * DEPTH) ** 0.25
DEEPNORM_BETA = (8.0 * DEPTH) ** -0.25
NORM_EPS = 1e-5
D_FF = 2816

A_HEADS = 8
A_KV_HEADS = 2
A_HEAD_DIM = 128
IDX_HEADS = 8
IDX_DIM = 64
IDX_TOPK_MAX = 256
Q_BLOCK = 128
T5_BUCKETS = 32
T5_MAX_DIST = 128
A_SPLITS = (A_HEADS * A_HEAD_DIM, A_KV_HEADS * A_HEAD_DIM, A_KV_HEADS * A_HEAD_DIM,
            IDX_HEADS * IDX_DIM, IDX_DIM, IDX_HEADS)

BAND_HEADS = 16
BAND_HEAD_DIM = 64
BAND_LEFT_CHUNKS = 8
BAND_WINDOW = BAND_LEFT_CHUNKS * CHUNK
BAND_MAX_REL = 128

SSD_D_INNER = 2 * D_MODEL
SSD_HEAD_DIM = 64
SSD_HEADS = SSD_D_INNER // SSD_HEAD_DIM
SSD_GROUPS = 8
SSD_D_STATE = 128
SSD_CONV = 4
SSD_CONV_DIM = SSD_D_INNER + 2 * SSD_GROUPS * SSD_D_STATE

MLSTM_D_INNER = 2 * D_MODEL
MLSTM_HEADS = 4
MLSTM_HEAD_DIM = MLSTM_D_INNER // MLSTM_HEADS
MLSTM_CONV = 4
MLSTM_QK_BLOCK = 4

kernel_name = "hybrid_streaming_encoder_step"

F32 = jnp.float32


def split_cols(z, sizes):
    cuts = [int(c) for c in np.cumsum(sizes)[:-1]]
    return jnp.split(z, cuts, axis=-1)


def layer_norm(x, g, b):
    xf = x.astype(F32)
    mu = jnp.mean(xf, axis=-1, keepdims=True)
    var = jnp.mean(jnp.square(xf - mu), axis=-1, keepdims=True)
    return ((xf - mu) * lax.rsqrt(var + NORM_EPS) * g + b).astype(x.dtype)


def post_norm(x, sub, g, b):
    return layer_norm(DEEPNORM_ALPHA * x + sub, g, b)


def swiglu_ffn(x, wg, wu, wd):
    return (jax.nn.silu(x @ wg) * (x @ wu)) @ wd


def group_norm(x, g, n_groups, center):
    xf = x.astype(F32)
    shp = xf.shape
    xg = xf.reshape(shp[:-1] + (n_groups, shp[-1] // n_groups))
    if center:
        xg = xg - jnp.mean(xg, axis=-1, keepdims=True)
    xg = xg * lax.rsqrt(jnp.mean(jnp.square(xg), axis=-1, keepdims=True) + NORM_EPS)
    return xg.reshape(shp) * g


def causal_conv(x, buf, w, b):
    width, length = w.shape[0], x.shape[1]
    xp = jnp.concatenate([buf.astype(x.dtype), x], axis=1)
    out = b + xp[:, 0:length] * w[0]
    for j in range(1, width):
        out = out + xp[:, j:j + length] * w[j]
    return out, xp[:, length:]


def chunk_scan(step, state, xs):
    length = xs[0].shape[1]
    if length <= CHUNK:
        return step(state, xs)
    n_chunks = length // CHUNK
    xs_c = tuple(jnp.swapaxes(a.reshape((a.shape[0], n_chunks, CHUNK) + a.shape[2:]), 0, 1) for a in xs)
    state, ys = lax.scan(step, state, xs_c)
    ys = jnp.swapaxes(ys, 0, 1)
    return state, ys.reshape((ys.shape[0], length) + ys.shape[3:])


def t5_bucket(rel):
    half = T5_BUCKETS // 2
    max_exact = half // 2
    ret = jnp.where(rel > 0, half, 0)
    n = jnp.abs(rel)
    nf = jnp.maximum(n, 1).astype(F32)
    large = max_exact + (jnp.log(nf / max_exact) / math.log(T5_MAX_DIST / max_exact)
                         * (half - max_exact)).astype(jnp.int32)
    large = jnp.minimum(large, half - 1)
    return ret + jnp.where(n < max_exact, n, large)


def project_a(x, w_in):
    bsz, length, _ = x.shape
    q, k, v, qi, ki, wi = split_cols(x @ w_in, A_SPLITS)
    return (q.reshape(bsz, length, A_HEADS, A_HEAD_DIM),
            k.reshape(bsz, length, A_KV_HEADS, A_HEAD_DIM),
            v.reshape(bsz, length, A_KV_HEADS, A_HEAD_DIM),
            qi.reshape(bsz, length, IDX_HEADS, IDX_DIM), ki, wi * IDX_HEADS ** -0.5)


def dsa_core(q, qi, wi, k, v, ki, q_pos, k_pos, n_sel, t5_table):
    bsz, t_len = q.shape[0], q.shape[1]
    group = A_HEADS // A_KV_HEADS
    q_chunk = q_pos // CHUNK
    admissible = (k_pos[None, :] // CHUNK) <= q_chunk[:, None]
    idx_rel = jax.nn.relu(jnp.einsum('bthd,bsd->bths', qi, ki).astype(F32) * IDX_DIM ** -0.5)
    score = jnp.einsum('bths,bth->bts', idx_rel, wi.astype(F32))
    score = jnp.where(admissible[None], score, -jnp.inf)
    _, sel = lax.top_k(score, n_sel)
    gather = jax.vmap(lambda rows, ids: rows[ids])
    k_sel = gather(k, sel)
    v_sel = gather(v, sel)
    sel_pos = k_pos[sel]
    valid = (sel_pos // CHUNK) <= q_chunk[None, :, None]
    qg = q.reshape(bsz, t_len, A_KV_HEADS, group, A_HEAD_DIM)
    logits = jnp.einsum('btkgd,btnkd->btkgn', qg, k_sel).astype(F32) * A_HEAD_DIM ** -0.5
    bias = t5_table[t5_bucket(sel_pos - q_pos[None, :, None])]
    bias = jnp.transpose(bias.reshape(bsz, t_len, n_sel, A_KV_HEADS, group), (0, 1, 3, 4, 2))
    logits = jnp.where(valid[:, :, None, None, :], logits + bias, -jnp.inf)
    p = jax.nn.softmax(logits, axis=-1).astype(v.dtype)
    out = jnp.einsum('btkgn,btnkd->btkgd', p, v_sel)
    return out.reshape(bsz, t_len, A_HEADS * A_HEAD_DIM)


def mixer_a_prompt(x, w_in, w_out, t5_table):
    bsz, length, _ = x.shape
    q, k, v, qi, ki, wi = project_a(x, w_in)
    k_pos = jnp.arange(length)
    n_sel = min(IDX_TOPK_MAX, length // 4)

    def block(b):
        s0 = b * Q_BLOCK
        sl = lambda a: lax.dynamic_slice_in_dim(a, s0, Q_BLOCK, axis=1)
        return dsa_core(sl(q), sl(qi), sl(wi), k, v, ki, s0 + jnp.arange(Q_BLOCK), k_pos, n_sel, t5_table)

    out = lax.map(block, jnp.arange(length // Q_BLOCK))
    out = jnp.swapaxes(out, 0, 1).reshape(bsz, length, A_HEADS * A_HEAD_DIM)
    return out @ w_out, k, v, ki


def mixer_a_sample(x, cache_k, cache_v, cache_ki, w_in, w_out, t5_table):
    t_len = x.shape[1]
    q, k, v, qi, ki, wi = project_a(x, w_in)
    past = cache_k.shape[1]
    k_all = jnp.concatenate([cache_k.astype(k.dtype), k], axis=1)
    v_all = jnp.concatenate([cache_v.astype(v.dtype), v], axis=1)
    ki_all = jnp.concatenate([cache_ki.astype(ki.dtype), ki], axis=1)
    n_sel = min(IDX_TOPK_MAX, (past + t_len) // 4)
    out = dsa_core(q, qi, wi, k_all, v_all, ki_all, past + jnp.arange(t_len), jnp.arange(past + t_len),
                   n_sel, t5_table)
    return out @ w_out, k, v, ki


def project_b(x, w_in):
    bsz, length, _ = x.shape
    q, k, v = split_cols(x @ w_in, (BAND_HEADS * BAND_HEAD_DIM,) * 3)
    shp = (bsz, length, BAND_HEADS, BAND_HEAD_DIM)
    return q.reshape(shp), k.reshape(shp), v.reshape(shp)


def band_core(q, k, v, q_pos, k_pos, valid, rel_table):
    logits = jnp.einsum('bthd,bshd->bhts', q, k).astype(F32) * BAND_HEAD_DIM ** -0.5
    rel = jnp.clip(q_pos[:, None] - k_pos[None, :], -BAND_MAX_REL, BAND_MAX_REL) + BAND_MAX_REL
    logits = jnp.where(valid[None, None], logits + rel_table[:, rel][None], -jnp.inf)
    p = jax.nn.softmax(logits, axis=-1).astype(v.dtype)
    return jnp.einsum('bhts,bshd->bthd', p, v)


def mixer_b_prompt(x, w_in, w_out, rel_table):
    bsz, length, _ = x.shape
    q, k, v = project_b(x, w_in)
    pad = ((0, 0), (BAND_WINDOW, 0), (0, 0), (0, 0))
    k_pad, v_pad = jnp.pad(k, pad), jnp.pad(v, pad)
    band = BAND_WINDOW + CHUNK

    def block(c):
        s0 = c * CHUNK
        k_pos = s0 - BAND_WINDOW + jnp.arange(band)
        valid = jnp.broadcast_to(k_pos[None, :] >= 0, (CHUNK, band))
        return band_core(lax.dynamic_slice_in_dim(q, s0, CHUNK, axis=1),
                         lax.dynamic_slice_in_dim(k_pad, s0, band, axis=1),
                         lax.dynamic_slice_in_dim(v_pad, s0, band, axis=1),
                         s0 + jnp.arange(CHUNK), k_pos, valid, rel_table)

    out = lax.map(block, jnp.arange(length // CHUNK))
    out = jnp.swapaxes(out, 0, 1).reshape(bsz, length, BAND_HEADS * BAND_HEAD_DIM)
    keep = min(BAND_WINDOW, length)
    return out @ w_out, k[:, length - keep:], v[:, length - keep:]


def mixer_b_sample(x, cache_k, cache_v, w_in, w_out, rel_table):
    bsz, t_len, _ = x.shape
    q, k, v = project_b(x, w_in)
    past = cache_k.shape[1]
    k_all = jnp.concatenate([cache_k.astype(k.dtype), k], axis=1)
    v_all = jnp.concatenate([cache_v.astype(v.dtype), v], axis=1)
    valid = jnp.ones((t_len, past + t_len), bool)
    out = band_core(q, k_all, v_all, past + jnp.arange(t_len), jnp.arange(past + t_len), valid, rel_table)
    out = out.reshape(bsz, t_len, BAND_HEADS * BAND_HEAD_DIM)
    return out @ w_out, k_all[:, t_len:], v_all[:, t_len:]


def ssd_step(a_head, h0, xs):
    xh, dt, bm, cm = xs
    length = xh.shape[1]
    cum = jnp.cumsum(dt * a_head, axis=1)
    causal = jnp.tril(jnp.ones((length, length), bool))
    seg = cum[:, :, None] - cum[:, None, :]
    decay = jnp.exp(jnp.where(causal[None, :, :, None, None], seg, -jnp.inf))
    xdt = xh * dt[..., None]
    cb = jnp.einsum('btgn,bsgn->btsg', cm, bm)
    y = jnp.einsum('btsg,btsgh,bsghp->btghp', cb, decay, xdt)
    y = y + jnp.exp(cum)[..., None] * jnp.einsum('btgn,bghpn->btghp', cm, h0)
    dec_end = jnp.exp(cum[:, -1:] - cum)
    h_new = (jnp.exp(cum[:, -1])[..., None, None] * h0
             + jnp.einsum('bsgh,bsghp,bsgn->bghpn', dec_end, xdt, bm))
    return h_new, y


def mixer_c(x, ssm0, conv0, w_in, conv_w, conv_b, dt_bias, a_log, d_skip, norm_g, w_out):
    bsz, length, _ = x.shape
    hg = SSD_HEADS // SSD_GROUPS
    z, xbc, dt = split_cols(x @ w_in, (SSD_D_INNER, SSD_CONV_DIM, SSD_HEADS))
    xbc, conv_new = causal_conv(xbc, conv0, conv_w, conv_b)
    xbc = jax.nn.silu(xbc)
    xh, bm, cm = split_cols(xbc, (SSD_D_INNER, SSD_GROUPS * SSD_D_STATE, SSD_GROUPS * SSD_D_STATE))
    xh = xh.reshape(bsz, length, SSD_GROUPS, hg, SSD_HEAD_DIM)
    bm = bm.reshape(bsz, length, SSD_GROUPS, SSD_D_STATE)
    cm = cm.reshape(bsz, length, SSD_GROUPS, SSD_D_STATE)
    dt = jax.nn.softplus(dt.astype(F32) + dt_bias).reshape(bsz, length, SSD_GROUPS, hg)
    a_head = -jnp.exp(a_log.astype(F32)).reshape(SSD_GROUPS, hg)
    h0 = ssm0.astype(F32).reshape(bsz, SSD_GROUPS, hg, SSD_HEAD_DIM, SSD_D_STATE)
    h_new, y = chunk_scan(functools.partial(ssd_step, a_head), h0, (xh, dt, bm, cm))
    y = y + d_skip.reshape(SSD_GROUPS, hg)[..., None] * xh
    y = y.reshape(bsz, length, SSD_D_INNER) * jax.nn.silu(z)
    y = group_norm(y, norm_g, SSD_GROUPS, False)
    out = y.astype(x.dtype) @ w_out
    return out, h_new.reshape(bsz, SSD_HEADS, SSD_HEAD_DIM, SSD_D_STATE), conv_new


def mlstm_step(state, xs):
    c0, n0, m0 = state
    q, k, v, i_pre, log_f = xs
    length = q.shape[1]
    f_cum = jnp.swapaxes(jnp.cumsum(log_f, axis=1), 1, 2)
    i_t = jnp.swapaxes(i_pre, 1, 2)
    causal = jnp.tril(jnp.ones((length, length), bool))
    d_log = jnp.where(causal, f_cum[..., :, None] - f_cum[..., None, :] + i_t[..., None, :], -jnp.inf)
    inter = f_cum + m0[..., None]
    m = jnp.maximum(jnp.max(d_log, axis=-1), inter)
    s = jnp.einsum('bthd,bshd->bhts', q, k).astype(F32) * jnp.exp(d_log - m[..., None])
    w_inter = jnp.exp(inter - m)
    num = (jnp.einsum('bhts,bshv->bthv', s, v)
           + jnp.swapaxes(w_inter, 1, 2)[..., None] * jnp.einsum('bthk,bhkv->bthv', q, c0))
    den = jnp.sum(s, axis=-1) + w_inter * jnp.einsum('bthk,bhk->bht', q, n0)
    h = num / jnp.swapaxes(jnp.maximum(jnp.abs(den), jnp.exp(-m)), 1, 2)[..., None]
    m_end = m[..., -1]
    w_end = jnp.exp(f_cum[..., -1:] - f_cum + i_t - m_end[..., None])
    decay = jnp.exp(f_cum[..., -1] + m0 - m_end)
    c_new = decay[..., None, None] * c0 + jnp.einsum('bhs,bshk,bshv->bhkv', w_end, k, v)
    n_new = decay[..., None] * n0 + jnp.einsum('bhs,bshk->bhk', w_end, k)
    return (c_new, n_new, m_end), h


def mixer_d(x, c0, n0, m0, conv0, w_in, conv_w, conv_b, wq_blk, wk_blk, gate_b, norm_g, w_out):
    bsz, length, _ = x.shape
    xc, v, o, gates = split_cols(x @ w_in, (MLSTM_D_INNER, MLSTM_D_INNER, MLSTM_D_INNER, 2 * MLSTM_HEADS))
    xa, conv_new = causal_conv(xc, conv0, conv_w, conv_b)
    xa = jax.nn.silu(xa).reshape(bsz, length, MLSTM_D_INNER // MLSTM_QK_BLOCK, MLSTM_QK_BLOCK)
    heads = (bsz, length, MLSTM_HEADS, MLSTM_HEAD_DIM)
    q = jnp.einsum('blnc,ncd->blnd', xa, wq_blk).reshape(heads)
    k = jnp.einsum('blnc,ncd->blnd', xa, wk_blk).reshape(heads) * MLSTM_HEAD_DIM ** -0.5
    v = v.reshape(heads)
    gates = gates.astype(F32) + gate_b
    i_pre = gates[..., :MLSTM_HEADS]
    log_f = jax.nn.log_sigmoid(gates[..., MLSTM_HEADS:])
    state = (c0.astype(F32), n0.astype(F32), m0.astype(F32))
    (c_new, n_new, m_new), h = chunk_scan(mlstm_step, state, (q, k, v, i_pre, log_f))
    h = jax.nn.sigmoid(o.astype(F32)).reshape(heads) * h
    h = group_norm(h.reshape(bsz, length, MLSTM_D_INNER), norm_g, MLSTM_HEADS, True)
    return h.astype(x.dtype) @ w_out, c_new, n_new, m_new, conv_new


def setup_inputs(seed: int = 0) -> dict:
    key = jax.random.key(seed)
    ks = iter(jax.random.split(key, 64))

    def nrm(shape, scale):
        return jax.random.normal(next(ks), shape, F32) * scale

    band_rows = min(BAND_WINDOW, PAST_LEN)
    dt0 = jnp.exp(jax.random.uniform(next(ks), (SSD_HEADS,), F32, math.log(1e-3), math.log(1e-1)))
    a_init = jax.random.uniform(next(ks), (SSD_HEADS,), F32, 1.0, 16.0)
    gate_b = jnp.concatenate([nrm((MLSTM_HEADS,), 0.1),
                              jnp.linspace(3.0, 6.0, MLSTM_HEADS, dtype=F32) + nrm((MLSTM_HEADS,), 0.1)])
    a_in = sum(A_SPLITS)
    c_in = SSD_D_INNER + SSD_CONV_DIM + SSD_HEADS
    d_in = 3 * MLSTM_D_INNER + 2 * MLSTM_HEADS
    return {
        "x_prompt": nrm((BATCH, SEQ, D_MODEL), 1.0),
        "x_sample": nrm((DEC_BATCH, DEC_SEQ, D_MODEL), 1.0),
        "cache_a_k": nrm((DEC_BATCH, PAST_LEN, A_KV_HEADS, A_HEAD_DIM), 1.0),
        "cache_a_v": nrm((DEC_BATCH, PAST_LEN, A_KV_HEADS, A_HEAD_DIM), 1.0),
        "cache_a_kidx": nrm((DEC_BATCH, PAST_LEN, IDX_DIM), 1.0),
        "cache_b_k": nrm((DEC_BATCH, band_rows, BAND_HEADS, BAND_HEAD_DIM), 1.0),
        "cache_b_v": nrm((DEC_BATCH, band_rows, BAND_HEADS, BAND_HEAD_DIM), 1.0),
        "state_c_ssm": nrm((DEC_BATCH, SSD_HEADS, SSD_HEAD_DIM, SSD_D_STATE), 0.5),
        "state_c_conv": nrm((DEC_BATCH, SSD_CONV - 1, SSD_CONV_DIM), 1.0),
        "state_d_c": nrm((DEC_BATCH, MLSTM_HEADS, MLSTM_HEAD_DIM, MLSTM_HEAD_DIM), 0.05),
        "state_d_n": nrm((DEC_BATCH, MLSTM_HEADS, MLSTM_HEAD_DIM), 0.05),
        "state_d_m": nrm((DEC_BATCH, MLSTM_HEADS), 1.0),
        "state_d_conv": nrm((DEC_BATCH, MLSTM_CONV - 1, MLSTM_D_INNER), 1.0),
        "a_w_in": nrm((D_MODEL, a_in), D_MODEL ** -0.5),
        "a_w_out": nrm((A_HEADS * A_HEAD_DIM, D_MODEL), (A_HEADS * A_HEAD_DIM) ** -0.5 * DEEPNORM_BETA),
        "t5_table": nrm((T5_BUCKETS, A_HEADS), 0.5),
        "b_w_in": nrm((D_MODEL, 3 * BAND_HEADS * BAND_HEAD_DIM), D_MODEL ** -0.5),
        "b_w_out": nrm((BAND_HEADS * BAND_HEAD_DIM, D_MODEL), (BAND_HEADS * BAND_HEAD_DIM) ** -0.5 * DEEPNORM_BETA),
        "b_rel_table": nrm((BAND_HEADS, 2 * BAND_MAX_REL + 1), 0.5),
        "c_w_in": nrm((D_MODEL, c_in), D_MODEL ** -0.5),
        "c_conv_w": nrm((SSD_CONV, SSD_CONV_DIM), SSD_CONV ** -0.5),
        "c_conv_b": nrm((SSD_CONV_DIM,), 0.02),
        "c_dt_bias": dt0 + jnp.log(-jnp.expm1(-dt0)),
        "c_a_log": jnp.log(a_init),
        "c_d_skip": 1.0 + nrm((SSD_HEADS,), 0.1),
        "c_norm_g": 1.0 + nrm((SSD_D_INNER,), 0.1),
        "c_w_out": nrm((SSD_D_INNER, D_MODEL), SSD_D_INNER ** -0.5 * DEEPNORM_BETA),
        "d_w_in": nrm((D_MODEL, d_in), D_MODEL ** -0.5),
        "d_conv_w": nrm((MLSTM_CONV, MLSTM_D_INNER), MLSTM_CONV ** -0.5),
        "d_conv_b": nrm((MLSTM_D_INNER,), 0.02),
        "d_wq_blk": nrm((MLSTM_D_INNER // MLSTM_QK_BLOCK, MLSTM_QK_BLOCK, MLSTM_QK_BLOCK), MLSTM_QK_BLOCK ** -0.5),
        "d_wk_blk": nrm((MLSTM_D_INNER // MLSTM_QK_BLOCK, MLSTM_QK_BLOCK, MLSTM_QK_BLOCK), MLSTM_QK_BLOCK ** -0.5),
        "d_gate_b": gate_b,
        "d_norm_g": 1.0 + nrm((MLSTM_D_INNER,), 0.1),
        "d_w_out": nrm((MLSTM_D_INNER, D_MODEL), MLSTM_D_INNER ** -0.5 * DEEPNORM_BETA),
        "ffn1_wg": nrm((DEPTH, D_MODEL, D_FF), D_MODEL ** -0.5),
        "ffn1_wu": nrm((DEPTH, D_MODEL, D_FF), D_MODEL ** -0.5),
        "ffn1_wd": nrm((DEPTH, D_FF, D_MODEL), D_FF ** -0.5 * DEEPNORM_BETA),
        "ffn2_wg": nrm((DEPTH, D_MODEL, D_FF), D_MODEL ** -0.5),
        "ffn2_wu": nrm((DEPTH, D_MODEL, D_FF), D_MODEL ** -0.5),
        "ffn2_wd": nrm((DEPTH, D_FF, D_MODEL), D_FF ** -0.5 * DEEPNORM_BETA),
        "ln_g": 1.0 + nrm((DEPTH, 3, D_MODEL), 0.05),
        "ln_b": nrm((DEPTH, 3, D_MODEL), 0.02),
    }


def reference(x_prompt, x_sample, cache_a_k, cache_a_v, cache_a_kidx, cache_b_k, cache_b_v,
              state_c_ssm, state_c_conv, state_d_c, state_d_n, state_d_m, state_d_conv,
              a_w_in, a_w_out, t5_table, b_w_in, b_w_out, b_rel_table,
              c_w_in, c_conv_w, c_conv_b, c_dt_bias, c_a_log, c_d_skip, c_norm_g, c_w_out,
              d_w_in, d_conv_w, d_conv_b, d_wq_blk, d_wk_blk, d_gate_b, d_norm_g, d_w_out,
              ffn1_wg, ffn1_wu, ffn1_wd, ffn2_wg, ffn2_wu, ffn2_wd, ln_g, ln_b):
    bp = x_prompt.shape[0]
    xp, xs = x_prompt, x_sample
    for i in range(DEPTH):
        xp = post_norm(xp, 0.5 * swiglu_ffn(xp, ffn1_wg[i], ffn1_wu[i], ffn1_wd[i]), ln_g[i, 0], ln_b[i, 0])
        xs = post_norm(xs, 0.5 * swiglu_ffn(xs, ffn1_wg[i], ffn1_wu[i], ffn1_wd[i]), ln_g[i, 0], ln_b[i, 0])
        kind = i % N_MIXERS
        if kind == 0:
            mp, a_k_p, a_v_p, a_kidx_p = mixer_a_prompt(xp, a_w_in, a_w_out, t5_table)
            ms, a_k_s, a_v_s, a_kidx_s = mixer_a_sample(xs, cache_a_k, cache_a_v, cache_a_kidx,
                                                        a_w_in, a_w_out, t5_table)
        elif kind == 1:
            mp, b_k_p, b_v_p = mixer_b_prompt(xp, b_w_in, b_w_out, b_rel_table)
            ms, b_k_s, b_v_s = mixer_b_sample(xs, cache_b_k, cache_b_v, b_w_in, b_w_out, b_rel_table)
        elif kind == 2:
            mp, c_ssm_p, c_conv_p = mixer_c(
                xp, jnp.zeros((bp, SSD_HEADS, SSD_HEAD_DIM, SSD_D_STATE), F32),
                jnp.zeros((bp, SSD_CONV - 1, SSD_CONV_DIM), xp.dtype),
                c_w_in, c_conv_w, c_conv_b, c_dt_bias, c_a_log, c_d_skip, c_norm_g, c_w_out)
            ms, c_ssm_s, c_conv_s = mixer_c(
                xs, state_c_ssm, state_c_conv,
                c_w_in, c_conv_w, c_conv_b, c_dt_bias, c_a_log, c_d_skip, c_norm_g, c_w_out)
        else:
            mp, d_c_p, d_n_p, d_m_p, d_conv_p = mixer_d(
                xp, jnp.zeros((bp, MLSTM_HEADS, MLSTM_HEAD_DIM, MLSTM_HEAD_DIM), F32),
                jnp.zeros((bp, MLSTM_HEADS, MLSTM_HEAD_DIM), F32), jnp.zeros((bp, MLSTM_HEADS), F32),
                jnp.zeros((bp, MLSTM_CONV - 1, MLSTM_D_INNER), xp.dtype),
                d_w_in, d_conv_w, d_conv_b, d_wq_blk, d_wk_blk, d_gate_b, d_norm_g, d_w_out)
            ms, d_c_s, d_n_s, d_m_s, d_conv_s = mixer_d(
                xs, state_d_c, state_d_n, state_d_m, state_d_conv,
                d_w_in, d_conv_w, d_conv_b, d_wq_blk, d_wk_blk, d_gate_b, d_norm_g, d_w_out)
        xp = post_norm(xp, mp, ln_g[i, 1], ln_b[i, 1])
        xs = post_norm(xs, ms, ln_g[i, 1], ln_b[i, 1])
        xp = post_norm(xp, 0.5 * swiglu_ffn(xp, ffn2_wg[i], ffn2_wu[i], ffn2_wd[i]), ln_g[i, 2], ln_b[i, 2])
        xs = post_norm(xs, 0.5 * swiglu_ffn(xs, ffn2_wg[i], ffn2_wu[i], ffn2_wd[i]), ln_g[i, 2], ln_b[i, 2])
    return (xp, xs,
            a_k_p, a_v_p, a_kidx_p, b_k_p, b_v_p, c_ssm_p, c_conv_p, d_c_p, d_n_p, d_m_p, d_conv_p,
            a_k_s, a_v_s, a_kidx_s, b_k_s, b_v_s, c_ssm_s, c_conv_s, d_c_s, d_n_s, d_m_s, d_conv_s)
```

```python
import numpy as np
from contextlib import ExitStack
import concourse.bass as bass
import concourse.mybir as mybir
from concourse.bass_utils import run_bass_kernel_spmd

F32 = mybir.dt.float32
BF16 = mybir.dt.bfloat16
ALU = mybir.AluOpType
AF = mybir.ActivationFunctionType

D = 1024
KD = 8
DFF = 2816
KF = 22
SEQ = 4096
NPS = 2
TS = 32
TP = NPS * SEQ
TT = TP + TS
ALPHA = (2.0 * 4) ** 0.25
EPS = 1e-5
NCORES = 8

ENGS = ["pe", "act", "dve", "pool", "sp"]


class _Rec:
    def __getattr__(self, name):
        def f(*a, **k):
            return (name, a, k)
        return f


R = _Rec()


class Tok:
    __slots__ = ("sem", "val", "eng", "idx", "snap", "dma")

    def __init__(self, sem, val, eng, idx, snap, dma):
        self.sem, self.val, self.eng, self.idx, self.snap, self.dma = sem, val, eng, idx, snap, dma


class Prog:
    def __init__(self, nc, es):
        self.nc, self.es = nc, es
        self.streams = {e: [] for e in ENGS}
        self.n = {e: 0 for e in ENGS}
        self.clock = {e: {} for e in ENGS}
        self.lastw = {}
        self.readers = {}
        self.sems = {}
        self.semcnt = {}
        self.last_tok = {e: None for e in ENGS}
        self.dma_last = {}
        self.lmap = {}
        self.free_phys = []
        self.all_phys = []
        self.n_wait = 0

    def sem(self, name):
        if name not in self.sems:
            self.sems[name] = self.es.enter_context(self.nc.semaphore(name))
            self.semcnt[name] = 0
        return self.sems[name]

    def sb(self, name, shape, dt):
        return self.es.enter_context(self.nc.sbuf_tensor(name, list(shape), dt))

    def ps(self, name, shape, dt=F32):
        return self.es.enter_context(self.nc.psum_tensor(name, list(shape), dt))

    def _need(self, eng, tk, waits):
        if tk is None:
            return
        if (not tk.dma) and tk.eng == eng:
            if eng == "pe" or tk.idx < self.n[eng] - 3:
                return
        ck = self.clock[eng]
        if ck.get(tk.sem, 0) >= tk.val:
            return
        waits[tk.sem] = max(waits.get(tk.sem, 0), tk.val)
        for s, v in tk.snap.items():
            if ck.get(s, 0) < v:
                ck[s] = v
        ck[tk.sem] = tk.val

    def op(self, eng, fn, reads=(), writes=(), dma=None, extra=()):
        waits = {}
        psr = [k for k in reads if isinstance(k, str) and k.startswith("ps")]
        if psr:
            reads = [k for k in reads if k not in psr]
            writes = list(writes) + psr
        for k in reads:
            self._need(eng, self.lastw.get(k), waits)
        for k in writes:
            self._need(eng, self.lastw.get(k), waits)
            for tk in self.readers.get(k, {}).values():
                self._need(eng, tk, waits)
        for tk in extra:
            self._need(eng, tk, waits)
        idx = self.n[eng]
        if dma is not None:
            lname = "d_" + dma
            if lname not in self.lmap:
                if self.free_phys:
                    self.lmap[lname] = self.free_phys.pop()
                else:
                    self.lmap[lname] = f"dq{len(self.all_phys)}"
                    self.all_phys.append(self.lmap[lname])
            sname = self.lmap[lname]
            self.sem(sname)
            self.semcnt[sname] += 16
            tk = Tok(sname, self.semcnt[sname], eng, idx, dict(self.clock[eng]), True)
            self.dma_last[sname] = tk
            inc = (sname, 16)
        else:
            sname = "e_" + eng
            self.sem(sname)
            self.n[eng] += 1
            self.semcnt[sname] = self.n[eng]
            tk = Tok(sname, self.n[eng], eng, idx, dict(self.clock[eng]), False)
            inc = (sname, 1)
        self.n_wait += len(waits)
        self.streams[eng].append((list(waits.items()), fn, inc))
        for k in writes:
            self.lastw[k] = tk
            self.readers[k] = {}
        for k in reads:
            self.readers.setdefault(k, {})[(eng, sname)] = tk
        if dma is None:
            self.last_tok[eng] = tk
        return tk

    def barrier(self):
        toks = [t for t in self.last_tok.values() if t is not None] + list(self.dma_last.values())
        for e in ENGS:
            self.op(e, R.nop(), extra=toks)
        self.lastw.clear()
        self.readers.clear()
        self.free_phys = list(self.all_phys)
        self.lmap.clear()

    def emit(self):
        nc = self.nc
        finals = [(s, c) for s, c in self.semcnt.items() if c > 0]
        with nc.Block() as block:
            def run(ename, engine):
                for waits, fn, inc in self.streams[ename]:
                    for s, v in waits:
                        engine.wait_ge(self.sems[s], v)
                    try:
                        ins = getattr(engine, fn[0])(*fn[1], **fn[2])
                    except Exception:
                        print("FAILED OP:", ename, fn[0], fn[1], fn[2])
                        raise
                    ins.then_inc(self.sems[inc[0]], inc[1])
                if ename == "sp":
                    for s, c in finals:
                        engine.wait_ge(self.sems[s], c)

            @block.tensor
            def _(e):
                run("pe", e)

            @block.scalar
            def _(e):
                run("act", e)

            @block.vector
            def _(e):
                run("dve", e)

            @block.gpsimd
            def _(e):
                run("pool", e)

            @block.sync
            def _(e):
                run("sp", e)


class Ctx:
    pass


class Arena:
    def __init__(self, t, nelem):
        self.t, self.nelem, self.off = t, nelem, 0

    def reset(self):
        self.off = 0

    def alloc(self, shape, dt):
        n = int(np.prod(shape))
        ne = n * (2 if dt == F32 else 1)
        ne = (ne + 15) // 16 * 16
        assert self.off + ne <= self.nelem, ("arena overflow", self.off, ne, self.nelem)
        v = self.t[:, self.off:self.off + (n * 2 if dt == F32 else n)]
        self.off += ne
        if dt == F32:
            v = v.bitcast(F32)
        if len(shape) == 2:
            return v.rearrange("p (a b) -> p a b", a=shape[0])
        if len(shape) == 3:
            return v.rearrange("p (a b c) -> p a b c", a=shape[0], b=shape[1])
        return v


def setup_common(P, C):
    C.A = Arena(P.sb("arena", [128, ARENA_N], BF16), ARENA_N)
    C.ones = P.sb("ones_bf", [128, 128], BF16)
    C.lng = P.sb("lng", [128, 12, KD], F32)
    C.lnb = P.sb("lnb", [128, 12, KD], F32)
    C.epsc = P.sb("epsc", [128, 1], F32)
    C.PSALL = P.ps("psall", [128, 4096], F32)
    C.PSB = [C.PSALL[:, i * 512:(i + 1) * 512] for i in range(8)]
    C.onec = P.sb("onec", [128, 1], F32)
    P.op("pool", R.memset(C.onec[:], 1.0), writes=["onec"])
    P.op("pool", R.memset(C.ones[:], 1.0 / D), writes=["ones"])
    P.op("pool", R.memset(C.epsc[:], EPS), writes=["epsc"])
    P.op("sp", R.dma_start(out=C.lng[:], in_=C.d_lng),
         writes=["lng"], dma="lng")
    P.op("sp", R.dma_start(out=C.lnb[:], in_=C.d_lnb),
         writes=["lnb"], dma="lnb")


def load_w(P, C, name, dram_ap, kin, nout):
    view = C.A.alloc([kin, nout], BF16)
    src = dram_ap.rearrange("(k p) n -> p k n", p=128)
    for k in range(kin):
        for c0 in range(0, nout, WSTEP):
            c1 = min(nout, c0 + WSTEP)
            P.op("pool", R.dma_start(out=view[:, k, c0:c1], in_=src[:, k, c0:c1]),
                 writes=[(name, k, c0)], dma=f"w_{name[:2]}_{k % 4}")
    return view


WSTEP = 1024


def wkeys(name, k, c0, c1):
    return [(name, k, c) for c in range((c0 // WSTEP) * WSTEP, c1, WSTEP)]


def ln_bufs(P, C):
    L = Ctx()
    L.MEAN = C.A.alloc([512], F32)
    L.MSQ = C.A.alloc([512], F32)
    L.RSTD = C.A.alloc([512], F32)
    L.TMP = [C.A.alloc([512], F32) for _ in range(2)]
    return L


def postnorm_tile(P, C, L, lnidx, zk, Z, ZB, zbk, ZQ, zqk, N):
    ps1, ps2 = C.PSB[6], C.PSB[7]
    P.op("act", R.activation(out=ZB[:, :, :N], in_=Z[:, :, :N], func=AF.Copy),
         reads=[zk], writes=zbk)
    P.op("act", R.activation(out=ZQ[:, :, :N], in_=Z[:, :, :N], func=AF.Square),
         reads=[zk], writes=zqk)
    for m in range(KD):
        P.op("pe", R.matmul(ps1[:, :N], lhsT=C.ones[:], rhs=ZB[:, m, :N],
                                             start=(m == 0), stop=(m == KD - 1)),
             reads=["ones"] + zbk, writes=["ps6"])
    for m in range(KD):
        P.op("pe", R.matmul(ps2[:, :N], lhsT=C.ones[:], rhs=ZQ[:, m, :N],
                                             start=(m == 0), stop=(m == KD - 1)),
             reads=["ones"] + zqk, writes=["ps7"])
    P.op("act", R.activation(out=L.MEAN[:, :N], in_=ps1[:, :N], func=AF.Copy),
         reads=["ps6"], writes=["mean"])
    P.op("act", R.activation(out=L.MSQ[:, :N], in_=ps1[:, :N], func=AF.Square),
         reads=["ps6"], writes=["msq"])
    P.op("dve", R.tensor_tensor(out=L.RSTD[:, :N], in0=ps2[:, :N], in1=L.MSQ[:, :N], op=ALU.subtract),
         reads=["ps7", "msq"], writes=["rstd"])
    P.op("act", R.activation(out=L.RSTD[:, :N], in_=L.RSTD[:, :N], func=AF.Sqrt, bias=C.epsc[:, 0:1]),
         reads=["rstd", "epsc"], writes=["rstd"])
    P.op("dve", R.reciprocal(out=L.RSTD[:, :N], in_=L.RSTD[:, :N]), reads=["rstd"], writes=["rstd"])
    for m in range(KD):
        T = L.TMP[m % 2]
        tk = ("lntmp", m % 2)
        P.op("dve", R.tensor_tensor(out=T[:, :N], in0=Z[:, m, :N], in1=L.MEAN[:, :N],
                                                       op=ALU.subtract),
             reads=[zk, "mean"], writes=[tk])
        P.op("dve", R.tensor_tensor(out=T[:, :N], in0=T[:, :N], in1=L.RSTD[:, :N], op=ALU.mult),
             reads=[tk, "rstd"], writes=[tk])
        P.op("act", R.activation(out=Z[:, m, :N], in_=T[:, :N], func=AF.Identity,
                                                     bias=C.lnb[:, lnidx, m:m + 1], scale=C.lng[:, lnidx, m:m + 1]),
             reads=[tk, "lng", "lnb"], writes=[zk])


def tiles_of(with_sample=True):
    t = [(i * 512, 512) for i in range(TP // 512)]
    if with_sample:
        t.append((TP, TS))
    return t


def ffn_phase(P, C, pid, Xin, Xout, wg, wu, wd, lnidx, tiles):
    P.barrier()
    C.A.reset()
    WG = load_w(P, C, f"wg{pid}", wg, KD, DFF)
    WU = load_w(P, C, f"wu{pid}", wu, KD, DFF)
    WD = load_w(P, C, f"wd{pid}", wd, KF, D)
    X = C.A.alloc([KD, 512], F32)
    XBs = [C.A.alloc([KD, 512], BF16) for _ in range(2)]
    H = C.A.alloc([KF, 512], BF16)
    SGs = [C.A.alloc([512], BF16) for _ in range(2)]
    L = ln_bufs(P, C)
    xin = Xin.rearrange("(k p) t -> p k t", p=128)
    xout = Xout.rearrange("(k p) t -> p k t", p=128)
    xk = "x"
    for ti, (t0, N) in enumerate(tiles):
        s = ti % 2
        XB = XBs[s]
        xbk = ("xb", s)
        P.op("pool", R.dma_start(out=XB[:, :, :N], in_=xin[:, :, t0:t0 + N]),
             writes=[xbk], dma=f"xb{s}")
        for j in range(KF):
            b = j % 2
            pg, pu = C.PSB[b], C.PSB[2 + b]
            for k in range(KD):
                P.op("pe", R.matmul(
                    pg[:, :N], lhsT=WG[:, k, j * 128:(j + 1) * 128], rhs=XB[:, k, :N], start=(k == 0), stop=(k == KD - 1)),
                     reads=[xbk] + wkeys(f"wg{pid}", k, j * 128, (j + 1) * 128), writes=[f"ps{b}"])
            for k in range(KD):
                P.op("pe", R.matmul(
                    pu[:, :N], lhsT=WU[:, k, j * 128:(j + 1) * 128], rhs=XB[:, k, :N], start=(k == 0), stop=(k == KD - 1)),
                     reads=[xbk] + wkeys(f"wu{pid}", k, j * 128, (j + 1) * 128), writes=[f"ps{2 + b}"])
            SG = SGs[b]
            P.op("act", R.activation(out=SG[:, :N], in_=pg[:, :N], func=AF.Silu),
                 reads=[f"ps{b}"], writes=[("sg", b)])
            P.op("dve", R.tensor_tensor(out=H[:, j, :N], in0=pu[:, :N],
                                                                        in1=SG[:, :N], op=ALU.mult),
                 reads=[f"ps{2 + b}", ("sg", b)], writes=[("h", j)])
        P.op("sp", R.dma_start(out=X[:, :, :N], in_=xin[:, :, t0:t0 + N]),
             writes=[xk], dma="x")
        P.op("pool", R.tensor_scalar(out=X[:, :, :N], in0=X[:, :, :N], scalar1=ALPHA,
                                                    scalar2=None, op0=ALU.mult),
             reads=[xk], writes=[xk])
        for m in range(KD):
            b = m % 2
            py = C.PSB[4 + b]
            for j in range(KF):
                P.op("pe", R.matmul(
                    py[:, :N], lhsT=WD[:, j, m * 128:(m + 1) * 128], rhs=H[:, j, :N], start=(j == 0), stop=(j == KF - 1)),
                     reads=[("h", j)] + wkeys(f"wd{pid}", j, m * 128, (m + 1) * 128), writes=[f"ps{4 + b}"])
            P.op("dve", R.scalar_tensor_tensor(
                out=X[:, m, :N], in0=py[:, :N], scalar=0.5, in1=X[:, m, :N], op0=ALU.mult, op1=ALU.add),
                 reads=[f"ps{4 + b}", xk], writes=[xk])
        postnorm_tile(P, C, L, lnidx, xk, X, XB, [xbk], H[:, 0:KD, :], [("h", j) for j in range(KD)], N)
        P.op("sp", R.dma_start(out=xout[:, :, t0:t0 + N], in_=X[:, :, :N]),
             reads=[xk], dma="xst")


ARENA_N = 102400
SC_B = 64.0 ** -0.5
NEG = -30000.0


def outproj_phase(P, C, name, AO, kin, w_out, Xin, Xout, lnidx, tiles, rowscale=None):
    P.barrier()
    C.A.reset()
    W = load_w(P, C, name, w_out, kin, D)
    if rowscale is not None:
        RSC = C.A.alloc([kin], F32)
        P.op("sp", R.dma_start(out=RSC[:], in_=rowscale), writes=["rsc"], dma="rsc")
        for j in range(kin):
            P.op("pool", R.tensor_scalar(out=W[:, j, :], in0=W[:, j, :], scalar1=RSC[:, j:j + 1], scalar2=None, op0=ALU.mult),
                 reads=["rsc"] + wkeys(name, j, 0, D), writes=wkeys(name, j, 0, D))
    X = C.A.alloc([KD, 512], F32)
    AOs = [C.A.alloc([kin, 512], BF16) for _ in range(2)]
    ZB = C.A.alloc([KD, 512], BF16)
    ZQ = C.A.alloc([KD, 512], BF16)
    L = ln_bufs(P, C)
    xin = Xin.rearrange("(k p) t -> p k t", p=128)
    xout = Xout.rearrange("(k p) t -> p k t", p=128)
    ao = AO.rearrange("(k p) t -> p k t", p=128)
    xk = "x"
    for ti, (t0, N) in enumerate(tiles):
        s = ti % 2
        A_ = AOs[s]
        ak = ("ao", s)
        P.op("sp", R.dma_start(out=A_[:, :, :N], in_=ao[:, :, t0:t0 + N]),
             writes=[ak], dma=f"ao{s}")
        P.op("sp", R.dma_start(out=X[:, :, :N], in_=xin[:, :, t0:t0 + N]),
             writes=[xk], dma="x")
        P.op("pool", R.tensor_scalar(out=X[:, :, :N], in0=X[:, :, :N], scalar1=ALPHA,
                                                    scalar2=None, op0=ALU.mult),
             reads=[xk], writes=[xk])
        for m in range(KD):
            b = m % 2
            py = C.PSB[4 + b]
            for j in range(kin):
                P.op("pe", R.matmul(
                    py[:, :N], lhsT=W[:, j, m * 128:(m + 1) * 128], rhs=A_[:, j, :N], start=(j == 0), stop=(j == kin - 1)),
                     reads=[ak] + wkeys(name, j, m * 128, (m + 1) * 128), writes=[f"ps{4 + b}"])
            P.op("dve", R.tensor_tensor(
                out=X[:, m, :N], in0=py[:, :N], in1=X[:, m, :N], op=ALU.add),
                 reads=[f"ps{4 + b}", xk], writes=[xk])
        postnorm_tile(P, C, L, lnidx, xk, X, ZB, ["zb"], ZQ, ["zq"], N)
        P.op("sp", R.dma_start(out=xout[:, :, t0:t0 + N], in_=X[:, :, :N]),
             reads=[xk], dma="xst")


def inproj_generic(P, C, name, Xin, w_in, nout, tiles, fm_specs, tm_specs):
    W = load_w(P, C, name, w_in, KD, nout)
    XBs = [C.A.alloc([KD, 512], BF16) for _ in range(2)]
    xin = Xin.rearrange("(k p) t -> p k t", p=128)
    bank = [0]
    for ti, (t0, N) in enumerate(tiles):
        s = ti % 2
        XB = XBs[s]
        xbk = ("xb", s)
        P.op("pool", R.dma_start(out=XB[:, :, :N], in_=xin[:, :, t0:t0 + N]),
             writes=[xbk], dma=f"xb{s}")
        for (c0, nc_, cb) in fm_specs:
            b = bank[0] % 6
            bank[0] += 1
            ps = C.PSB[b]
            for k in range(KD):
                P.op("pe", R.matmul(
                    ps[:nc_, :N], lhsT=W[:, k, c0:c0 + nc_], rhs=XB[:, k, :N], start=(k == 0), stop=(k == KD - 1)),
                     reads=[xbk] + wkeys(name, k, c0, c0 + nc_), writes=[f"ps{b}"])
            cb(ps[:nc_, :N], f"ps{b}", t0, N)
        for tb in range((N + 127) // 128):
            nt = min(128, N - tb * 128)
            for (c0, nc_, cb) in tm_specs:
                b = bank[0] % 6
                bank[0] += 1
                ps = C.PSB[b]
                for k in range(KD):
                    P.op("pe", R.matmul(
                        ps[:nt, :nc_], lhsT=XB[:, k, tb * 128:tb * 128 + nt], rhs=W[:, k, c0:c0 + nc_],
                        start=(k == 0), stop=(k == KD - 1)),
                         reads=[xbk] + wkeys(name, k, c0, c0 + nc_), writes=[f"ps{b}"])
                cb(ps[:nt, :nc_], f"ps{b}", t0 + tb * 128, nt)


class Stager:
    def __init__(self, P, C, name, shape, dt, n=3):
        self.P, self.name, self.n, self.i = P, name, n, 0
        self.bufs = [C.A.alloc(shape, dt) for _ in range(n)]

    def put(self, ps_ap, pskey, dst_ap, rows, cols, eng="act", func=None):
        P = self.P
        s = self.i % self.n
        self.i += 1
        B = self.bufs[s]
        k = (self.name, s)
        if eng == "act":
            P.op("act", R.activation(out=B[:rows, :cols], in_=ps_ap, func=(func or AF.Copy)), reads=[pskey], writes=[k])
        else:
            P.op("dve", R.tensor_copy(out=B[:rows, :cols], in_=ps_ap), reads=[pskey], writes=[k])
        P.op("sp", R.dma_start(out=dst_ap, in_=B[:rows, :cols]), reads=[k], dma=f"{self.name}{s}")
        return B, k


def mixer_b(P, C, Xin, Xout, lnidx):
    nc = P.nc
    S, NP_, TPc, TTc = C.SEQ, C.NPS, C.TP, C.TT
    keep = min(512, S)
    tiles = C.tiles
    P.barrier()
    C.A.reset()
    QT, KT, V = C.sB_QT, C.sB_KT, C.sB_V
    st_fm = Stager(P, C, "stfm", [512], BF16, 4)
    st_tm = Stager(P, C, "sttm", [512], BF16, 4)
    st_o = Stager(P, C, "sto", [512], F32, 4)

    def kcol(t0):
        return t0 + 512 if t0 >= TPc else t0

    def out_rows(t0, nt):
        if t0 >= TPc:
            return C.d_bks[480:480 + nt, :], C.d_bvs[480:480 + nt, :]
        sq, tl = divmod(t0, S)
        if tl >= S - keep:
            r = tl - (S - keep)
            return C.d_bkp[sq, r:r + nt, :], C.d_bvp[sq, r:r + nt, :]
        return None

    fm = []
    for c in range(8):
        fm.append((c * 128, 128, lambda ps, pk, t0, N, c=c: st_fm.put(ps, pk, QT[c * 128:(c + 1) * 128, t0:t0 + N], 128, N)))
    for c in range(8):
        fm.append((1024 + c * 128, 128, lambda ps, pk, t0, N, c=c: st_fm.put(
            ps, pk, KT[c * 128:(c + 1) * 128, kcol(t0):kcol(t0) + N], 128, N, eng="dve")))
    tm = []
    for hf in range(2):
        def cbv(ps, pk, t0, nt, hf=hf):
            st_tm.put(ps, pk, V[kcol(t0):kcol(t0) + nt, hf * 512:(hf + 1) * 512], nt, 512, eng="dve")
            o = out_rows(t0, nt)
            if o is not None:
                st_o.put(ps, pk, o[1][:, hf * 512:(hf + 1) * 512], nt, 512)
        tm.append((2048 + hf * 512, 512, cbv))

        def cbk(ps, pk, t0, nt, hf=hf):
            o = out_rows(t0, nt)
            if o is not None:
                st_o.put(ps, pk, o[0][:, hf * 512:(hf + 1) * 512], nt, 512)
        tm.append((1024 + hf * 512, 512, cbk))
    inproj_generic(P, C, "bwin", Xin, C.d_b_w_in, 3072, tiles, fm, tm)
    P.op("pool", R.dma_start(out=KT[:, TPc:TPc + 512], in_=C.d_cbkT), dma="cbk")
    P.op("pool", R.dma_start(out=V[TPc:TPc + 512, :], in_=C.d_cbk_v), dma="cbv")
    P.op("sp", R.dma_start(out=C.d_bks[0:480, :], in_=C.d_cbk_k[32:512, :]), dma="cbk2")
    P.op("sp", R.dma_start(out=C.d_bvs[0:480, :], in_=C.d_cbk_v[32:512, :]), dma="cbv2")

    P.barrier()
    C.A.reset()
    A = C.A
    BT = A.alloc([16, 2, 128], F32)
    FAR = A.alloc([16], F32)
    ONE_E = A.alloc([128], BF16)
    ONE_O = A.alloc([128], BF16)
    NR = 6
    KB = [A.alloc([8, 128], BF16) for _ in range(NR)]
    VB = [A.alloc([16, 128], BF16) for _ in range(NR)]
    QE = [A.alloc([8, 128], BF16) for _ in range(2)]
    QO = [A.alloc([8, 128], BF16) for _ in range(2)]
    AOt = [A.alloc([8, 128], BF16) for _ in range(2)]
    EN = [A.alloc([2, 128], BF16) for _ in range(2)]
    EF = [A.alloc([3, 128], BF16) for _ in range(2)]
    LG = [A.alloc([2, 128], F32) for _ in range(2)]
    RD = [A.alloc([128], F32) for _ in range(2)]
    P.op("sp", R.dma_start(out=BT[:], in_=C.d_b_bias), writes=["bt"], dma="bt")
    P.op("sp", R.dma_start(out=FAR[:], in_=C.d_b_far), writes=["far"], dma="far")
    P.op("pool", R.memset(BT[64:128, :, 0, 0:64], NEG), reads=["bt"], writes=["bt"])
    P.op("pool", R.memset(ONE_E[:, 0:64], 1.0), writes=["one_e"])
    P.op("pool", R.memset(ONE_E[:, 64:128], 0.0), writes=["one_e"])
    P.op("pool", R.memset(ONE_O[:, 0:64], 0.0), writes=["one_o"])
    P.op("pool", R.memset(ONE_O[:, 64:128], 1.0), writes=["one_o"])
    for r in range(NR):
        P.op("pool", R.memset(VB[r][:], 0.0), writes=[("vb", r)])
    for r in range(2):
        P.op("pool", R.memset(QE[r][:], 0.0), writes=[("qe", r)])
        P.op("pool", R.memset(QO[r][:], 0.0), writes=[("qo", r)])
        P.op("pool", R.memset(EF[r][:], 0.0), writes=[("ef", r)])
    ao_d = C.sAO[0:1024, :].rearrange("(k p) t -> p k t", p=128)
    qt_d = QT.rearrange("(k p) t -> p k t", p=128)
    kt_d = KT.rearrange("(k p) t -> p k t", p=128)
    seqs = [(sq * S, sq * S, S // 128, 128, 0) for sq in range(NP_)]
    seqs.append((TPc, TPc, 1, TS, 4))
    bank = 0
    qi_ = 0
    for (q0, k0, nqb, qw, ib0) in seqs:
        loaded = {}

        def load_kv(j, nk):
            r = j % NR
            P.op("sp", R.dma_start(out=KB[r][:, :, :nk], in_=kt_d[:, :, k0 + j * 128:k0 + j * 128 + nk]),
                 writes=[("kb", r)], dma=f"kb{r}")
            for par in range(2):
                src = V[k0 + j * 128:k0 + j * 128 + nk, :].rearrange("s (h p d) -> s h p d", p=2, d=64)[:, :, par, :]
                P.op("sp", R.dma_start(out=VB[r].rearrange("p (c q) d -> p c q d", q=2)[:nk, :, par, par * 64:(par + 1) * 64], in_=src),
                     writes=[("vb", r)], dma=f"vb{r}")
        for ib in range(nqb):
            i = ib + ib0
            s2 = qi_ % 2
            qi_ += 1
            tq = q0 + ib * 128 if qw == 128 else q0 + 512
            qcol = q0 + ib * 128
            P.op("sp", R.dma_start(out=QE[s2][0:64, :, :qw], in_=qt_d[0:64, :, qcol:qcol + qw]),
                 writes=[("qe", s2)], dma=f"qe{s2}")
            P.op("sp", R.dma_start(out=QO[s2][64:128, :, :qw], in_=qt_d[64:128, :, qcol:qcol + qw]),
                 writes=[("qo", s2)], dma=f"qo{s2}")
            jlist = list(range(max(0, i - 4), i + 1))
            for j in jlist:
                if j not in loaded:
                    nk = TS if (qw != 128 and j == 4) else 128
                    load_kv(j, nk)
                    loaded[j] = nk
            for c in range(8):
                po = C.PSB[4 + c % 2]
                pd = C.PSB[6 + c % 2]
                pok, pdk = f"ps{4 + c % 2}", f"ps{6 + c % 2}"
                first = True
                for par in range(2):
                    h = 2 * c + par
                    Qh = (QE if par == 0 else QO)[s2]
                    qk = ("qe" if par == 0 else "qo", s2)
                    near = [j for j in jlist if i - j <= 1]
                    far = [j for j in jlist if i - j >= 2]
                    pn, pf = C.PSB[2 * par], C.PSB[2 * par + 1]
                    pnk, pfk = f"ps{2 * par}", f"ps{2 * par + 1}"
                    pnv = pn[:, 0:256].rearrange("p (d t) -> p d t", d=2)
                    pfv = pf[:, 0:384].rearrange("p (d t) -> p d t", d=3)
                    for j in jlist:
                        d = i - j
                        nk = loaded[j]
                        dst = pnv[:nk, d, :qw] if d <= 1 else pfv[:nk, d - 2, :qw]
                        P.op("pe", R.matmul(
                            dst, lhsT=KB[j % NR][:, c, :nk], rhs=Qh[:, c, :qw], start=True, stop=True),
                             reads=[("kb", j % NR), qk], writes=[pnk if d <= 1 else pfk])
                    ENp, LGp, EFp = EN[par], LG[par], EF[par]
                    for j in near:
                        d = i - j
                        nk = loaded[j]
                        P.op("dve", R.scalar_tensor_tensor(
                            out=LGp[:nk, d, :qw], in0=pnv[:nk, d, :qw], scalar=SC_B,
                            in1=BT[:nk, h, d, :qw], op0=ALU.mult, op1=ALU.add),
                             reads=[pnk, "bt"], writes=[("lg", par)])
                        P.op("act", R.activation(
                            out=ENp[:nk, d, :qw], in_=LGp[:nk, d, :qw], func=AF.Exp),
                             reads=[("lg", par)], writes=[("en", par)])
                    if far:
                        dlo, dhi = min(i - j for j in far), max(i - j for j in far)
                        if dhi == 4 and qw == 128:
                            P.op("act", R.activation(
                                out=EFp[:, 0:2, :], in_=pfv[:, 0:2, :], func=AF.Exp,
                                bias=FAR[:, h:h + 1], scale=SC_B), reads=[pfk, "far"], writes=[("ef", par)])
                            P.op("act", R.activation(
                                out=EFp[:, 2, 0:64], in_=pfv[:, 2, 0:64], func=AF.Exp,
                                bias=FAR[:, h:h + 1], scale=SC_B), reads=[pfk, "far"], writes=[("ef", par)])
                            P.op("act", R.activation(
                                out=EFp[64:128, 2, 64:128], in_=pfv[64:128, 2, 64:128], func=AF.Exp,
                                bias=FAR[64:128, h:h + 1], scale=SC_B), reads=[pfk, "far"], writes=[("ef", par)])
                        else:
                            P.op("act", R.activation(
                                out=EFp[:, dlo - 2:dhi - 1, :qw], in_=pfv[:, dlo - 2:dhi - 1, :qw],
                                func=AF.Exp, bias=FAR[:, h:h + 1], scale=SC_B), reads=[pfk, "far"], writes=[("ef", par)])
                    ONE = ONE_E if par == 0 else ONE_O
                    onek = "one_e" if par == 0 else "one_o"
                    blocks = [(j, ENp, ("en", par), i - j) for j in near] + [(j, EFp, ("ef", par), i - j - 2) for j in far]
                    for bi, (j, E_, ek, slot) in enumerate(blocks):
                        nk = loaded[j]
                        last = (par == 1 and bi == len(blocks) - 1)
                        P.op("pe", R.matmul(
                            po[:, :qw], lhsT=VB[j % NR][:nk, h, :], rhs=E_[:nk, slot, :qw], start=first, stop=last),
                             reads=[("vb", j % NR), ek], writes=[pok])
                        P.op("pe", R.matmul(
                            pd[:, :qw], lhsT=ONE[:nk, :], rhs=E_[:nk, slot, :qw], start=first, stop=last),
                             reads=[onek, ek], writes=[pdk])
                        first = False
                r2 = c % 2
                P.op("dve", R.reciprocal(out=RD[r2][:, :qw], in_=pd[:, :qw]),
                     reads=[pdk], writes=[("rd", r2)])
                P.op("dve", R.tensor_tensor(
                    out=AOt[s2][:, c, :qw], in0=po[:, :qw], in1=RD[r2][:, :qw], op=ALU.mult),
                     reads=[pok, ("rd", r2)], writes=[("aot", s2)])
            P.op("sp", R.dma_start(out=ao_d[:, :, qcol:qcol + qw], in_=AOt[s2][:, :, :qw]),
                 reads=[("aot", s2)], dma=f"aot{s2}")
    outproj_phase(P, C, "bwout", C.sAO[0:1024, :], 8, C.d_b_w_out, Xin, Xout, lnidx, tiles)


SC_A = 128.0 ** -0.5
C_IDX = (64.0 ** -0.5) * (8.0 ** -0.5)
NBIS = 22


def mixer_a(P, C, Xin, Xout, lnidx):
    S, NP_, TPc, TTc = C.SEQ, C.NPS, C.TP, C.TT
    tiles = C.tiles
    PAST = 1024
    P.barrier()
    C.A.reset()
    QT, KT, V, QIT, KIT, WI = C.sA_QT, C.sA_KT, C.sA_V, C.sA_QIT, C.sA_KIT, C.sA_WI
    st_fm = Stager(P, C, "stfm", [512], BF16, 4)
    st_tm = Stager(P, C, "sttm", [256], BF16, 3)
    st_o = Stager(P, C, "sto", [512], F32, 4)

    def kcol(t0):
        return t0 + PAST if t0 >= TPc else t0

    def orow(t0, nt, dp, ds):
        if t0 >= TPc:
            return ds[0:nt, :]
        sq, tl = divmod(t0, S)
        return dp[sq, tl:tl + nt, :]

    fm = []
    for c in range(8):
        fm.append((c * 128, 128, lambda ps, pk, t0, N, c=c: st_fm.put(ps, pk, QT[c * 128:(c + 1) * 128, t0:t0 + N], 128, N)))
    for c in range(2):
        fm.append((1024 + c * 128, 128, lambda ps, pk, t0, N, c=c: st_fm.put(
            ps, pk, KT[c * 128:(c + 1) * 128, kcol(t0):kcol(t0) + N], 128, N, eng="dve")))
    for c in range(4):
        fm.append((1536 + c * 128, 128, lambda ps, pk, t0, N, c=c: st_fm.put(ps, pk, QIT[c * 128:(c + 1) * 128, t0:t0 + N], 128, N)))
    fm.append((2048, 64, lambda ps, pk, t0, N: st_fm.put(ps, pk, KIT[:, kcol(t0):kcol(t0) + N], 64, N, eng="dve")))

    def cb_kv(ps, pk, t0, nt):
        st_o.put(ps[:, 0:256], pk, orow(t0, nt, C.d_akp, C.d_aks), nt, 256)
        st_o.put(ps[:, 256:512], pk, orow(t0, nt, C.d_avp, C.d_avs), nt, 256, eng="dve")
        st_tm.put(ps[:, 256:512], pk, V[kcol(t0):kcol(t0) + nt, :], nt, 256)

    def cb_iw(ps, pk, t0, nt):
        st_o.put(ps[:, 0:64], pk, orow(t0, nt, C.d_aip, C.d_ais), nt, 64)
        st_o.put(ps[:, 64:72], pk, WI[t0:t0 + nt, :], nt, 8, eng="dve")
    tm = [(1024, 512, cb_kv), (2048, 72, cb_iw)]
    inproj_generic(P, C, "awin", Xin, C.d_a_w_in, 2120, tiles, fm, tm)
    P.op("pool", R.dma_start(out=KT[:, TPc:TPc + PAST], in_=C.d_cakT), dma="cak")
    P.op("pool", R.dma_start(out=V[TPc:TPc + PAST, :], in_=C.d_cav), dma="cav")
    P.op("pool", R.dma_start(out=KIT[:, TPc:TPc + PAST], in_=C.d_cakiT), dma="caki")

    P.barrier()
    C.A.reset()
    A = C.A
    NBmax = max(S, PAST + 128) // 128
    KTs = A.alloc([2, NBmax * 128], BF16)
    Vs = A.alloc([NBmax, 256], BF16)
    KITs = A.alloc([NBmax * 128], BF16)
    SC = A.alloc([NBmax * 128], F32)
    JUNK = A.alloc([NBmax * 128], BF16)
    MK = A.alloc([NBmax * 128], BF16)
    MT = A.alloc([NBmax, 128], BF16)
    BTF = A.alloc([3, 8, 128], F32)
    BTA = A.alloc([3, 8, 128], BF16)
    IDN = A.alloc([128], BF16)
    ONES1 = A.alloc([128], BF16)
    QTb = [A.alloc([8, 128], BF16) for _ in range(2)]
    QIb = [A.alloc([8, 128], BF16) for _ in range(2)]
    WIb = [A.alloc([8], F32) for _ in range(2)]
    RL = [A.alloc([512], F32) for _ in range(3)]
    EX = [A.alloc([512], BF16) for _ in range(3)]
    AOt = [A.alloc([8, 128], BF16) for _ in range(2)]
    RD = [A.alloc([512], F32) for _ in range(2)]
    LO = A.alloc([1], F32)
    MID = A.alloc([1], F32)
    CNT = A.alloc([1], F32)
    PRD = A.alloc([1], F32)
    P.op("sp", R.dma_start(out=BTF[:], in_=C.d_a_bias), writes=["btf"], dma="bt")
    P.op("act", R.activation(out=BTA[:], in_=BTF[:], func=AF.Copy, scale=1.0 / SC_A), reads=["btf"], writes=["bta"])
    P.op("pool", R.dma_start(out=IDN[:], in_=C.d_ident), writes=["idn"], dma="idn")
    P.op("pool", R.memset(ONES1[:], 1.0), writes=["ones1"])
    ao_d = C.sAO[0:1024, :].rearrange("(k p) t -> p k t", p=128)
    qt_d = QT.rearrange("(k p) t -> p k t", p=128)
    qit_d = QIT.rearrange("(h p) t -> p h t", p=64)
    kt_d = KT.rearrange("(g p) t -> p g t", p=128)
    seqs = [(sq * S, sq * S, S // 128, 128, 0, S) for sq in range(NP_)]
    seqs.append((TPc, TPc, 1, TS, PAST // 128, PAST + TS))
    qi_ = 0
    rl_i = 0
    ex_i = 0
    for (q0, k0, nqb, qw, ib0, nkeys) in seqs:
        nblk = (nkeys + 127) // 128
        P.op("sp", R.dma_start(out=KTs[:, :, :nkeys], in_=kt_d[:, :, k0:k0 + nkeys]), writes=["kts"], dma="kts")
        P.op("sp", R.dma_start(out=KITs[0:64, :nkeys], in_=KIT[:, k0:k0 + nkeys]), writes=["kits"], dma="kits")
        nfull = nkeys // 128
        P.op("sp", R.dma_start(out=Vs[:, :nfull, :], in_=V[k0:k0 + nfull * 128, :].rearrange("(j s) c -> s j c", s=128)),
             writes=["vs"], dma="vs")
        if nkeys % 128:
            rem = nkeys % 128
            P.op("sp", R.dma_start(out=Vs[:rem, nfull, :], in_=V[k0 + nfull * 128:k0 + nkeys, :]), writes=["vs"], dma="vs")
        for ib in range(nqb):
            i = ib + ib0
            s2 = qi_ % 2
            qi_ += 1
            qcol = q0 + ib * 128
            nk = min(nkeys, 128 * (i + 1))
            nb = (nk + 127) // 128
            P.op("sp", R.dma_start(out=QTb[s2][:, :, :qw], in_=qt_d[:, :, qcol:qcol + qw]), writes=[("qtb", s2)], dma=f"qtb{s2}")
            P.op("sp", R.dma_start(out=QIb[s2][0:64, :, :qw], in_=qit_d[:, :, qcol:qcol + qw]), writes=[("qib", s2)], dma=f"qib{s2}")
            P.op("sp", R.dma_start(out=WIb[s2][:qw, :], in_=WI[qcol:qcol + qw, :]), writes=[("wib", s2)], dma=f"wib{s2}")
            for st in range((nk + 511) // 512):
                c0 = st * 512
                n_ = min(512, nk - c0)
                for h in range(8):
                    b = (st * 8 + h) % 2
                    ps = C.PSB[b]
                    P.op("pe", R.matmul(ps[:qw, :n_], lhsT=QIb[s2][0:64, h, :qw], rhs=KITs[0:64, c0:c0 + n_], start=True, stop=True),
                         reads=[("qib", s2), "kits"], writes=[f"ps{b}"])
                    rb = rl_i % 3
                    rl_i += 1
                    P.op("act", R.activation(out=RL[rb][:qw, :n_], in_=ps[:qw, :n_], func=AF.Relu, scale=C_IDX),
                         reads=[f"ps{b}"], writes=[("rl", rb)])
                    if h == 0:
                        P.op("dve", R.tensor_scalar(out=SC[:qw, c0:c0 + n_], in0=RL[rb][:qw, :n_], scalar1=WIb[s2][:qw, 0:1],
                                                    scalar2=None, op0=ALU.mult),
                             reads=[("rl", rb), ("wib", s2)], writes=[("sc", st)])
                    else:
                        P.op("dve", R.scalar_tensor_tensor(out=SC[:qw, c0:c0 + n_], in0=RL[rb][:qw, :n_], scalar=WIb[s2][:qw, h:h + 1],
                                                           in1=SC[:qw, c0:c0 + n_], op0=ALU.mult, op1=ALU.add),
                             reads=[("rl", rb), ("wib", s2), ("sc", st)], writes=[("sc", st)])
            sck = [("sc", st) for st in range((nk + 511) // 512)]
            if qw == 128:
                P.op("pool", R.memset(SC[0:64, nk - 64:nk], -1.0e30), reads=sck, writes=sck)
            P.op("pool", R.memset(LO[:qw, :], -64.0), writes=["lo"])
            for it in range(NBIS):
                step = 64.0 / (2 ** it)
                P.op("dve", R.tensor_scalar(out=MID[:qw, :], in0=LO[:qw, :], scalar1=step, scalar2=None, op0=ALU.add),
                     reads=["lo"], writes=["mid"])
                P.op("dve", R.tensor_scalar(out=JUNK[:qw, :nk], in0=SC[:qw, :nk], scalar1=MID[:qw, 0:1], scalar2=0.0,
                                            op0=ALU.is_ge, op1=ALU.add, accum_out=CNT[:qw, :]),
                     reads=sck + ["mid"], writes=["junk", "cnt"])
                P.op("dve", R.tensor_scalar(out=PRD[:qw, :], in0=CNT[:qw, :], scalar1=255.5, scalar2=step,
                                            op0=ALU.is_ge, op1=ALU.mult),
                     reads=["cnt"], writes=["prd"])
                P.op("dve", R.tensor_tensor(out=LO[:qw, :], in0=LO[:qw, :], in1=PRD[:qw, :], op=ALU.add),
                     reads=["lo", "prd"], writes=["lo"])
            P.op("dve", R.tensor_scalar(out=MK[:qw, :nk], in0=SC[:qw, :nk], scalar1=LO[:qw, 0:1], scalar2=NEG,
                                        op0=ALU.is_lt, op1=ALU.mult),
                 reads=sck + ["lo"], writes=["mk"])
            pT = C.PSALL.bitcast(BF16)[:, 2048:3072]
            for j0 in range(0, nb, 4):
                jn = min(4, nb - j0)
                for j in range(j0, j0 + jn):
                    w_ = min(128, nk - j * 128)
                    P.op("pe", R.transpose(out=pT[:w_, (j - j0) * 128:(j - j0) * 128 + qw], in_=MK[:qw, j * 128:j * 128 + w_],
                                           identity=IDN[:qw, :qw]),
                         reads=["mk", "idn"], writes=["ps2"])
                if nk - j0 * 128 >= jn * 128:
                    P.op("act", R.activation(out=MT[:, j0:j0 + jn, :qw],
                                             in_=pT[:, 0:jn * 128].rearrange("p (j t) -> p j t", j=jn)[:, :, :qw], func=AF.Copy),
                         reads=["ps2"], writes=[("mt", j0 // 4)])
                else:
                    for j in range(j0, j0 + jn):
                        w_ = min(128, nk - j * 128)
                        P.op("act", R.activation(out=MT[:w_, j, :qw], in_=pT[:w_, (j - j0) * 128:(j - j0) * 128 + qw], func=AF.Copy),
                             reads=["ps2"], writes=[("mt", j0 // 4)])
            for g in range(2):
                po, pd = C.PSB[4 + g], C.PSB[6 + g]
                pok, pdk = f"ps{4 + g}", f"ps{6 + g}"
                NQ = 4 * qw
                for j in range(nb):
                    w_ = min(128, nk - j * 128)
                    d = i - j
                    dc = min(d, 2)
                    b = j % 2
                    pl = C.PSB[b]
                    plv = pl[:, :].rearrange("p (h t) -> p h t", h=4)
                    P.op("pe", R.matmul(plv[:w_, :, :qw], lhsT=KTs[:, g, j * 128:j * 128 + w_], rhs=QTb[s2][:, 4 * g:4 * g + 4, :qw],
                                        start=True, stop=False),
                         reads=["kts", ("qtb", s2)], writes=[f"ps{b}"])
                    P.op("pe", R.matmul(plv[:w_, :, :qw], lhsT=IDN[:w_, :w_], rhs=BTA[:w_, dc, 4 * g:4 * g + 4, :qw],
                                        start=False, stop=False),
                         reads=["idn", "bta"], writes=[f"ps{b}"])
                    for hh in range(4):
                        P.op("pe", R.matmul(plv[:w_, hh, :qw], lhsT=IDN[:w_, :w_], rhs=MT[:w_, j, :qw], start=False, stop=(hh == 3)),
                             reads=["idn", ("mt", j // 4)], writes=[f"ps{b}"])
                    eb = ex_i % 3
                    ex_i += 1
                    EXv = EX[eb][:, :].rearrange("p (h t) -> p h t", h=4)
                    P.op("act", R.activation(out=EXv[:w_, :, :qw], in_=plv[:w_, :, :qw], func=AF.Exp, scale=SC_A),
                         reads=[f"ps{b}"], writes=[("ex", eb)])
                    pov = po[:, :].rearrange("p (h t) -> p h t", h=4)
                    pdv = pd[:, :].rearrange("p (h t) -> p h t", h=4)
                    P.op("pe", R.matmul(pov[:, :, :qw], lhsT=Vs[:w_, j, g * 128:(g + 1) * 128], rhs=EXv[:w_, :, :qw],
                                        start=(j == 0), stop=(j == nb - 1)),
                         reads=["vs", ("ex", eb)], writes=[pok])
                    P.op("pe", R.matmul(pdv[:, :, :qw], lhsT=ONES1[:w_, :], rhs=EXv[:w_, :, :qw],
                                        start=(j == 0), stop=(j == nb - 1)),
                         reads=["ones1", ("ex", eb)], writes=[pdk])
                RDv = RD[g][:, :].rearrange("p (h t) -> p h t", h=4)
                P.op("dve", R.reciprocal(out=RDv[:, :, :qw], in_=pdv[:, :, :qw]), reads=[pdk], writes=[("rd", g)])
                P.op("dve", R.tensor_tensor(out=AOt[s2][:, 4 * g:4 * g + 4, :qw], in0=pov[:, :, :qw], in1=RDv[:, :, :qw], op=ALU.mult),
                     reads=[pok, ("rd", g)], writes=[("aot", s2)])
            P.op("sp", R.dma_start(out=ao_d[:, :, qcol:qcol + qw], in_=AOt[s2][:, :, :qw]), reads=[("aot", s2)], dma=f"aot{s2}")
    outproj_phase(P, C, "awout", C.sAO[0:1024, :], 8, C.d_a_w_out, Xin, Xout, lnidx, tiles)


def mixer_c(P, C, Xin, Xout, lnidx):
    S, NP_, TPc, TTc = C.SEQ, C.NPS, C.TP, C.TT
    tiles = C.tiles
    ZT, XC, XTM, DTS = C.sC_ZT, C.sC_XC, C.sC_XTM, C.sC_DTS
    P.barrier()
    C.A.reset()
    A = C.A
    st_z = Stager(P, C, "stz", [512], BF16, 3)
    st_t = Stager(P, C, "stt", [512], BF16, 3)
    HALO = A.alloc([32, 3], F32)
    CW = A.alloc([32, 4], F32)
    CB = A.alloc([32], F32)
    DTB = A.alloc([32], F32)
    ANEG = A.alloc([32], F32)
    IDN = A.alloc([128], BF16)
    XP = [A.alloc([515], F32) for _ in range(3)]
    ACC = [A.alloc([512], F32) for _ in range(2)]
    XS = [A.alloc([4, 512], BF16) for _ in range(2)]
    DTT = [A.alloc([96], F32) for _ in range(2)]
    P.op("sp", R.dma_start(out=CW[:], in_=C.d_c_cw), writes=["cw"], dma="cw")
    P.op("sp", R.dma_start(out=CB[:], in_=C.d_c_cb), writes=["cb"], dma="cb")
    P.op("sp", R.dma_start(out=DTB[:], in_=C.d_c_dtb), writes=["dtb"], dma="dtb")
    P.op("sp", R.dma_start(out=ANEG[:], in_=C.d_c_alog), writes=["aneg"], dma="aneg")
    P.op("act", R.activation(out=ANEG[:], in_=ANEG[:], func=AF.Exp), reads=["aneg"], writes=["aneg"])
    P.op("pool", R.tensor_scalar(out=ANEG[:], in0=ANEG[:], scalar1=-1.0, scalar2=None, op0=ALU.mult), reads=["aneg"], writes=["aneg"])
    P.op("pool", R.dma_start(out=IDN[:], in_=C.d_ident), writes=["idn"], dma="idn")
    st = {"xp": 0, "dt": 0, "tb": 0}
    pTall = C.PSALL.bitcast(BF16)

    def cb_z(c):
        return lambda ps, pk, t0, N: st_z.put(ps, pk, ZT[c * 128:(c + 1) * 128, t0:t0 + N], 128, N, func=AF.Silu)

    def cb_x(cc):
        def f(ps, pk, t0, N):
            if cc == 0:
                if t0 >= TPc:
                    P.op("sp", R.dma_start(out=HALO[:], in_=C.d_c_conv0.rearrange("(c p) j -> p c j", p=128)), writes=["halo"], dma="halo")
                elif t0 % S == 0:
                    P.op("pool", R.memset(HALO[:], 0.0), writes=["halo"])
            r = st["xp"] % 3
            st["xp"] += 1
            X_ = XP[r]
            xk = ("xp", r)
            P.op("act", R.activation(out=X_[:, 3:3 + N], in_=ps, func=AF.Copy), reads=[pk], writes=[xk])
            P.op("pool", R.tensor_copy(out=X_[:, 0:3], in_=HALO[:, cc, :]), reads=["halo"], writes=[xk])
            a = ACC[cc % 2]
            ak = ("acc", cc % 2)
            P.op("dve", R.tensor_scalar(out=a[:, :N], in0=X_[:, 0:N], scalar1=CW[:, cc, 0:1], scalar2=CB[:, cc:cc + 1],
                                        op0=ALU.mult, op1=ALU.add), reads=[xk, "cw", "cb"], writes=[ak])
            for j in range(1, 4):
                P.op("dve", R.scalar_tensor_tensor(out=a[:, :N], in0=X_[:, j:j + N], scalar=CW[:, cc, j:j + 1], in1=a[:, :N],
                                                   op0=ALU.mult, op1=ALU.add), reads=[xk, "cw", ak], writes=[ak])
            P.op("pool", R.tensor_copy(out=HALO[:, cc, :], in_=X_[:, N:N + 3]), reads=[xk], writes=["halo"])
            gi = (cc // 4) % 2
            xsk = ("xs", gi)
            P.op("act", R.activation(out=XS[gi][:, cc % 4, :N], in_=a[:, :N], func=AF.Silu), reads=[ak], writes=[xsk])
            P.op("sp", R.dma_start(out=XC[cc * 128:(cc + 1) * 128, t0:t0 + N], in_=XS[gi][:, cc % 4, :N]), reads=[xsk], dma=f"xc{gi}")
            if cc % 4 == 3 and cc < 24:
                cc0 = cc - 3
                for tb in range((N + 127) // 128):
                    nt = min(128, N - tb * 128)
                    b = 6 + st["tb"] % 2
                    st["tb"] += 1
                    pT = pTall[:, b * 1024:b * 1024 + 512]
                    for q in range(4):
                        P.op("pe", R.transpose(out=pT[:nt, q * 128:(q + 1) * 128], in_=XS[gi][:, q, tb * 128:tb * 128 + nt], identity=IDN[:, :]),
                             reads=[xsk, "idn"], writes=[f"ps{b}"])
                    st_t.put(pT[:nt, :], f"ps{b}", XTM[t0 + tb * 128:t0 + tb * 128 + nt, cc0 * 128:cc0 * 128 + 512], nt, 512, eng="dve")
            if cc == 31:
                if t0 >= TPc:
                    P.op("sp", R.dma_start(out=C.d_cconv_s.rearrange("(c p) j -> p c j", p=128), in_=HALO[:]), reads=["halo"], dma="halo_o")
                elif (t0 + N) % S == 0:
                    P.op("sp", R.dma_start(out=C.d_cconv_p[t0 // S].rearrange("(c p) j -> p c j", p=128), in_=HALO[:]), reads=["halo"], dma="halo_o")
        return f

    def cb_dt(ps, pk, t0, nt):
        r = st["dt"] % 2
        st["dt"] += 1
        T_ = DTT[r]
        k = ("dtt", r)
        P.op("dve", R.tensor_tensor(out=T_[:nt, 0:32], in0=ps, in1=DTB[:nt, :], op=ALU.add), reads=[pk, "dtb"], writes=[k])
        P.op("act", R.activation(out=T_[:nt, 0:32], in_=T_[:nt, 0:32], func=AF.Exp), reads=[k], writes=[k])
        P.op("act", R.activation(out=T_[:nt, 0:32], in_=T_[:nt, 0:32], func=AF.Ln, bias=C.onec[:nt, 0:1]), reads=[k, "onec"], writes=[k])
        P.op("act", R.activation(out=T_[:nt, 32:64], in_=T_[:nt, 0:32], func=AF.Ln), reads=[k], writes=[k])
        P.op("dve", R.tensor_tensor(out=T_[:nt, 64:96], in0=T_[:nt, 0:32], in1=ANEG[:nt, :], op=ALU.mult), reads=[k, "aneg"], writes=[k])
        P.op("sp", R.dma_start(out=DTS[t0:t0 + nt, :], in_=T_[:nt, :]), reads=[k], dma=f"dtt{r}")
    fm = [(c * 128, 128, cb_z(c)) for c in range(16)] + [(2048 + cc * 128, 128, cb_x(cc)) for cc in range(32)]
    tm = [(6144, 32, cb_dt)]
    inproj_generic(P, C, "cwin", Xin, C.d_c_w_in, 6176, tiles, fm, tm)

    P.barrier()
    A.reset()
    UT = A.alloc([64], F32)
    NEG32 = A.alloc([32, 64], F32)
    NEGT = A.alloc([64], F32)
    ONESF = A.alloc([128], F32)
    ONESG = A.alloc([128], BF16)
    DSK = A.alloc([16], F32)
    H = A.alloc([32, 64], F32)
    Hb = A.alloc([32, 64], BF16)
    DTc = [A.alloc([96], F32) for _ in range(2)]
    XHT = [A.alloc([3072], BF16) for _ in range(2)]
    XCF = [A.alloc([32, 64], BF16) for _ in range(2)]
    ZTc = [A.alloc([16, 64], BF16) for _ in range(2)]
    CML = A.alloc([32], F32)
    RR = A.alloc([32, 64], F32)
    ECB = A.alloc([32, 64], BF16)
    CEND = A.alloc([32], F32)
    SEG = A.alloc([32, 64], F32)
    DEC = A.alloc([32, 64], BF16)
    G = A.alloc([32, 64], BF16)
    CE = A.alloc([32, 64], BF16)
    T1 = A.alloc([16, 64], F32)
    YS = A.alloc([16, 64], F32)
    SQ = A.alloc([16, 64], BF16)
    RS = A.alloc([8, 64], F32)
    AOc = [A.alloc([16, 64], BF16) for _ in range(2)]
    W2 = A.alloc([32], F32)
    XW = A.alloc([32, 64], BF16)
    EE = A.alloc([32], F32)
    P.op("sp", R.dma_start(out=UT[0:64, :], in_=C.d_ut), writes=["ut"], dma="ut")
    P.op("sp", R.dma_start(out=NEGT[0:64, :], in_=C.d_negt), writes=["negt"], dma="negt")
    P.op("sp", R.dma_start(out=DSK[:], in_=C.d_c_dsk), writes=["dsk"], dma="dsk")
    P.op("dve", R.tensor_copy(out=NEG32[0:64], in_=NEGT[0:64, :].rearrange("p (o t) -> p o t", o=1).to_broadcast([64, 32, 64])),
         reads=["negt"], writes=["neg32"])
    P.op("pool", R.memset(ONESF[:], 1.0), writes=["onesf"])
    P.op("pool", R.memset(ONESG[:], 1.0 / 256.0), writes=["onesg"])
    xc_d = XC.rearrange("(c p) t -> p c t", p=128)
    zt_d = ZT.rearrange("(c p) t -> p c t", p=128)
    ao_d = C.sAO.rearrange("(c p) t -> p c t", p=128)
    seqs = [(sq * S, S // 64, 64, None, C.d_cssm_p[sq]) for sq in range(NP_)] + [(TPc, 1, TS, C.d_c_ssm0, C.d_cssm_s)]
    ci_ = 0
    for (q0, nch, L, h0, hout) in seqs:
        if h0 is None:
            P.op("pool", R.memset(H[:], 0.0), writes=["h"])
        else:
            P.op("sp", R.dma_start(out=H[:].rearrange("p h q -> p (h q)"), in_=h0), writes=["h"], dma="h0")
        P.op("act", R.activation(out=Hb[:], in_=H[:], func=AF.Copy), reads=["h"], writes=["hb"])
        HPM = 512 // L
        for ci in range(nch):
            tc = q0 + ci * L
            s2 = ci_ % 2
            ci_ += 1
            dk, xhk, xck, ztk = ("dtc", s2), ("xht", s2), ("xcf", s2), ("ztc", s2)
            D_, XH_, XC_, ZT_ = DTc[s2], XHT[s2], XCF[s2], ZTc[s2]
            P.op("sp", R.dma_start(out=D_[:L, :], in_=DTS[tc:tc + L, :]), writes=[dk], dma=f"dtc{s2}")
            P.op("sp", R.dma_start(out=XH_[:L, :], in_=XTM[tc:tc + L, :]), writes=[xhk], dma=f"xht{s2}")
            P.op("sp", R.dma_start(out=XC_[:, :, :L], in_=xc_d[:, :, tc:tc + L]), writes=[xck], dma=f"xcf{s2}")
            P.op("sp", R.dma_start(out=ZT_[:, :, :L], in_=zt_d[:, :, tc:tc + L]), writes=[ztk], dma=f"ztc{s2}")
            pc = C.PSB[7]
            P.op("pe", R.matmul(pc[:L, 0:32], lhsT=UT[:L, :L], rhs=D_[:L, 64:96], start=True, stop=True), reads=["ut", dk], writes=["ps7"])
            P.op("dve", R.tensor_tensor(out=CML[:L, :], in0=pc[:L, 0:32], in1=D_[:L, 32:64], op=ALU.subtract), reads=["ps7", dk], writes=["cml"])
            P.op("dve", R.tensor_tensor(out=RR[:L, :, :L], in0=D_[:L, 64:96].rearrange("p (h o) -> p h o", o=1).to_broadcast([L, 32, L]),
                                        in1=UT[:L, :L].rearrange("p (o t) -> p o t", o=1).to_broadcast([L, 32, L]), op=ALU.mult),
                 reads=[dk, "ut"], writes=["rr"])
            PBC = C.PSALL[:, 0:32 * L].rearrange("p (h t) -> p h t", h=32)
            bk = ["ps0", "ps1", "ps2", "ps3"]
            for hb in range(0, 32, HPM):
                P.op("pe", R.matmul(PBC[:, hb:hb + HPM, :], lhsT=ONESF[:L, :], rhs=RR[:L, hb:hb + HPM, :L], start=True, stop=True),
                     reads=["onesf", "rr"], writes=[f"ps{(hb * L) // 512}"])
            ubk = bk[:(32 * L + 511) // 512]
            P.op("act", R.activation(out=ECB[:, :, :L], in_=PBC, func=AF.Exp), reads=ubk, writes=["ecb"])
            P.op("act", R.activation(out=CEND[:, :], in_=PBC[:, :, L - 1], func=AF.Copy), reads=ubk, writes=["cend"])
            P.op("dve", R.tensor_tensor(out=SEG[:L, :, :L], in0=PBC[:L], in1=CML[:L, :].rearrange("p (h o) -> p h o", o=1).to_broadcast([L, 32, L]),
                                        op=ALU.subtract), reads=ubk + ["cml"], writes=["seg"])
            P.op("pool", R.tensor_tensor(out=SEG[:L, :, :L], in0=SEG[:L, :, :L], in1=NEG32[:L, :, :L], op=ALU.add), reads=["seg", "neg32"], writes=["seg"])
            P.op("act", R.activation(out=DEC[:L, :, :L], in_=SEG[:L, :, :L], func=AF.Exp), reads=["seg"], writes=["dec"])
            pcb = C.PSB[4][:, 0:8 * L].rearrange("p (g t) -> p g t", g=8)
            for g in range(8):
                P.op("pe", R.matmul(pcb[:L, g, :], lhsT=XC_[:, 16 + g, :L], rhs=XC_[:, 24 + g, :L], start=True, stop=True),
                     reads=[xck], writes=["ps4"])
            Gv = G[:, :, :].rearrange("p (g q) t -> p g q t", q=4)
            DECv = DEC[:, :, :].rearrange("p (g q) t -> p g q t", q=4)
            ECBv = ECB[:, :, :].rearrange("p (g q) t -> p g q t", q=4)
            CEv = CE[:, :, :].rearrange("p (g q) t -> p g q t", q=4)
            for hh in range(4):
                P.op("dve", R.tensor_tensor(out=Gv[:L, :, hh, :L], in0=pcb[:L], in1=DECv[:L, :, hh, :L], op=ALU.mult),
                     reads=["ps4", "dec"], writes=["g"])
                P.op("pool", R.tensor_tensor(out=CEv[:, :, hh, :L], in0=ECBv[:, :, hh, :L], in1=XC_[:, 24:32, :L], op=ALU.mult),
                     reads=["ecb", xck], writes=["ce"])
            pyv = C.PSALL[:, 2560:2560 + 16 * L].rearrange("p (c t) -> p c t", c=16)
            pyk = ["ps5", "ps6"][:(16 * L + 511) // 512]
            for h in range(32):
                c, par = divmod(h, 2)
                o_ = pyv[par * 64:(par + 1) * 64, c, :]
                bkk = [f"ps{(2560 + c * L) // 512}"]
                P.op("pe", R.matmul(o_, lhsT=XH_[:L, h * 64:(h + 1) * 64], rhs=G[:L, h, :L], start=True, stop=False),
                     reads=[xhk, "g"], writes=bkk)
                P.op("pe", R.matmul(o_, lhsT=Hb[:, h, :], rhs=CE[:, h, :L], start=False, stop=True), reads=["hb", "ce"], writes=bkk)
            P.op("pool", R.tensor_tensor(out=T1[:, :, :L], in0=XC_[:, 0:16, :L], in1=DSK[:, :].rearrange("p (c o) -> p c o", o=1).to_broadcast([128, 16, L]),
                                         op=ALU.mult), reads=[xck, "dsk"], writes=["t1"])
            P.op("dve", R.tensor_tensor(out=YS[:, :, :L], in0=pyv, in1=T1[:, :, :L], op=ALU.add), reads=pyk + ["t1"], writes=["ys"])
            P.op("dve", R.tensor_tensor(out=YS[:, :, :L], in0=YS[:, :, :L], in1=ZT_[:, :, :L], op=ALU.mult), reads=["ys", ztk], writes=["ys"])
            P.op("act", R.activation(out=SQ[:, :, :L], in_=YS[:, :, :L], func=AF.Square), reads=["ys"], writes=["sq"])
            pgn = C.PSB[7][:, 0:8 * L].rearrange("p (g t) -> p g t", g=8)
            for g in range(8):
                for q in range(2):
                    P.op("pe", R.matmul(pgn[:, g, :], lhsT=ONESG[:, :], rhs=SQ[:, 2 * g + q, :L], start=(q == 0), stop=(q == 1)),
                         reads=["onesg", "sq"], writes=["ps7"])
            P.op("act", R.activation(out=RS[:, :, :L], in_=pgn, func=AF.Sqrt, bias=C.epsc[:, 0:1]), reads=["ps7", "epsc"], writes=["rs"])
            P.op("dve", R.reciprocal(out=RS[:, :, :L], in_=RS[:, :, :L]), reads=["rs"], writes=["rs"])
            AOv = AOc[s2][:, :, :].rearrange("p (g q) t -> p g q t", q=2)
            YSv = YS[:, :, :].rearrange("p (g q) t -> p g q t", q=2)
            for q in range(2):
                P.op("dve", R.tensor_tensor(out=AOv[:, :, q, :L], in0=YSv[:, :, q, :L], in1=RS[:, :, :L], op=ALU.mult),
                     reads=["ys", "rs"], writes=[("aoc", s2)])
            P.op("sp", R.dma_start(out=ao_d[:, :, tc:tc + L], in_=AOc[s2][:, :, :L]), reads=[("aoc", s2)], dma=f"aoc{s2}")
            P.op("dve", R.tensor_tensor(out=W2[:L, :], in0=CEND[:L, :], in1=CML[:L, :], op=ALU.subtract), reads=["cend", "cml"], writes=["w2"])
            P.op("act", R.activation(out=W2[:L, :], in_=W2[:L, :], func=AF.Exp), reads=["w2"], writes=["w2"])
            P.op("dve", R.tensor_tensor(out=XW[:L, :, :], in0=XH_[:L, 0:2048].rearrange("p (h q) -> p h q", q=64),
                                        in1=W2[:L, :].rearrange("p (h o) -> p h o", o=1).to_broadcast([L, 32, 64]), op=ALU.mult),
                 reads=[xhk, "w2"], writes=["xw"])
            pst = C.PSALL[:, 0:2048].rearrange("p (h q) -> p h q", q=64)
            for g in range(8):
                P.op("pe", R.matmul(pst[:, 4 * g:4 * g + 4, :], lhsT=XH_[:L, 2048 + g * 128:2048 + (g + 1) * 128], rhs=XW[:L, 4 * g:4 * g + 4, :],
                                    start=True, stop=True), reads=[xhk, "xw"], writes=[f"ps{g // 2}"])
            P.op("act", R.activation(out=EE[:, :], in_=CEND[:, :], func=AF.Exp), reads=["cend"], writes=["ee"])
            P.op("dve", R.tensor_tensor(out=H[:, :, :], in0=H[:, :, :], in1=EE[:, :].rearrange("p (h o) -> p h o", o=1).to_broadcast([128, 32, 64]),
                                        op=ALU.mult), reads=["h", "ee"], writes=["h"])
            P.op("dve", R.tensor_tensor(out=H[:, :, :], in0=pst, in1=H[:, :, :], op=ALU.add), reads=bk + ["h"], writes=["h"])
            P.op("act", R.activation(out=Hb[:], in_=H[:], func=AF.Copy), reads=["h"], writes=["hb"])
        P.op("sp", R.dma_start(out=hout, in_=H[:].rearrange("p h q -> p (h q)")), reads=["h"], dma="hout")
    outproj_phase(P, C, "cwout", C.sAO, 16, C.d_c_w_out, Xin, Xout, lnidx, tiles, rowscale=C.d_c_ng)


SC_K = 512.0 ** -0.5


def mixer_d(P, C, Xin, Xout, lnidx):
    S, NP_, TPc, TTc = C.SEQ, C.NPS, C.TP, C.TT
    tiles = C.tiles
    QT, KT, KTM, VTM, OS, GS = C.sD_QT, C.sD_KT, C.sD_KTM, C.sD_VTM, C.sD_OS, C.sD_GS
    P.barrier()
    C.A.reset()
    A = C.A
    st_q = Stager(P, C, "stq", [512], BF16, 4)
    st_t = Stager(P, C, "stt", [512], BF16, 4)
    HALO = A.alloc([16, 3], F32)
    CW = A.alloc([16, 4], F32)
    CB = A.alloc([16], F32)
    GB = A.alloc([8], F32)
    WQ = A.alloc([16, 128], BF16)
    WK = A.alloc([16, 128], BF16)
    XP = [A.alloc([515], F32) for _ in range(3)]
    ACC = [A.alloc([512], F32) for _ in range(2)]
    XS = [A.alloc([4, 512], BF16) for _ in range(2)]
    GT = [A.alloc([8], F32) for _ in range(2)]
    P.op("sp", R.dma_start(out=CW[:], in_=C.d_d_cw), writes=["cw"], dma="cw")
    P.op("sp", R.dma_start(out=CB[:], in_=C.d_d_cb), writes=["cb"], dma="cb")
    P.op("sp", R.dma_start(out=GB[:], in_=C.d_d_gb), writes=["gb"], dma="gb")
    P.op("pool", R.dma_start(out=WQ[:], in_=C.d_d_wq), writes=["wq"], dma="wq")
    P.op("pool", R.dma_start(out=WK[:], in_=C.d_d_wk), writes=["wk"], dma="wk")
    st = {"xp": 0, "g": 0, "b": 0}

    def cb_x(cc):
        def f(ps, pk, t0, N):
            if cc == 0:
                if t0 >= TPc:
                    P.op("sp", R.dma_start(out=HALO[:], in_=C.d_d_conv0.rearrange("(c p) j -> p c j", p=128)), writes=["halo"], dma="halo")
                elif t0 % S == 0:
                    P.op("pool", R.memset(HALO[:], 0.0), writes=["halo"])
            r = st["xp"] % 3
            st["xp"] += 1
            X_ = XP[r]
            xk = ("xp", r)
            P.op("act", R.activation(out=X_[:, 3:3 + N], in_=ps, func=AF.Copy), reads=[pk], writes=[xk])
            P.op("pool", R.tensor_copy(out=X_[:, 0:3], in_=HALO[:, cc, :]), reads=["halo"], writes=[xk])
            a = ACC[cc % 2]
            ak = ("acc", cc % 2)
            P.op("dve", R.tensor_scalar(out=a[:, :N], in0=X_[:, 0:N], scalar1=CW[:, cc, 0:1], scalar2=CB[:, cc:cc + 1],
                                        op0=ALU.mult, op1=ALU.add), reads=[xk, "cw", "cb"], writes=[ak])
            for j in range(1, 4):
                P.op("dve", R.scalar_tensor_tensor(out=a[:, :N], in0=X_[:, j:j + N], scalar=CW[:, cc, j:j + 1], in1=a[:, :N],
                                                   op0=ALU.mult, op1=ALU.add), reads=[xk, "cw", ak], writes=[ak])
            P.op("pool", R.tensor_copy(out=HALO[:, cc, :], in_=X_[:, N:N + 3]), reads=[xk], writes=["halo"])
            gi = (cc // 4) % 2
            xsk = ("xs", gi)
            P.op("act", R.activation(out=XS[gi][:, cc % 4, :N], in_=a[:, :N], func=AF.Silu), reads=[ak], writes=[xsk])
            for (Wm, wkey, dst, sc) in ((WQ, "wq", QT, 1.0), (WK, "wk", KT, SC_K)):
                b = 6 + st["b"] % 2
                st["b"] += 1
                pq = C.PSB[b]
                P.op("pe", R.matmul(pq[:, :N], lhsT=Wm[:, cc, :], rhs=XS[gi][:, cc % 4, :N], start=True, stop=True),
                     reads=[wkey, xsk], writes=[f"ps{b}"])
                B_, k_ = st_q.bufs[st_q.i % st_q.n], (st_q.name, st_q.i % st_q.n)
                s_ = st_q.i % st_q.n
                st_q.i += 1
                P.op("act", R.activation(out=B_[:, :N], in_=pq[:, :N], func=AF.Copy, scale=sc), reads=[f"ps{b}"], writes=[k_])
                P.op("sp", R.dma_start(out=dst[cc * 128:(cc + 1) * 128, t0:t0 + N], in_=B_[:, :N]), reads=[k_], dma=f"stq{s_}")
            if cc % 4 == 3:
                cc0 = cc - 3
                for tb in range((N + 127) // 128):
                    nt = min(128, N - tb * 128)
                    b = 6 + st["b"] % 2
                    st["b"] += 1
                    pq = C.PSB[b]
                    for q in range(4):
                        P.op("pe", R.matmul(pq[:nt, q * 128:(q + 1) * 128], lhsT=XS[gi][:, q, tb * 128:tb * 128 + nt], rhs=WK[:, cc0 + q, :],
                                            start=True, stop=True), reads=[xsk, "wk"], writes=[f"ps{b}"])
                    B_, k_ = st_t.bufs[st_t.i % st_t.n], (st_t.name, st_t.i % st_t.n)
                    s_ = st_t.i % st_t.n
                    st_t.i += 1
                    P.op("act", R.activation(out=B_[:nt, :], in_=pq[:nt, :], func=AF.Copy, scale=SC_K), reads=[f"ps{b}"], writes=[k_])
                    P.op("sp", R.dma_start(out=KTM[t0 + tb * 128:t0 + tb * 128 + nt, cc0 * 128:cc0 * 128 + 512], in_=B_[:nt, :]),
                         reads=[k_], dma=f"stt{s_}")
            if cc == 15:
                if t0 >= TPc:
                    P.op("sp", R.dma_start(out=C.d_dconv_s.rearrange("(c p) j -> p c j", p=128), in_=HALO[:]), reads=["halo"], dma="halo_o")
                elif (t0 + N) % S == 0:
                    P.op("sp", R.dma_start(out=C.d_dconv_p[t0 // S].rearrange("(c p) j -> p c j", p=128), in_=HALO[:]), reads=["halo"], dma="halo_o")
        return f

    def cb_v(q):
        return lambda ps, pk, t0, nt: st_t.put(ps, pk, VTM[t0:t0 + nt, q * 512:(q + 1) * 512], nt, 512, eng="dve")

    def cb_o(q):
        return lambda ps, pk, t0, nt: st_t.put(ps, pk, OS[t0:t0 + nt, q * 512:(q + 1) * 512], nt, 512, func=AF.Sigmoid)

    def cb_g(ps, pk, t0, nt):
        r = st["g"] % 2
        st["g"] += 1
        T_ = GT[r]
        k = ("gt", r)
        P.op("dve", R.tensor_tensor(out=T_[:nt, :], in0=ps, in1=GB[:nt, :], op=ALU.add), reads=[pk, "gb"], writes=[k])
        P.op("act", R.activation(out=T_[:nt, 4:8], in_=T_[:nt, 4:8], func=AF.Exp, scale=-1.0), reads=[k], writes=[k])
        P.op("act", R.activation(out=T_[:nt, 4:8], in_=T_[:nt, 4:8], func=AF.Ln, bias=C.onec[:nt, 0:1]), reads=[k, "onec"], writes=[k])
        P.op("pool", R.tensor_scalar(out=T_[:nt, 4:8], in0=T_[:nt, 4:8], scalar1=-1.0, scalar2=None, op0=ALU.mult), reads=[k], writes=[k])
        P.op("sp", R.dma_start(out=GS[t0:t0 + nt, :], in_=T_[:nt, :]), reads=[k], dma=f"gt{r}")
    fm = [(cc * 128, 128, cb_x(cc)) for cc in range(16)]
    tm = [(2048 + q * 512, 512, cb_v(q)) for q in range(4)] + [(4096 + q * 512, 512, cb_o(q)) for q in range(4)] + [(6144, 8, cb_g)]
    inproj_generic(P, C, "dwin", Xin, C.d_d_w_in, 6152, tiles, fm, tm)

    DSTOP = 99
    if DSTOP <= 1:
        return
    P.barrier()
    A.reset()
    UT = A.alloc([64], F32)
    IDF = A.alloc([64], F32)
    NEGU = A.alloc([64], F32)
    NEGU4 = A.alloc([4, 64], F32)
    SEL = {64: A.alloc([128], F32), 32: A.alloc([128], F32)}
    ONESF = A.alloc([64], F32)
    ONEB = A.alloc([2], BF16)
    IDN = A.alloc([128], BF16)
    CST = A.alloc([4, 4, 512], F32)
    CBF = A.alloc([4, 4, 512], BF16)
    NS = A.alloc([4, 4], F32)
    NSB = A.alloc([4, 4], BF16)
    M0 = A.alloc([4], F32)
    GSc = [A.alloc([8], F32) for _ in range(2)]
    QTc = [A.alloc([16, 64], BF16) for _ in range(2)]
    KTc = [A.alloc([16, 64], BF16) for _ in range(2)]
    KMc = [A.alloc([2048], BF16) for _ in range(2)]
    VMc = [A.alloc([2048], BF16) for _ in range(2)]
    OSc = [A.alloc([2048], BF16) for _ in range(2)]
    FC = A.alloc([4], F32)
    RV = A.alloc([4], F32)
    RRr = A.alloc([4, 64], F32)
    DL = A.alloc([4, 64], F32)
    RMX = A.alloc([4], F32)
    INTER = A.alloc([4], F32)
    MM = A.alloc([8], F32)
    NEGM = A.alloc([4], F32)
    DEX = A.alloc([4, 64], F32)
    SM = A.alloc([4, 64], F32)
    SMB = A.alloc([4, 64], BF16)
    STT = A.alloc([4, 64], BF16)
    DENI = A.alloc([4], F32)
    WINT = A.alloc([4], F32)
    DEN = A.alloc([4], F32)
    EM = A.alloc([4], F32)
    RDEN = A.alloc([4], F32)
    WR = A.alloc([4], F32)
    TT_ = [A.alloc([512], F32) for _ in range(2)]
    HN = A.alloc([4, 512], F32)
    MEAN4 = A.alloc([4], F32)
    VAR4 = A.alloc([4], F32)
    HJ = A.alloc([512], BF16)
    HNB = A.alloc([2048], BF16)
    AOc = [A.alloc([16, 64], BF16) for _ in range(2)]
    BC8 = A.alloc([8], F32)
    DD = A.alloc([4], F32)
    WE = A.alloc([4], F32)
    DECAY = A.alloc([4], F32)
    KW = A.alloc([4, 512], BF16)
    for (t_, d_, nm) in ((UT, C.d_ut, "ut"), (IDF, C.d_identf, "idf"), (NEGU, C.d_negu, "negu")):
        P.op("sp", R.dma_start(out=t_[0:64, :], in_=d_), writes=[nm], dma=nm)
    P.op("sp", R.dma_start(out=SEL[64][0:64, :], in_=C.d_sel64), writes=["sel"], dma="sel")
    P.op("sp", R.dma_start(out=SEL[32][0:32, :], in_=C.d_sel32), writes=["sel"], dma="sel")
    P.op("pool", R.dma_start(out=IDN[:], in_=C.d_ident), writes=["idn"], dma="idn")
    P.op("dve", R.tensor_copy(out=NEGU4[0:64], in_=NEGU[0:64, :].rearrange("p (o t) -> p o t", o=1).to_broadcast([64, 4, 64])),
         reads=["negu"], writes=["negu4"])
    P.op("pool", R.memset(ONESF[:], 1.0), writes=["onesf"])
    P.op("pool", R.memset(ONEB[:], 1.0), writes=["oneb"])
    qt_d = QT.rearrange("(c p) t -> p c t", p=128)
    kt_d = KT.rearrange("(c p) t -> p c t", p=128)
    ao_d = C.sAO.rearrange("(c p) t -> p c t", p=128)
    seqs = [(sq * S, S // 64, 64, None, sq) for sq in range(NP_)] + [(TPc, 1, TS, True, None)]
    ci_ = 0
    for (q0, nch, L, init, sq) in seqs:
        if init is None:
            P.op("pool", R.memset(CST[:], 0.0), writes=["cst"])
            P.op("pool", R.memset(NS[:], 0.0), writes=["ns"])
            P.op("pool", R.memset(M0[:], 0.0), writes=["m0"])
        else:
            P.op("sp", R.dma_start(out=CST[:], in_=C.d_d_c0.rearrange("h (kc p) v -> p h kc v", p=128)), writes=["cst"], dma="c0")
            P.op("sp", R.dma_start(out=NS[:].rearrange("p h k -> p (h k)"), in_=C.d_d_n0), writes=["ns"], dma="n0")
            P.op("sp", R.dma_start(out=M0[:], in_=C.d_d_m0), writes=["m0"], dma="m0")
        P.op("act", R.activation(out=CBF[:], in_=CST[:], func=AF.Copy), reads=["cst"], writes=["cbf"])
        P.op("act", R.activation(out=NSB[:], in_=NS[:], func=AF.Copy), reads=["ns"], writes=["nsb"])
        for ci in range(nch):
            tc = q0 + ci * L
            s2 = ci_ % 2
            ci_ += 1
            gk, qk, kk, kmk, vmk, osk = [(n_, s2) for n_ in ("gsc", "qtc", "ktc", "kmc", "vmc", "osc")]
            G_, Q_, K_, KM_, VM_, OS_ = GSc[s2], QTc[s2], KTc[s2], KMc[s2], VMc[s2], OSc[s2]
            P.op("sp", R.dma_start(out=G_[:L, :], in_=GS[tc:tc + L, :]), writes=[gk], dma=f"gsc{s2}")
            P.op("sp", R.dma_start(out=Q_[:, :, :L], in_=qt_d[:, :, tc:tc + L]), writes=[qk], dma=f"qtc{s2}")
            P.op("sp", R.dma_start(out=K_[:, :, :L], in_=kt_d[:, :, tc:tc + L]), writes=[kk], dma=f"ktc{s2}")
            P.op("sp", R.dma_start(out=KM_[:L, :], in_=KTM[tc:tc + L, :]), writes=[kmk], dma=f"kmc{s2}")
            P.op("sp", R.dma_start(out=VM_[:L, :], in_=VTM[tc:tc + L, :]), writes=[vmk], dma=f"vmc{s2}")
            P.op("sp", R.dma_start(out=OS_[:L, :], in_=OS[tc:tc + L, :]), writes=[osk], dma=f"osc{s2}")
            p7 = C.PSB[7]
            P.op("pe", R.matmul(p7[:L, 0:4], lhsT=UT[:L, :L], rhs=G_[:L, 4:8], start=True, stop=True), reads=["ut", gk], writes=["ps7"])
            P.op("act", R.activation(out=MM[:L, 4:8], in_=p7[:L, 0:4], func=AF.Copy), reads=["ps7"], writes=["fc"])
            P.op("dve", R.tensor_tensor(out=RV[:L, :], in0=G_[:L, 0:4], in1=MM[:L, 4:8], op=ALU.subtract), reads=[gk, "fc"], writes=["rv"])
            P.op("dve", R.tensor_tensor(out=RRr[:L, :, :L], in0=RV[:L, :].rearrange("p (h o) -> p h o", o=1).to_broadcast([L, 4, L]),
                                        in1=IDF[:L, :L].rearrange("p (o t) -> p o t", o=1).to_broadcast([L, 4, L]), op=ALU.mult),
                 reads=["rv", "idf"], writes=["rrr"])
            p6 = C.PSB[6][:, 0:4 * L].rearrange("p (h t) -> p h t", h=4)
            P.op("pe", R.matmul(p6[:L], lhsT=ONESF[:L, :L], rhs=RRr[:L, :, :L], start=True, stop=True), reads=["onesf", "rrr"], writes=["ps6"])
            P.op("dve", R.tensor_tensor(out=DL[:L, :, :L], in0=p6[:L], in1=MM[:L, 4:8].rearrange("p (h o) -> p h o", o=1).to_broadcast([L, 4, L]),
                                        op=ALU.add), reads=["ps6", "fc"], writes=["dl"])
            P.op("pool", R.tensor_tensor(out=DL[:L, :, :L], in0=DL[:L, :, :L], in1=NEGU4[:L, :, :L], op=ALU.add), reads=["dl", "negu4"], writes=["dl"])
            P.op("dve", R.tensor_reduce(out=RMX[:L, :], in_=DL[:L, :, :L], axis=mybir.AxisListType.X, op=ALU.max), reads=["dl"], writes=["rmx"])
            P.op("dve", R.tensor_tensor(out=INTER[:L, :], in0=MM[:L, 4:8], in1=M0[:L, :], op=ALU.add), reads=["fc", "m0"], writes=["inter"])
            P.op("dve", R.tensor_tensor(out=MM[:L, 0:4], in0=RMX[:L, :], in1=INTER[:L, :], op=ALU.max), reads=["rmx", "inter"], writes=["mm"])
            P.op("pool", R.tensor_scalar(out=NEGM[:L, :], in0=MM[:L, 0:4], scalar1=-1.0, scalar2=None, op0=ALU.mult), reads=["mm"], writes=["negm"])
            for h in range(4):
                P.op("act", R.activation(out=DEX[:L, h, :L], in_=DL[:L, h, :L], func=AF.Exp, bias=NEGM[:L, h:h + 1]),
                     reads=["dl", "negm"], writes=["dex"])
            if DSTOP <= 2:
                continue
            p5 = C.PSB[5][:, 0:4 * L].rearrange("p (h t) -> p h t", h=4)
            for h in range(4):
                for kc in range(4):
                    P.op("pe", R.matmul(p5[:L, h, :], lhsT=Q_[:, 4 * h + kc, :L], rhs=K_[:, 4 * h + kc, :L], start=(kc == 0), stop=(kc == 3)),
                         reads=[qk, kk], writes=["ps5"])
            P.op("dve", R.tensor_tensor(out=SM[:L, :, :L], in0=p5[:L], in1=DEX[:L, :, :L], op=ALU.mult), reads=["ps5", "dex"], writes=["sm"])
            P.op("dve", R.tensor_reduce(out=DENI[:L, :], in_=SM[:L, :, :L], axis=mybir.AxisListType.X, op=ALU.add), reads=["sm"], writes=["deni"])
            P.op("act", R.activation(out=SMB[:L, :, :L], in_=SM[:L, :, :L], func=AF.Copy), reads=["sm"], writes=["smb"])
            pT = C.PSALL.bitcast(BF16)[:, 4 * 1024:4 * 1024 + 1024]
            pTv = pT[:, 0:4 * L].rearrange("p (h t) -> p h t", h=4)
            for h in range(4):
                P.op("pe", R.transpose(out=pTv[:L, h, :], in_=SMB[:L, h, :L], identity=IDN[:L, :L]), reads=["smb", "idn"], writes=["ps4"])
            P.op("act", R.activation(out=STT[:L, :, :L], in_=pTv[:L], func=AF.Copy), reads=["ps4"], writes=["stt"])
            P.op("dve", R.tensor_tensor(out=WINT[:L, :], in0=INTER[:L, :], in1=MM[:L, 0:4], op=ALU.subtract), reads=["inter", "mm"], writes=["wint"])
            P.op("act", R.activation(out=WINT[:L, :], in_=WINT[:L, :], func=AF.Exp), reads=["wint"], writes=["wint"])
            for h in range(4):
                for kc in range(4):
                    P.op("pe", R.matmul(p7[:L, 8 + h:9 + h], lhsT=Q_[:, 4 * h + kc, :L], rhs=NSB[:, h, kc:kc + 1], start=(kc == 0), stop=(kc == 3)),
                         reads=[qk, "nsb"], writes=["ps7"])
            P.op("dve", R.tensor_tensor(out=DEN[:L, :], in0=p7[:L, 8:12], in1=WINT[:L, :], op=ALU.mult), reads=["ps7", "wint"], writes=["den"])
            P.op("dve", R.tensor_tensor(out=DEN[:L, :], in0=DEN[:L, :], in1=DENI[:L, :], op=ALU.add), reads=["den", "deni"], writes=["den"])
            P.op("pool", R.tensor_scalar(out=WR[:L, :], in0=DEN[:L, :], scalar1=-1.0, scalar2=None, op0=ALU.mult), reads=["den"], writes=["wr"])
            P.op("dve", R.tensor_tensor(out=DEN[:L, :], in0=DEN[:L, :], in1=WR[:L, :], op=ALU.max), reads=["den", "wr"], writes=["den"])
            P.op("act", R.activation(out=EM[:L, :], in_=NEGM[:L, :], func=AF.Exp), reads=["negm"], writes=["em"])
            P.op("dve", R.tensor_tensor(out=DEN[:L, :], in0=DEN[:L, :], in1=EM[:L, :], op=ALU.max), reads=["den", "em"], writes=["den"])
            P.op("dve", R.reciprocal(out=RDEN[:L, :], in_=DEN[:L, :]), reads=["den"], writes=["rden"])
            P.op("dve", R.tensor_tensor(out=WR[:L, :], in0=WINT[:L, :], in1=RDEN[:L, :], op=ALU.mult), reads=["wint", "rden"], writes=["wr"])
            if DSTOP <= 3:
                continue
            for hp in range(2):
                for hl in range(2):
                    h = 2 * hp + hl
                    P.op("pe", R.matmul(C.PSB[hl][:L, :], lhsT=STT[:L, h, :L], rhs=VM_[:L, h * 512:(h + 1) * 512], start=True, stop=True),
                         reads=["stt", vmk], writes=[f"ps{hl}"])
                    for kc in range(4):
                        P.op("pe", R.matmul(C.PSB[2 + hl][:L, :], lhsT=Q_[:, 4 * h + kc, :L], rhs=CBF[:, h, kc, :], start=(kc == 0), stop=(kc == 3)),
                             reads=[qk, "cbf"], writes=[f"ps{2 + hl}"])
                    T_ = TT_[hl]
                    P.op("act", R.activation(out=T_[:L, :], in_=C.PSB[2 + hl][:L, :], func=AF.Copy, scale=WR[:L, h:h + 1]),
                         reads=[f"ps{2 + hl}", "wr"], writes=[("tt", hl)])
                    P.op("dve", R.scalar_tensor_tensor(out=HN[:L, h, :], in0=C.PSB[hl][:L, :], scalar=RDEN[:L, h:h + 1], in1=T_[:L, :],
                                                       op0=ALU.mult, op1=ALU.add), reads=[f"ps{hl}", "rden", ("tt", hl)], writes=["hn"])
            if DSTOP <= 4:
                continue
            P.op("dve", R.tensor_tensor(out=HN[:L, :, :], in0=HN[:L, :, :], in1=OS_[:L, :].rearrange("p (h v) -> p h v", h=4), op=ALU.mult),
                 reads=["hn", osk], writes=["hn"])
            P.op("dve", R.tensor_reduce(out=MEAN4[:L, :], in_=HN[:L, :, :], axis=mybir.AxisListType.X, op=ALU.add), reads=["hn"], writes=["mean4"])
            P.op("pool", R.tensor_scalar(out=MEAN4[:L, :], in0=MEAN4[:L, :], scalar1=-1.0 / 512.0, scalar2=None, op0=ALU.mult), reads=["mean4"], writes=["mean4"])
            P.op("dve", R.tensor_tensor(out=HN[:L, :, :], in0=HN[:L, :, :], in1=MEAN4[:L, :].rearrange("p (h o) -> p h o", o=1).to_broadcast([L, 4, 512]),
                                        op=ALU.add), reads=["hn", "mean4"], writes=["hn"])
            for h in range(4):
                P.op("act", R.activation(out=HJ[:L, :], in_=HN[:L, h, :], func=AF.Square, accum_out=VAR4[:L, h:h + 1]),
                     reads=["hn"], writes=["hj", "var4"])
            P.op("act", R.activation(out=VAR4[:L, :], in_=VAR4[:L, :], func=AF.Sqrt, bias=C.epsc[:L, 0:1], scale=1.0 / 512.0),
                 reads=["var4", "epsc"], writes=["var4"])
            P.op("dve", R.reciprocal(out=VAR4[:L, :], in_=VAR4[:L, :]), reads=["var4"], writes=["var4"])
            P.op("dve", R.tensor_tensor(out=HNB[:L, :].rearrange("p (h v) -> p h v", h=4), in0=HN[:L, :, :],
                                        in1=VAR4[:L, :].rearrange("p (h o) -> p h o", o=1).to_broadcast([L, 4, 512]), op=ALU.mult),
                 reads=["hn", "var4"], writes=["hnb"])
            pA = C.PSALL.bitcast(BF16)[:, 4 * 1024:4 * 1024 + 16 * L].rearrange("p (c t) -> p c t", c=16)
            for c in range(16):
                P.op("pe", R.transpose(out=pA[:, c, :], in_=HNB[:L, c * 128:(c + 1) * 128], identity=IDN[:L, :L]), reads=["hnb", "idn"], writes=["ps4"])
            P.op("act", R.activation(out=AOc[s2][:, :, :L], in_=pA, func=AF.Copy), reads=["ps4"], writes=[("aoc", s2)])
            P.op("sp", R.dma_start(out=ao_d[:, :, tc:tc + L], in_=AOc[s2][:, :, :L]), reads=[("aoc", s2)], dma=f"aoc{s2}")
            if DSTOP <= 5:
                continue
            P.op("pe", R.matmul(p7[:, 16:24], lhsT=SEL[L][:L, :], rhs=MM[:L, :], start=True, stop=True), reads=["sel", "mm", "fc"], writes=["ps7"])
            P.op("act", R.activation(out=BC8[:, :], in_=p7[:, 16:24], func=AF.Copy), reads=["ps7"], writes=["bc8"])
            P.op("dve", R.tensor_tensor(out=DD[:, :], in0=BC8[:, 4:8], in1=BC8[:, 0:4], op=ALU.subtract), reads=["bc8"], writes=["dd"])
            P.op("dve", R.tensor_tensor(out=WE[:L, :], in0=RV[:L, :], in1=DD[:L, :], op=ALU.add), reads=["rv", "dd"], writes=["we"])
            P.op("act", R.activation(out=WE[:L, :], in_=WE[:L, :], func=AF.Exp), reads=["we"], writes=["we"])
            P.op("dve", R.tensor_tensor(out=DECAY[:, :], in0=DD[:, :], in1=M0[:, :], op=ALU.add), reads=["dd", "m0"], writes=["decay"])
            P.op("act", R.activation(out=DECAY[:, :], in_=DECAY[:, :], func=AF.Exp), reads=["decay"], writes=["decay"])
            P.op("pool", R.tensor_copy(out=M0[:, :], in_=BC8[:, 0:4]), reads=["bc8"], writes=["m0"])
            P.op("dve", R.tensor_tensor(out=KW[:L, :, :], in0=KM_[:L, :].rearrange("p (h k) -> p h k", h=4),
                                        in1=WE[:L, :].rearrange("p (h o) -> p h o", o=1).to_broadcast([L, 4, 512]), op=ALU.mult),
                 reads=[kmk, "we"], writes=["kw"])
            if DSTOP <= 6:
                continue
            bi = 0
            for h in range(4):
                for kc in range(4):
                    b = bi % 4
                    bi += 1
                    P.op("pe", R.matmul(C.PSB[b][:, :], lhsT=KW[:L, h, kc * 128:(kc + 1) * 128], rhs=VM_[:L, h * 512:(h + 1) * 512],
                                        start=True, stop=True), reads=["kw", vmk], writes=[f"ps{b}"])
                    P.op("dve", R.scalar_tensor_tensor(out=CST[:, h, kc, :], in0=CST[:, h, kc, :], scalar=DECAY[:, h:h + 1], in1=C.PSB[b][:, :],
                                                       op0=ALU.mult, op1=ALU.add), reads=[f"ps{b}", "decay", "cst"], writes=["cst"])
                    if DSTOP <= 7:
                        continue
                    P.op("pe", R.matmul(p7[:, 32 + 2 * (4 * h + kc):34 + 2 * (4 * h + kc)], lhsT=KW[:L, h, kc * 128:(kc + 1) * 128], rhs=ONEB[:L, :],
                                        start=True, stop=True), reads=["kw", "oneb"], writes=["ps7"])
            P.op("dve", R.tensor_tensor(out=NS[:, :, :], in0=NS[:, :, :], in1=DECAY[:, :].rearrange("p (h o) -> p h o", o=1).to_broadcast([128, 4, 4]),
                                        op=ALU.mult), reads=["ns", "decay"], writes=["ns"])
            P.op("dve", R.tensor_tensor(out=NS[:, :, :], in0=p7[:, 32:64].rearrange("p (h k two) -> p h k two", h=4, two=2)[:, :, :, 0], in1=NS[:, :, :], op=ALU.add),
                 reads=["ps7", "ns"], writes=["ns"])
            P.op("act", R.activation(out=CBF[:], in_=CST[:], func=AF.Copy), reads=["cst"], writes=["cbf"])
            P.op("act", R.activation(out=NSB[:], in_=NS[:], func=AF.Copy), reads=["ns"], writes=["nsb"])
        dc, dn, dm = (C.d_dc_s, C.d_dn_s, C.d_dm_s) if sq is None else (C.d_dc_p[sq], C.d_dn_p[sq], C.d_dm_p[sq])
        P.op("sp", R.dma_start(out=dc.rearrange("h (kc p) v -> p h kc v", p=128), in_=CST[:]), reads=["cst"], dma="dco")
        P.op("sp", R.dma_start(out=dn, in_=NS[:].rearrange("p h k -> p (h k)")), reads=["ns"], dma="dno")
        P.op("sp", R.dma_start(out=dm, in_=M0[0:1, :]), reads=["m0"], dma="dmo")
    outproj_phase(P, C, "dwout", C.sAO, 16, C.d_d_w_out, Xin, Xout, lnidx, tiles, rowscale=C.d_d_ng)


def build(cfg):
    nc = bass.Bass("TRN2", target_bir_lowering=False)
    es = ExitStack()
    P = Prog(nc, es)
    C = Ctx()
    C.SEQ = cfg.get("seq", SEQ)
    C.NPS = cfg.get("nps", NPS)
    C.TP = C.SEQ * C.NPS
    C.TT = C.TP + TS
    S, TPc, tt = C.SEQ, C.TP, C.TT
    C.tiles = [(i * 512, 512) for i in range(TPc // 512)] + [(TPc, TS)]
    phases = cfg.get("phases", None)

    def din(name, shape, dt=F32):
        return nc.dram_tensor(name, list(shape), dt, kind="ExternalInput").ap()

    def dout(name, shape, dt=F32):
        return nc.dram_tensor(name, list(shape), dt, kind="ExternalOutput").ap()

    def dscr(name, shape, dt):
        return nc.dram_tensor(name, list(shape), dt).ap()

    C.d_x = din("x_in", [D, tt])
    C.d_lng = din("ln_g", [128, 12, KD])
    C.d_lnb = din("ln_b", [128, 12, KD])
    C.d_wg = [din(f"wg{i}", [D, DFF]) for i in range(8)]
    C.d_wu = [din(f"wu{i}", [D, DFF]) for i in range(8)]
    C.d_wd = [din(f"wd{i}", [DFF, D]) for i in range(8)]
    C.d_y = dout("y_out", [D, tt])
    C.sX = [dscr(f"sx{i}", [D, tt], F32) for i in range(2)]
    C.sAO = dscr("s_ao", [2048, tt], BF16)
    keep = min(512, S)
    C.d_b_w_in = din("b_w_in", [D, 3072])
    C.d_b_w_out = din("b_w_out", [D, D])
    C.d_b_bias = din("b_bias", [128, 16, 2, 128])
    C.d_b_far = din("b_far", [128, 16])
    C.d_cbkT = din("cache_b_kT", [1024, 512])
    C.d_cbk_k = din("cache_b_k", [512, 1024])
    C.d_cbk_v = din("cache_b_v", [512, 1024])
    C.d_bkp = dout("b_k_p", [C.NPS, keep, 1024])
    C.d_bvp = dout("b_v_p", [C.NPS, keep, 1024])
    C.d_bks = dout("b_k_s", [512, 1024])
    C.d_bvs = dout("b_v_s", [512, 1024])
    C.sB_QT = dscr("sb_qt", [1024, tt], BF16)
    C.sB_KT = dscr("sb_kt", [1024, tt + 512], BF16)
    C.sB_V = dscr("sb_v", [tt + 512, 1024], BF16)
    C.d_a_w_in = din("a_w_in", [D, 2120])
    C.d_a_w_out = din("a_w_out", [D, D])
    C.d_a_bias = din("a_bias", [128, 3, 8, 128])
    C.d_ident = din("ident", [128, 128])
    C.d_cakT = din("cache_a_kT", [256, 1024])
    C.d_cav = din("cache_a_v", [1024, 256])
    C.d_cakiT = din("cache_a_kiT", [64, 1024])
    C.d_akp = dout("a_k_p", [C.NPS, S, 256])
    C.d_avp = dout("a_v_p", [C.NPS, S, 256])
    C.d_aip = dout("a_kidx_p", [C.NPS, S, 64])
    C.d_aks = dout("a_k_s", [TS, 256])
    C.d_avs = dout("a_v_s", [TS, 256])
    C.d_ais = dout("a_kidx_s", [TS, 64])
    C.sA_QT = dscr("sa_qt", [1024, tt], BF16)
    C.sA_KT = dscr("sa_kt", [256, tt + 1024], BF16)
    C.sA_V = dscr("sa_v", [tt + 1024, 256], BF16)
    C.sA_QIT = dscr("sa_qit", [512, tt], BF16)
    C.sA_KIT = dscr("sa_kit", [64, tt + 1024], BF16)
    C.sA_WI = dscr("sa_wi", [tt, 8], F32)
    C.d_c_w_in = din("c_w_in", [D, 6176])
    C.d_c_w_out = din("c_w_out", [2048, D])
    C.d_c_cw = din("c_cw", [128, 32, 4])
    C.d_c_cb = din("c_cb", [128, 32])
    C.d_c_dtb = din("c_dtb", [128, 32])
    C.d_c_alog = din("c_alog", [128, 32])
    C.d_c_dsk = din("c_dsk", [128, 16])
    C.d_c_ng = din("c_ng", [128, 16])
    C.d_ut = din("ut", [64, 64])
    C.d_negt = din("negt", [64, 64])
    C.d_c_conv0 = din("c_conv0", [4096, 3])
    C.d_c_ssm0 = din("c_ssm0", [128, 2048])
    C.d_cssm_p = dout("c_ssm_p", [C.NPS, 128, 2048])
    C.d_cssm_s = dout("c_ssm_s", [128, 2048])
    C.d_cconv_p = dout("c_conv_p", [C.NPS, 4096, 3])
    C.d_cconv_s = dout("c_conv_s", [4096, 3])
    C.sC_ZT = dscr("sc_zt", [2048, tt], BF16)
    C.sC_XC = dscr("sc_xc", [4096, tt], BF16)
    C.sC_XTM = dscr("sc_xtm", [tt, 3072], BF16)
    C.sC_DTS = dscr("sc_dts", [tt, 96], F32)
    C.d_d_w_in = din("d_w_in", [D, 6152])
    C.d_d_w_out = din("d_w_out", [2048, D])
    C.d_d_cw = din("d_cw", [128, 16, 4])
    C.d_d_cb = din("d_cb", [128, 16])
    C.d_d_gb = din("d_gb", [128, 8])
    C.d_d_wq = din("d_wq", [128, 16, 128])
    C.d_d_wk = din("d_wk", [128, 16, 128])
    C.d_d_ng = din("d_ng", [128, 16])
    C.d_identf = din("identf", [64, 64])
    C.d_negu = din("negu", [64, 64])
    C.d_sel64 = din("sel64", [64, 128])
    C.d_sel32 = din("sel32", [32, 128])
    C.d_d_conv0 = din("d_conv0", [2048, 3])
    C.d_d_c0 = din("d_c0", [4, 512, 512])
    C.d_d_n0 = din("d_n0", [128, 16])
    C.d_d_m0 = din("d_m0", [128, 4])
    C.d_dc_p = dout("d_c_p", [C.NPS, 4, 512, 512])
    C.d_dn_p = dout("d_n_p", [C.NPS, 128, 16])
    C.d_dm_p = dout("d_m_p", [C.NPS, 1, 4])
    C.d_dconv_p = dout("d_conv_p", [C.NPS, 2048, 3])
    C.d_dc_s = dout("d_c_s", [4, 512, 512])
    C.d_dn_s = dout("d_n_s", [128, 16])
    C.d_dm_s = dout("d_m_s", [1, 4])
    C.d_dconv_s = dout("d_conv_s", [2048, 3])
    C.sD_QT = dscr("sd_qt", [2048, tt], BF16)
    C.sD_KT = dscr("sd_kt", [2048, tt], BF16)
    C.sD_KTM = dscr("sd_ktm", [tt, 2048], BF16)
    C.sD_VTM = dscr("sd_vtm", [tt, 2048], BF16)
    C.sD_OS = dscr("sd_os", [tt, 2048], BF16)
    C.sD_GS = dscr("sd_gs", [tt, 8], F32)
    with es:
        setup_common(P, C)
        if phases is None:
            phases = []
            for l in range(4):
                phases += [("ffn", 2 * l), ("mix", l), ("ffn", 2 * l + 1)]
        cur = C.d_x
        for pi, ph in enumerate(phases):
            nxt = C.d_y if pi == len(phases) - 1 else C.sX[pi % 2]
            if ph[0] == "ffn":
                i = ph[1]
                l, which = divmod(i, 2)
                ffn_phase(P, C, i, cur, nxt, C.d_wg[i], C.d_wu[i], C.d_wd[i], 3 * l + (0 if which == 0 else 2), C.tiles)
            elif ph[0] == "mix":
                l = ph[1]
                [mixer_a, mixer_b, mixer_c, mixer_d][l](P, C, cur, nxt, 3 * l + 1)
            cur = nxt
        P.emit()
    return nc, P


def _fm(a):
    return np.ascontiguousarray(a.T)


def prep_core_inputs(inp, c, S=SEQ, nps=NPS):
    f = np.float32
    m = {}
    xp = inp["x_prompt"][c * nps:(c + 1) * nps].reshape(nps * S, D)
    xs = inp["x_sample"][c]
    m["x_in"] = np.ascontiguousarray(np.concatenate([xp, xs], 0).T)
    m["ln_g"] = np.ascontiguousarray(inp["ln_g"].reshape(12, KD, 128).transpose(2, 0, 1))
    m["ln_b"] = np.ascontiguousarray(inp["ln_b"].reshape(12, KD, 128).transpose(2, 0, 1))
    for l in range(4):
        for w, nm in ((0, "ffn1"), (1, "ffn2")):
            m[f"wg{2 * l + w}"] = inp[f"{nm}_wg"][l]
            m[f"wu{2 * l + w}"] = inp[f"{nm}_wu"][l]
            m[f"wd{2 * l + w}"] = inp[f"{nm}_wd"][l]
    m["b_w_in"] = inp["b_w_in"]
    m["b_w_out"] = inp["b_w_out"]
    tab = inp["b_rel_table"]
    ss = np.arange(128)[:, None]
    tt_ = np.arange(128)[None, :]
    bt = np.empty((128, 16, 2, 128), f)
    for d in range(2):
        idx = np.minimum(128 * (1 + d) + tt_ - ss, 256)
        bt[:, :, d, :] = tab[:, idx].transpose(1, 0, 2)
    m["b_bias"] = bt
    m["b_far"] = np.ascontiguousarray(np.broadcast_to(tab[:, 256][None, :], (128, 16)))
    ck = inp["cache_b_k"][c].reshape(512, 1024)
    m["cache_b_kT"] = _fm(ck)
    m["cache_b_k"] = np.ascontiguousarray(ck)
    m["cache_b_v"] = np.ascontiguousarray(inp["cache_b_v"][c].reshape(512, 1024))
    m["a_w_in"] = inp["a_w_in"]
    m["a_w_out"] = inp["a_w_out"]
    t5 = inp["t5_table"]
    ab = np.empty((128, 3, 8, 128), f)
    for d in range(2):
        rel = ss - tt_ - 128 * d
        ab[:, d, :, :] = t5[_t5_bucket(rel)].transpose(0, 2, 1)
    ab[:, 2, :, :] = t5[15][None, :, None]
    m["a_bias"] = ab
    m["ident"] = np.eye(128, dtype=f)
    m["d_w_in"] = inp["d_w_in"]
    m["d_w_out"] = inp["d_w_out"]
    m["d_cw"] = np.ascontiguousarray(inp["d_conv_w"].reshape(4, 16, 128).transpose(2, 1, 0))
    m["d_cb"] = np.ascontiguousarray(inp["d_conv_b"].reshape(16, 128).T)
    m["d_gb"] = np.ascontiguousarray(np.broadcast_to(inp["d_gate_b"][None, :], (128, 8)))
    for nm, src in (("d_wq", inp["d_wq_blk"]), ("d_wk", inp["d_wk_blk"])):
        bd = np.zeros((16, 32, 4, 32, 4), f)
        blk = src.reshape(16, 32, 4, 4)
        for b_ in range(32):
            bd[:, b_, :, b_, :] = blk[:, b_]
        m[nm] = np.ascontiguousarray(bd.reshape(16, 128, 128).transpose(1, 0, 2))
    m["d_ng"] = np.ascontiguousarray(inp["d_norm_g"].reshape(16, 128).T)
    m["identf"] = np.eye(64, dtype=f)
    m["negu"] = np.triu(np.full((64, 64), NEG, f), 1)
    s64 = np.zeros((64, 128), f); s64[63] = 1.0
    s32 = np.zeros((32, 128), f); s32[31] = 1.0
    m["sel64"], m["sel32"] = s64, s32
    m["d_conv0"] = _fm(inp["state_d_conv"][c])
    m["d_c0"] = np.ascontiguousarray(inp["state_d_c"][c])
    m["d_n0"] = np.ascontiguousarray(inp["state_d_n"][c].reshape(16, 128).T)
    m["d_m0"] = np.ascontiguousarray(np.broadcast_to(inp["state_d_m"][c][None, :], (128, 4)))
    m["c_w_in"] = inp["c_w_in"]
    m["c_w_out"] = inp["c_w_out"]
    m["c_cw"] = np.ascontiguousarray(inp["c_conv_w"].reshape(4, 32, 128).transpose(2, 1, 0))
    m["c_cb"] = np.ascontiguousarray(inp["c_conv_b"].reshape(32, 128).T)
    m["c_dtb"] = np.ascontiguousarray(np.broadcast_to(inp["c_dt_bias"][None, :], (128, 32)))
    m["c_alog"] = np.ascontiguousarray(np.broadcast_to(inp["c_a_log"][None, :], (128, 32)))
    m["c_dsk"] = np.ascontiguousarray(np.repeat(inp["c_d_skip"].reshape(16, 2), 64, axis=1).T)
    m["c_ng"] = np.ascontiguousarray(inp["c_norm_g"].reshape(16, 128).T)
    m["ut"] = np.triu(np.ones((64, 64), f))
    m["negt"] = np.tril(np.full((64, 64), NEG, f), -1)
    m["c_conv0"] = _fm(inp["state_c_conv"][c])
    m["c_ssm0"] = np.ascontiguousarray(inp["state_c_ssm"][c].reshape(2048, 128).T)
    m["cache_a_kT"] = _fm(inp["cache_a_k"][c].reshape(1024, 256))
    m["cache_a_v"] = np.ascontiguousarray(inp["cache_a_v"][c].reshape(1024, 256))
    m["cache_a_kiT"] = _fm(inp["cache_a_kidx"][c])
    return m


def _t5_bucket(rel):
    half, max_exact = 16, 8
    ret = np.where(rel > 0, half, 0)
    n = np.abs(rel)
    nf = np.maximum(n, 1).astype(np.float32)
    large = max_exact + (np.log(nf / np.float32(max_exact)) / np.float32(np.log(128 / 8)) * np.float32(half - max_exact)).astype(np.int32)
    large = np.minimum(large, half - 1)
    return ret + np.where(n < max_exact, n, large)


_OUT_ORDER = ("y_prompt", "y_sample", "a_k_p", "a_v_p", "a_kidx_p", "b_k_p", "b_v_p", "c_ssm_p", "c_conv_p",
              "d_c_p", "d_n_p", "d_m_p", "d_conv_p", "a_k_s", "a_v_s", "a_kidx_s", "b_k_s", "b_v_s",
              "c_ssm_s", "c_conv_s", "d_c_s", "d_n_s", "d_m_s", "d_conv_s")


def assemble(results, S=SEQ, nps=NPS):
    o = {k: [] for k in _OUT_ORDER}
    keep = min(512, S)
    for r in results:
        y = r["y_out"].T
        o["y_prompt"].append(y[:nps * S].reshape(nps, S, D))
        o["y_sample"].append(y[nps * S:].reshape(1, TS, D))
        o["a_k_p"].append(r["a_k_p"].reshape(nps, S, 2, 128))
        o["a_v_p"].append(r["a_v_p"].reshape(nps, S, 2, 128))
        o["a_kidx_p"].append(r["a_kidx_p"].reshape(nps, S, 64))
        o["b_k_p"].append(r["b_k_p"].reshape(nps, keep, 16, 64))
        o["b_v_p"].append(r["b_v_p"].reshape(nps, keep, 16, 64))
        o["c_ssm_p"].append(r["c_ssm_p"].reshape(nps, 128, 32, 64).transpose(0, 2, 3, 1))
        o["c_conv_p"].append(r["c_conv_p"].transpose(0, 2, 1))
        o["d_c_p"].append(r["d_c_p"])
        o["d_n_p"].append(r["d_n_p"].transpose(0, 2, 1).reshape(nps, 4, 512))
        o["d_m_p"].append(r["d_m_p"].reshape(nps, 4))
        o["d_conv_p"].append(r["d_conv_p"].transpose(0, 2, 1))
        o["a_k_s"].append(r["a_k_s"].reshape(1, TS, 2, 128))
        o["a_v_s"].append(r["a_v_s"].reshape(1, TS, 2, 128))
        o["a_kidx_s"].append(r["a_kidx_s"].reshape(1, TS, 64))
        o["b_k_s"].append(r["b_k_s"].reshape(1, 512, 16, 64))
        o["b_v_s"].append(r["b_v_s"].reshape(1, 512, 16, 64))
        o["c_ssm_s"].append(r["c_ssm_s"].reshape(1, 128, 32, 64).transpose(0, 2, 3, 1))
        o["c_conv_s"].append(r["c_conv_s"].T[None])
        o["d_c_s"].append(r["d_c_s"][None])
        o["d_n_s"].append(r["d_n_s"].T.reshape(1, 4, 512))
        o["d_m_s"].append(r["d_m_s"].reshape(1, 4))
        o["d_conv_s"].append(r["d_conv_s"].T[None])
    return tuple(np.ascontiguousarray(np.concatenate(o[k], 0), dtype=np.float32) for k in _OUT_ORDER)


def kernel(**inputs):
    inp = {k: np.asarray(v) for k, v in inputs.items()}
    nc, _ = build({})
    in_maps = [prep_core_inputs(inp, c) for c in range(NCORES)]
    res = run_bass_kernel_spmd(nc, in_maps, core_ids=list(range(NCORES)))
    return assemble(res.results)
```

```python
import numpy as np
from contextlib import ExitStack
import concourse.bass as bass
import concourse.mybir as mybir
from concourse.bass_utils import run_bass_kernel_spmd

F32 = mybir.dt.float32
BF16 = mybir.dt.bfloat16
ALU = mybir.AluOpType
AF = mybir.ActivationFunctionType

D = 1024
KD = 8
DFF = 2816
KF = 22
SEQ = 4096
NPS = 2
TS = 32
TP = NPS * SEQ
TT = TP + TS
ALPHA = (2.0 * 4) ** 0.25
EPS = 1e-5
NCORES = 8

ENGS = ["pe", "act", "dve", "pool", "sp"]


class _Rec:
    def __getattr__(self, name):
        def f(*a, **k):
            return (name, a, k)
        return f


R = _Rec()


class Tok:
    __slots__ = ("sem", "val", "eng", "idx", "snap", "dma")

    def __init__(self, sem, val, eng, idx, snap, dma):
        self.sem, self.val, self.eng, self.idx, self.snap, self.dma = sem, val, eng, idx, snap, dma


class Prog:
    def __init__(self, nc, es):
        self.nc, self.es = nc, es
        self.streams = {e: [] for e in ENGS}
        self.n = {e: 0 for e in ENGS}
        self.clock = {e: {} for e in ENGS}
        self.lastw = {}
        self.readers = {}
        self.sems = {}
        self.semcnt = {}
        self.last_tok = {e: None for e in ENGS}
        self.dma_last = {}
        self.lmap = {}
        self.free_phys = []
        self.all_phys = []
        self.n_wait = 0

    def sem(self, name):
        if name not in self.sems:
            self.sems[name] = self.es.enter_context(self.nc.semaphore(name))
            self.semcnt[name] = 0
        return self.sems[name]

    def sb(self, name, shape, dt):
        return self.es.enter_context(self.nc.sbuf_tensor(name, list(shape), dt))

    def ps(self, name, shape, dt=F32):
        return self.es.enter_context(self.nc.psum_tensor(name, list(shape), dt))

    def _need(self, eng, tk, waits):
        if tk is None:
            return
        if (not tk.dma) and tk.eng == eng:
            if eng == "pe" or tk.idx < self.n[eng] - 3:
                return
        ck = self.clock[eng]
        if ck.get(tk.sem, 0) >= tk.val:
            return
        waits[tk.sem] = max(waits.get(tk.sem, 0), tk.val)
        for s, v in tk.snap.items():
            if ck.get(s, 0) < v:
                ck[s] = v
        ck[tk.sem] = tk.val

    def op(self, eng, fn, reads=(), writes=(), dma=None, extra=()):
        waits = {}
        psr = [k for k in reads if isinstance(k, str) and k.startswith("ps")]
        if psr:
            reads = [k for k in reads if k not in psr]
            writes = list(writes) + psr
        for k in reads:
            self._need(eng, self.lastw.get(k), waits)
        for k in writes:
            self._need(eng, self.lastw.get(k), waits)
            for tk in self.readers.get(k, {}).values():
                self._need(eng, tk, waits)
        for tk in extra:
            self._need(eng, tk, waits)
        idx = self.n[eng]
        if dma is not None:
            lname = "d_" + dma
            if lname not in self.lmap:
                if self.free_phys:
                    self.lmap[lname] = self.free_phys.pop()
                else:
                    self.lmap[lname] = f"dq{len(self.all_phys)}"
                    self.all_phys.append(self.lmap[lname])
            sname = self.lmap[lname]
            self.sem(sname)
            self.semcnt[sname] += 16
            tk = Tok(sname, self.semcnt[sname], eng, idx, dict(self.clock[eng]), True)
            self.dma_last[sname] = tk
            inc = (sname, 16)
        else:
            sname = "e_" + eng
            self.sem(sname)
            self.n[eng] += 1
            self.semcnt[sname] = self.n[eng]
            tk = Tok(sname, self.n[eng], eng, idx, dict(self.clock[eng]), False)
            inc = (sname, 1)
        self.n_wait += len(waits)
        self.streams[eng].append((list(waits.items()), fn, inc))
        for k in writes:
            self.lastw[k] = tk
            self.readers[k] = {}
        for k in reads:
            self.readers.setdefault(k, {})[(eng, sname)] = tk
        if dma is None:
            self.last_tok[eng] = tk
        return tk

    def barrier(self):
        toks = [t for t in self.last_tok.values() if t is not None] + list(self.dma_last.values())
        for e in ENGS:
            self.op(e, R.nop(), extra=toks)
        self.lastw.clear()
        self.readers.clear()
        self.free_phys = list(self.all_phys)
        self.lmap.clear()

    def emit(self):
        nc = self.nc
        finals = [(s, c) for s, c in self.semcnt.items() if c > 0]
        with nc.Block() as block:
            def run(ename, engine):
                for waits, fn, inc in self.streams[ename]:
                    for s, v in waits:
                        engine.wait_ge(self.sems[s], v)
                    try:
                        ins = getattr(engine, fn[0])(*fn[1], **fn[2])
                    except Exception:
                        print("FAILED OP:", ename, fn[0], fn[1], fn[2])
                        raise
                    ins.then_inc(self.sems[inc[0]], inc[1])
                if ename == "sp":
                    for s, c in finals:
                        engine.wait_ge(self.sems[s], c)

            @block.tensor
            def _(e):
                run("pe", e)

            @block.scalar
            def _(e):
                run("act", e)

            @block.vector
            def _(e):
                run("dve", e)

            @block.gpsimd
            def _(e):
                run("pool", e)

            @block.sync
            def _(e):
                run("sp", e)


class Ctx:
    pass


class Arena:
    def __init__(self, t, nelem):
        self.t, self.nelem, self.off = t, nelem, 0

    def reset(self):
        self.off = 0

    def alloc(self, shape, dt):
        n = int(np.prod(shape))
        ne = n * (2 if dt == F32 else 1)
        ne = (ne + 15) // 16 * 16
        assert self.off + ne <= self.nelem, ("arena overflow", self.off, ne, self.nelem)
        v = self.t[:, self.off:self.off + (n * 2 if dt == F32 else n)]
        self.off += ne
        if dt == F32:
            v = v.bitcast(F32)
        if len(shape) == 2:
            return v.rearrange("p (a b) -> p a b", a=shape[0])
        if len(shape) == 3:
            return v.rearrange("p (a b c) -> p a b c", a=shape[0], b=shape[1])
        return v


def setup_common(P, C):
    C.A = Arena(P.sb("arena", [128, ARENA_N], BF16), ARENA_N)
    C.ones = P.sb("ones_bf", [128, 128], BF16)
    C.lng = P.sb("lng", [128, 12, KD], F32)
    C.lnb = P.sb("lnb", [128, 12, KD], F32)
    C.epsc = P.sb("epsc", [128, 1], F32)
    C.PSALL = P.ps("psall", [128, 4096], F32)
    C.PSB = [C.PSALL[:, i * 512:(i + 1) * 512] for i in range(8)]
    C.onec = P.sb("onec", [128, 1], F32)
    P.op("pool", R.memset(C.onec[:], 1.0), writes=["onec"])
    P.op("pool", R.memset(C.ones[:], 1.0 / D), writes=["ones"])
    P.op("pool", R.memset(C.epsc[:], EPS), writes=["epsc"])
    P.op("sp", R.dma_start(out=C.lng[:], in_=C.d_lng),
         writes=["lng"], dma="lng")
    P.op("sp", R.dma_start(out=C.lnb[:], in_=C.d_lnb),
         writes=["lnb"], dma="lnb")


def load_w(P, C, name, dram_ap, kin, nout, issue=True):
    view = C.A.alloc([kin, nout], BF16)
    src = dram_ap.rearrange("(k p) n -> p k n", p=128)
    jobs = []
    for c0 in range(0, nout, WSTEP):
        c1 = min(nout, c0 + WSTEP)
        blk = []
        for k in range(kin):
            blk.append((P, view, src, name, k, c0, c1))
        jobs.append(blk)
    if issue:
        for blk in jobs:
            for j in blk:
                issue_w(*j)
        return view
    return view, jobs


def issue_w(P, view, src, name, k, c0, c1):
    P.op("pool", R.dma_start(out=view[:, k, c0:c1], in_=src[:, k, c0:c1]), writes=[(name, k, c0)], dma=f"w_{name[:2]}_{k % 4}")


WSTEP = 1024


def wkeys(name, k, c0, c1):
    return [(name, k, c) for c in range((c0 // WSTEP) * WSTEP, c1, WSTEP)]


def ln_bufs(P, C):
    L = Ctx()
    L.MEAN = C.A.alloc([512], F32)
    L.MSQ = C.A.alloc([512], F32)
    L.RSTD = C.A.alloc([512], F32)
    L.TMP = [C.A.alloc([512], F32) for _ in range(2)]
    return L


def postnorm_tile(P, C, L, lnidx, zk, Z, ZB, zbk, ZQ, zqk, N):
    ps1, ps2 = C.PSB[6], C.PSB[7]
    P.op("act", R.activation(out=ZB[:, :, :N], in_=Z[:, :, :N], func=AF.Copy),
         reads=[zk], writes=zbk)
    P.op("act", R.activation(out=ZQ[:, :, :N], in_=Z[:, :, :N], func=AF.Square),
         reads=[zk], writes=zqk)
    for m in range(KD):
        P.op("pe", R.matmul(ps1[:, :N], lhsT=C.ones[:], rhs=ZB[:, m, :N],
                                             start=(m == 0), stop=(m == KD - 1)),
             reads=["ones"] + zbk, writes=["ps6"])
    for m in range(KD):
        P.op("pe", R.matmul(ps2[:, :N], lhsT=C.ones[:], rhs=ZQ[:, m, :N],
                                             start=(m == 0), stop=(m == KD - 1)),
             reads=["ones"] + zqk, writes=["ps7"])
    P.op("act", R.activation(out=L.MEAN[:, :N], in_=ps1[:, :N], func=AF.Copy),
         reads=["ps6"], writes=["mean"])
    P.op("act", R.activation(out=L.MSQ[:, :N], in_=ps1[:, :N], func=AF.Square),
         reads=["ps6"], writes=["msq"])
    P.op("dve", R.tensor_tensor(out=L.RSTD[:, :N], in0=ps2[:, :N], in1=L.MSQ[:, :N], op=ALU.subtract),
         reads=["ps7", "msq"], writes=["rstd"])
    P.op("act", R.activation(out=L.RSTD[:, :N], in_=L.RSTD[:, :N], func=AF.Sqrt, bias=C.epsc[:, 0:1]),
         reads=["rstd", "epsc"], writes=["rstd"])
    P.op("dve", R.reciprocal(out=L.RSTD[:, :N], in_=L.RSTD[:, :N]), reads=["rstd"], writes=["rstd"])
    for m in range(KD):
        T = L.TMP[m % 2]
        tk = ("lntmp", m % 2)
        P.op("dve", R.tensor_tensor(out=T[:, :N], in0=Z[:, m, :N], in1=L.MEAN[:, :N],
                                                       op=ALU.subtract),
             reads=[zk, "mean"], writes=[tk])
        P.op("dve", R.tensor_tensor(out=T[:, :N], in0=T[:, :N], in1=L.RSTD[:, :N], op=ALU.mult),
             reads=[tk, "rstd"], writes=[tk])
        P.op("act", R.activation(out=Z[:, m, :N], in_=T[:, :N], func=AF.Identity,
                                                     bias=C.lnb[:, lnidx, m:m + 1], scale=C.lng[:, lnidx, m:m + 1]),
             reads=[tk, "lng", "lnb"], writes=[zk])


def run_streams(gens):
    gens = list(gens)
    while gens:
        for g in list(gens):
            try:
                next(g)
            except StopIteration:
                gens.remove(g)


def tiles_of(with_sample=True):
    t = [(i * 512, 512) for i in range(TP // 512)]
    if with_sample:
        t.append((TP, TS))
    return t


def ffn_phase(P, C, pid, Xin, Xout, wg, wu, wd, lnidx, tiles):
    P.barrier()
    C.A.reset()
    WG, jg = load_w(P, C, f"wg{pid}", wg, KD, DFF, issue=False)
    WU, ju = load_w(P, C, f"wu{pid}", wu, KD, DFF, issue=False)
    for bg, bu in zip(jg, ju):
        for j in bg + bu:
            issue_w(*j)
    WD = load_w(P, C, f"wd{pid}", wd, KF, D)
    X = C.A.alloc([KD, 512], F32)
    XBs = [C.A.alloc([KD, 512], BF16) for _ in range(2)]
    H = C.A.alloc([KF, 512], BF16)
    SGs = [C.A.alloc([512], BF16) for _ in range(2)]
    L = ln_bufs(P, C)
    xin = Xin.rearrange("(k p) t -> p k t", p=128)
    xout = Xout.rearrange("(k p) t -> p k t", p=128)
    xk = "x"
    for ti, (t0, N) in enumerate(tiles):
        s = ti % 2
        XB = XBs[s]
        xbk = ("xb", s)
        P.op("pool", R.dma_start(out=XB[:, :, :N], in_=xin[:, :, t0:t0 + N]),
             writes=[xbk], dma=f"xb{s}")
        for j in range(KF):
            b = j % 2
            pg, pu = C.PSB[b], C.PSB[2 + b]
            for k in range(KD):
                P.op("pe", R.matmul(
                    pg[:, :N], lhsT=WG[:, k, j * 128:(j + 1) * 128], rhs=XB[:, k, :N], start=(k == 0), stop=(k == KD - 1)),
                     reads=[xbk] + wkeys(f"wg{pid}", k, j * 128, (j + 1) * 128), writes=[f"ps{b}"])
            for k in range(KD):
                P.op("pe", R.matmul(
                    pu[:, :N], lhsT=WU[:, k, j * 128:(j + 1) * 128], rhs=XB[:, k, :N], start=(k == 0), stop=(k == KD - 1)),
                     reads=[xbk] + wkeys(f"wu{pid}", k, j * 128, (j + 1) * 128), writes=[f"ps{2 + b}"])
            SG = SGs[b]
            P.op("act", R.activation(out=SG[:, :N], in_=pg[:, :N], func=AF.Silu),
                 reads=[f"ps{b}"], writes=[("sg", b)])
            P.op("dve", R.tensor_tensor(out=H[:, j, :N], in0=pu[:, :N],
                                                                        in1=SG[:, :N], op=ALU.mult),
                 reads=[f"ps{2 + b}", ("sg", b)], writes=[("h", j)])
        P.op("sp", R.dma_start(out=X[:, :, :N], in_=xin[:, :, t0:t0 + N]),
             writes=[xk], dma="x")
        P.op("pool", R.tensor_scalar(out=X[:, :, :N], in0=X[:, :, :N], scalar1=ALPHA,
                                                    scalar2=None, op0=ALU.mult),
             reads=[xk], writes=[xk])
        for m in range(KD):
            b = m % 2
            py = C.PSB[4 + b]
            for j in range(KF):
                P.op("pe", R.matmul(
                    py[:, :N], lhsT=WD[:, j, m * 128:(m + 1) * 128], rhs=H[:, j, :N], start=(j == 0), stop=(j == KF - 1)),
                     reads=[("h", j)] + wkeys(f"wd{pid}", j, m * 128, (m + 1) * 128), writes=[f"ps{4 + b}"])
            P.op("dve", R.scalar_tensor_tensor(
                out=X[:, m, :N], in0=py[:, :N], scalar=0.5, in1=X[:, m, :N], op0=ALU.mult, op1=ALU.add),
                 reads=[f"ps{4 + b}", xk], writes=[xk])
        postnorm_tile(P, C, L, lnidx, xk, X, XB, [xbk], H[:, 0:KD, :], [("h", j) for j in range(KD)], N)
        P.op("sp", R.dma_start(out=xout[:, :, t0:t0 + N], in_=X[:, :, :N]),
             reads=[xk], dma="xst")


ARENA_N = 102400
STOP_AFTER_INPROJ = False
SC_B = 64.0 ** -0.5
NEG = -30000.0


def outproj_phase(P, C, name, AO, kin, w_out, Xin, Xout, lnidx, tiles, rowscale=None):
    P.barrier()
    C.A.reset()
    W = load_w(P, C, name, w_out, kin, D)
    if rowscale is not None:
        RSC = C.A.alloc([kin], F32)
        P.op("sp", R.dma_start(out=RSC[:], in_=rowscale), writes=["rsc"], dma="rsc")
        for j in range(kin):
            P.op("pool", R.tensor_scalar(out=W[:, j, :], in0=W[:, j, :], scalar1=RSC[:, j:j + 1], scalar2=None, op0=ALU.mult),
                 reads=["rsc"] + wkeys(name, j, 0, D), writes=wkeys(name, j, 0, D))
    X = C.A.alloc([KD, 512], F32)
    AOs = [C.A.alloc([kin, 512], BF16) for _ in range(2)]
    ZB = C.A.alloc([KD, 512], BF16)
    ZQ = C.A.alloc([KD, 512], BF16)
    L = ln_bufs(P, C)
    xin = Xin.rearrange("(k p) t -> p k t", p=128)
    xout = Xout.rearrange("(k p) t -> p k t", p=128)
    ao = AO.rearrange("(k p) t -> p k t", p=128)
    xk = "x"
    for ti, (t0, N) in enumerate(tiles):
        s = ti % 2
        A_ = AOs[s]
        ak = ("ao", s)
        P.op("sp", R.dma_start(out=A_[:, :, :N], in_=ao[:, :, t0:t0 + N]),
             writes=[ak], dma=f"ao{s}")
        P.op("sp", R.dma_start(out=X[:, :, :N], in_=xin[:, :, t0:t0 + N]),
             writes=[xk], dma="x")
        P.op("pool", R.tensor_scalar(out=X[:, :, :N], in0=X[:, :, :N], scalar1=ALPHA,
                                                    scalar2=None, op0=ALU.mult),
             reads=[xk], writes=[xk])
        for m in range(KD):
            b = m % 2
            py = C.PSB[4 + b]
            for j in range(kin):
                P.op("pe", R.matmul(
                    py[:, :N], lhsT=W[:, j, m * 128:(m + 1) * 128], rhs=A_[:, j, :N], start=(j == 0), stop=(j == kin - 1)),
                     reads=[ak] + wkeys(name, j, m * 128, (m + 1) * 128), writes=[f"ps{4 + b}"])
            P.op("dve", R.tensor_tensor(
                out=X[:, m, :N], in0=py[:, :N], in1=X[:, m, :N], op=ALU.add),
                 reads=[f"ps{4 + b}", xk], writes=[xk])
        postnorm_tile(P, C, L, lnidx, xk, X, ZB, ["zb"], ZQ, ["zq"], N)
        P.op("sp", R.dma_start(out=xout[:, :, t0:t0 + N], in_=X[:, :, :N]),
             reads=[xk], dma="xst")


def inproj_generic(P, C, name, Xin, w_in, nout, tiles, fm_specs, tm_specs):
    W = load_w(P, C, name, w_in, KD, nout)
    XBs = [C.A.alloc([KD, 512], BF16) for _ in range(2)]
    xin = Xin.rearrange("(k p) t -> p k t", p=128)
    bank = [0]
    for ti, (t0, N) in enumerate(tiles):
        s = ti % 2
        XB = XBs[s]
        xbk = ("xb", s)
        P.op("pool", R.dma_start(out=XB[:, :, :N], in_=xin[:, :, t0:t0 + N]),
             writes=[xbk], dma=f"xb{s}")
        for (c0, nc_, cb) in fm_specs:
            b = bank[0] % 6
            bank[0] += 1
            ps = C.PSB[b]
            for k in range(KD):
                P.op("pe", R.matmul(
                    ps[:nc_, :N], lhsT=W[:, k, c0:c0 + nc_], rhs=XB[:, k, :N], start=(k == 0), stop=(k == KD - 1)),
                     reads=[xbk] + wkeys(name, k, c0, c0 + nc_), writes=[f"ps{b}"])
            cb(ps[:nc_, :N], f"ps{b}", t0, N)
        for tb in range((N + 127) // 128):
            nt = min(128, N - tb * 128)
            for (c0, nc_, cb) in tm_specs:
                b = bank[0] % 6
                bank[0] += 1
                ps = C.PSB[b]
                for k in range(KD):
                    P.op("pe", R.matmul(
                        ps[:nt, :nc_], lhsT=XB[:, k, tb * 128:tb * 128 + nt], rhs=W[:, k, c0:c0 + nc_],
                        start=(k == 0), stop=(k == KD - 1)),
                         reads=[xbk] + wkeys(name, k, c0, c0 + nc_), writes=[f"ps{b}"])
                cb(ps[:nt, :nc_], f"ps{b}", t0 + tb * 128, nt)


class Stager:
    def __init__(self, P, C, name, shape, dt, n=3):
        self.P, self.name, self.n, self.i = P, name, n, 0
        self.bufs = [C.A.alloc(shape, dt) for _ in range(n)]

    def put(self, ps_ap, pskey, dst_ap, rows, cols, eng="act", func=None):
        P = self.P
        s = self.i % self.n
        self.i += 1
        B = self.bufs[s]
        k = (self.name, s)
        if eng == "act":
            P.op("act", R.activation(out=B[:rows, :cols], in_=ps_ap, func=(func or AF.Copy)), reads=[pskey], writes=[k])
        else:
            P.op("dve", R.tensor_copy(out=B[:rows, :cols], in_=ps_ap), reads=[pskey], writes=[k])
        P.op("sp", R.dma_start(out=dst_ap, in_=B[:rows, :cols]), reads=[k], dma=f"{self.name}{s}")
        return B, k


def mixer_b(P, C, Xin, Xout, lnidx):
    nc = P.nc
    S, NP_, TPc, TTc = C.SEQ, C.NPS, C.TP, C.TT
    keep = min(512, S)
    tiles = C.tiles
    P.barrier()
    C.A.reset()
    QT, KT, V = C.sB_QT, C.sB_KT, C.sB_V
    st_fm = Stager(P, C, "stfm", [512], BF16, 4)
    st_tm = Stager(P, C, "sttm", [512], BF16, 4)
    st_o = Stager(P, C, "sto", [512], F32, 4)

    def kcol(t0):
        return t0 + 512 if t0 >= TPc else t0

    def out_rows(t0, nt):
        if t0 >= TPc:
            return C.d_bks[480:480 + nt, :], C.d_bvs[480:480 + nt, :]
        sq, tl = divmod(t0, S)
        if tl >= S - keep:
            r = tl - (S - keep)
            return C.d_bkp[sq, r:r + nt, :], C.d_bvp[sq, r:r + nt, :]
        return None

    fm = []
    for c in range(8):
        fm.append((c * 128, 128, lambda ps, pk, t0, N, c=c: st_fm.put(ps, pk, QT[c * 128:(c + 1) * 128, t0:t0 + N], 128, N)))
    for c in range(8):
        fm.append((1024 + c * 128, 128, lambda ps, pk, t0, N, c=c: st_fm.put(
            ps, pk, KT[c * 128:(c + 1) * 128, kcol(t0):kcol(t0) + N], 128, N, eng="dve")))
    tm = []
    for hf in range(2):
        def cbv(ps, pk, t0, nt, hf=hf):
            st_tm.put(ps, pk, V[kcol(t0):kcol(t0) + nt, hf * 512:(hf + 1) * 512], nt, 512, eng="dve")
            o = out_rows(t0, nt)
            if o is not None:
                st_o.put(ps, pk, o[1][:, hf * 512:(hf + 1) * 512], nt, 512)
        tm.append((2048 + hf * 512, 512, cbv))

        def cbk(ps, pk, t0, nt, hf=hf):
            o = out_rows(t0, nt)
            if o is not None:
                st_o.put(ps, pk, o[0][:, hf * 512:(hf + 1) * 512], nt, 512)
        tm.append((1024 + hf * 512, 512, cbk))
    inproj_generic(P, C, "bwin", Xin, C.d_b_w_in, 3072, tiles, fm, tm)
    P.op("pool", R.dma_start(out=KT[:, TPc:TPc + 512], in_=C.d_cbkT), dma="cbk")
    P.op("pool", R.dma_start(out=V[TPc:TPc + 512, :], in_=C.d_cbk_v), dma="cbv")
    P.op("sp", R.dma_start(out=C.d_bks[0:480, :], in_=C.d_cbk_k[32:512, :]), dma="cbk2")
    P.op("sp", R.dma_start(out=C.d_bvs[0:480, :], in_=C.d_cbk_v[32:512, :]), dma="cbv2")

    P.barrier()
    C.A.reset()
    A = C.A
    BT = A.alloc([16, 2, 128], F32)
    FAR = A.alloc([16], F32)
    ONE_E = A.alloc([128], BF16)
    ONE_O = A.alloc([128], BF16)
    NR = 6
    P.op("sp", R.dma_start(out=BT[:], in_=C.d_b_bias), writes=["bt"], dma="bt")
    P.op("sp", R.dma_start(out=FAR[:], in_=C.d_b_far), writes=["far"], dma="far")
    P.op("pool", R.memset(BT[64:128, :, 0, 0:64], NEG), reads=["bt"], writes=["bt"])
    P.op("pool", R.memset(ONE_E[:, 0:64], 1.0), writes=["one_e"])
    P.op("pool", R.memset(ONE_E[:, 64:128], 0.0), writes=["one_e"])
    P.op("pool", R.memset(ONE_O[:, 0:64], 0.0), writes=["one_o"])
    P.op("pool", R.memset(ONE_O[:, 64:128], 1.0), writes=["one_o"])
    ao_d = C.sAO[0:1024, :].rearrange("(k p) t -> p k t", p=128)
    qt_d = QT.rearrange("(k p) t -> p k t", p=128)
    kt_d = KT.rearrange("(k p) t -> p k t", p=128)
    SHARED = ("bt", "far", "one_e", "one_o")

    def b2_stream(si, seqs):
        KB = [A.alloc([8, 128], BF16) for _ in range(NR)]
        VB = [A.alloc([16, 128], BF16) for _ in range(NR)]
        QE = [A.alloc([8, 128], BF16) for _ in range(2)]
        QO = [A.alloc([8, 128], BF16) for _ in range(2)]
        AOt = [A.alloc([8, 128], BF16) for _ in range(2)]
        EN = [A.alloc([2, 128], BF16) for _ in range(2)]
        EF = [A.alloc([3, 128], BF16) for _ in range(2)]
        LG = [A.alloc([2, 128], F32) for _ in range(2)]
        RD = [A.alloc([128], F32) for _ in range(2)]
        Z = 4 * si

        def kx(k):
            if isinstance(k, str) and (k.startswith("ps") or k in SHARED):
                return k
            return (si, k)

        def op(eng, fn, reads=(), writes=(), dma=None):
            return P.op(eng, fn, reads=[kx(k) for k in reads], writes=[kx(k) for k in writes],
                        dma=(None if dma is None else f"{dma}_{si}"))
        for r in range(NR):
            yield op("pool", R.memset(VB[r][:], 0.0), writes=[("vb", r)])
        for r in range(2):
            yield op("pool", R.memset(QE[r][:], 0.0), writes=[("qe", r)])
            yield op("pool", R.memset(QO[r][:], 0.0), writes=[("qo", r)])
            yield op("pool", R.memset(EF[r][:], 0.0), writes=[("ef", r)])
        bank = 0
        qi_ = 0
        for (q0, k0, nqb, qw, ib0) in seqs:
            loaded = {}

            def load_kv(j, nk):
                r = j % NR
                yield op("sp", R.dma_start(out=KB[r][:, :, :nk], in_=kt_d[:, :, k0 + j * 128:k0 + j * 128 + nk]),
                     writes=[("kb", r)], dma=f"kb{r}")
                for par in range(2):
                    src = V[k0 + j * 128:k0 + j * 128 + nk, :].rearrange("s (h p d) -> s h p d", p=2, d=64)[:, :, par, :]
                    yield op("sp", R.dma_start(out=VB[r].rearrange("p (c q) d -> p c q d", q=2)[:nk, :, par, par * 64:(par + 1) * 64], in_=src),
                         writes=[("vb", r)], dma=f"vb{r}")
            for ib in range(nqb):
                i = ib + ib0
                s2 = qi_ % 2
                qi_ += 1
                tq = q0 + ib * 128 if qw == 128 else q0 + 512
                qcol = q0 + ib * 128
                yield op("sp", R.dma_start(out=QE[s2][0:64, :, :qw], in_=qt_d[0:64, :, qcol:qcol + qw]),
                     writes=[("qe", s2)], dma=f"qe{s2}")
                yield op("sp", R.dma_start(out=QO[s2][64:128, :, :qw], in_=qt_d[64:128, :, qcol:qcol + qw]),
                     writes=[("qo", s2)], dma=f"qo{s2}")
                jlist = list(range(max(0, i - 4), i + 1))
                for j in jlist:
                    if j not in loaded:
                        nk = TS if (qw != 128 and j == 4) else 128
                        yield from load_kv(j, nk)
                        loaded[j] = nk
                for c in range(8):
                    po = C.PSB[Z + 2]
                    pd = C.PSB[Z + 3]
                    pok, pdk = f"ps{Z + 2}", f"ps{Z + 3}"
                    first = True
                    for par in range(2):
                        h = 2 * c + par
                        Qh = (QE if par == 0 else QO)[s2]
                        qk = ("qe" if par == 0 else "qo", s2)
                        near = [j for j in jlist if i - j <= 1]
                        far = [j for j in jlist if i - j >= 2]
                        pn, pf = C.PSB[Z], C.PSB[Z + 1]
                        pnk, pfk = f"ps{Z}", f"ps{Z + 1}"
                        pnv = pn[:, 0:256].rearrange("p (d t) -> p d t", d=2)
                        pfv = pf[:, 0:384].rearrange("p (d t) -> p d t", d=3)
                        for j in jlist:
                            d = i - j
                            nk = loaded[j]
                            dst = pnv[:nk, d, :qw] if d <= 1 else pfv[:nk, d - 2, :qw]
                            yield op("pe", R.matmul(
                                dst, lhsT=KB[j % NR][:, c, :nk], rhs=Qh[:, c, :qw], start=True, stop=True),
                                 reads=[("kb", j % NR), qk], writes=[pnk if d <= 1 else pfk])
                        ENp, LGp, EFp = EN[par], LG[par], EF[par]
                        for j in near:
                            d = i - j
                            nk = loaded[j]
                            yield op("dve", R.scalar_tensor_tensor(
                                out=LGp[:nk, d, :qw], in0=pnv[:nk, d, :qw], scalar=SC_B,
                                in1=BT[:nk, h, d, :qw], op0=ALU.mult, op1=ALU.add),
                                 reads=[pnk, "bt"], writes=[("lg", par)])
                            yield op("act", R.activation(
                                out=ENp[:nk, d, :qw], in_=LGp[:nk, d, :qw], func=AF.Exp),
                                 reads=[("lg", par)], writes=[("en", par)])
                        if far:
                            dlo, dhi = min(i - j for j in far), max(i - j for j in far)
                            if dhi == 4 and qw == 128:
                                yield op("act", R.activation(
                                    out=EFp[:, 0:2, :], in_=pfv[:, 0:2, :], func=AF.Exp,
                                    bias=FAR[:, h:h + 1], scale=SC_B), reads=[pfk, "far"], writes=[("ef", par)])
                                yield op("act", R.activation(
                                    out=EFp[:, 2, 0:64], in_=pfv[:, 2, 0:64], func=AF.Exp,
                                    bias=FAR[:, h:h + 1], scale=SC_B), reads=[pfk, "far"], writes=[("ef", par)])
                                yield op("act", R.activation(
                                    out=EFp[64:128, 2, 64:128], in_=pfv[64:128, 2, 64:128], func=AF.Exp,
                                    bias=FAR[64:128, h:h + 1], scale=SC_B), reads=[pfk, "far"], writes=[("ef", par)])
                            else:
                                yield op("act", R.activation(
                                    out=EFp[:, dlo - 2:dhi - 1, :qw], in_=pfv[:, dlo - 2:dhi - 1, :qw],
                                    func=AF.Exp, bias=FAR[:, h:h + 1], scale=SC_B), reads=[pfk, "far"], writes=[("ef", par)])
                        ONE = ONE_E if par == 0 else ONE_O
                        onek = "one_e" if par == 0 else "one_o"
                        blocks = [(j, ENp, ("en", par), i - j) for j in near] + [(j, EFp, ("ef", par), i - j - 2) for j in far]
                        for bi, (j, E_, ek, slot) in enumerate(blocks):
                            nk = loaded[j]
                            last = (par == 1 and bi == len(blocks) - 1)
                            yield op("pe", R.matmul(
                                po[:, :qw], lhsT=VB[j % NR][:nk, h, :], rhs=E_[:nk, slot, :qw], start=first, stop=last),
                                 reads=[("vb", j % NR), ek], writes=[pok])
                            yield op("pe", R.matmul(
                                pd[:, :qw], lhsT=ONE[:nk, :], rhs=E_[:nk, slot, :qw], start=first, stop=last),
                                 reads=[onek, ek], writes=[pdk])
                            first = False
                    r2 = c % 2
                    yield op("dve", R.reciprocal(out=RD[r2][:, :qw], in_=pd[:, :qw]),
                         reads=[pdk], writes=[("rd", r2)])
                    yield op("dve", R.tensor_tensor(
                        out=AOt[s2][:, c, :qw], in0=po[:, :qw], in1=RD[r2][:, :qw], op=ALU.mult),
                         reads=[pok, ("rd", r2)], writes=[("aot", s2)])
                yield op("sp", R.dma_start(out=ao_d[:, :, qcol:qcol + qw], in_=AOt[s2][:, :, :qw]),
                     reads=[("aot", s2)], dma=f"aot{s2}")

    allseq = [(sq * S, sq * S, S // 128, 128, 0) for sq in range(NP_)]
    samp = (TPc, TPc, 1, TS, 4)
    if NP_ >= 2:
        run_streams([b2_stream(0, [allseq[0], samp]), b2_stream(1, allseq[1:])])
    else:
        run_streams([b2_stream(0, allseq), b2_stream(1, [samp])])
    outproj_phase(P, C, "bwout", C.sAO[0:1024, :], 8, C.d_b_w_out, Xin, Xout, lnidx, tiles)


SC_A = 128.0 ** -0.5
C_IDX = (64.0 ** -0.5) * (8.0 ** -0.5)
NBIS = 20


def mixer_a(P, C, Xin, Xout, lnidx):
    S, NP_, TPc, TTc = C.SEQ, C.NPS, C.TP, C.TT
    tiles = C.tiles
    PAST = 1024
    P.barrier()
    C.A.reset()
    QT, KT, V, QIT, KIT, WI = C.sA_QT, C.sA_KT, C.sA_V, C.sA_QIT, C.sA_KIT, C.sA_WI
    st_fm = Stager(P, C, "stfm", [512], BF16, 4)
    st_tm = Stager(P, C, "sttm", [256], BF16, 3)
    st_o = Stager(P, C, "sto", [512], F32, 4)

    def kcol(t0):
        return t0 + PAST if t0 >= TPc else t0

    def orow(t0, nt, dp, ds):
        if t0 >= TPc:
            return ds[0:nt, :]
        sq, tl = divmod(t0, S)
        return dp[sq, tl:tl + nt, :]

    fm = []
    for c in range(8):
        fm.append((c * 128, 128, lambda ps, pk, t0, N, c=c: st_fm.put(ps, pk, QT[c * 128:(c + 1) * 128, t0:t0 + N], 128, N)))
    for c in range(2):
        fm.append((1024 + c * 128, 128, lambda ps, pk, t0, N, c=c: st_fm.put(
            ps, pk, KT[c * 128:(c + 1) * 128, kcol(t0):kcol(t0) + N], 128, N, eng="dve")))
    for c in range(4):
        fm.append((1536 + c * 128, 128, lambda ps, pk, t0, N, c=c: st_fm.put(ps, pk, QIT[c * 128:(c + 1) * 128, t0:t0 + N], 128, N)))
    fm.append((2048, 64, lambda ps, pk, t0, N: st_fm.put(ps, pk, KIT[:, kcol(t0):kcol(t0) + N], 64, N, eng="dve")))

    def cb_kv(ps, pk, t0, nt):
        st_o.put(ps[:, 0:256], pk, orow(t0, nt, C.d_akp, C.d_aks), nt, 256)
        st_o.put(ps[:, 256:512], pk, orow(t0, nt, C.d_avp, C.d_avs), nt, 256, eng="dve")
        st_tm.put(ps[:, 256:512], pk, V[kcol(t0):kcol(t0) + nt, :], nt, 256)

    def cb_iw(ps, pk, t0, nt):
        st_o.put(ps[:, 0:64], pk, orow(t0, nt, C.d_aip, C.d_ais), nt, 64)
        st_o.put(ps[:, 64:72], pk, WI[t0:t0 + nt, :], nt, 8, eng="dve")
    tm = [(1024, 512, cb_kv), (2048, 72, cb_iw)]
    inproj_generic(P, C, "awin", Xin, C.d_a_w_in, 2120, tiles, fm, tm)
    P.op("pool", R.dma_start(out=KT[:, TPc:TPc + PAST], in_=C.d_cakT), dma="cak")
    P.op("pool", R.dma_start(out=V[TPc:TPc + PAST, :], in_=C.d_cav), dma="cav")
    P.op("pool", R.dma_start(out=KIT[:, TPc:TPc + PAST], in_=C.d_cakiT), dma="caki")

    P.barrier()
    C.A.reset()
    A = C.A
    NBmax = max(S, PAST + 128) // 128
    BTA = A.alloc([3, 8, 128], BF16)
    IDN = A.alloc([128], BF16)
    ONES1 = A.alloc([128], BF16)
    off0 = A.off
    BTF = A.alloc([3, 8, 128], F32)
    P.op("sp", R.dma_start(out=BTF[:], in_=C.d_a_bias), writes=["btf"], dma="bt")
    P.op("act", R.activation(out=BTA[:], in_=BTF[:], func=AF.Copy, scale=1.0 / SC_A), reads=["btf"], writes=["bta"])
    P.op("pool", R.dma_start(out=IDN[:], in_=C.d_ident), writes=["idn"], dma="idn")
    P.op("pool", R.memset(ONES1[:], 1.0), writes=["ones1"])
    P.barrier()
    A.off = off0

    def mkset():
        B = Ctx()
        B.KTs = A.alloc([2, NBmax * 128], BF16)
        B.Vs = A.alloc([NBmax, 256], BF16)
        B.KITs = A.alloc([NBmax * 128], BF16)
        B.SC = A.alloc([NBmax * 128], F32)
        B.MK = A.alloc([NBmax * 128], BF16)
        B.MT = A.alloc([NBmax, 128], BF16)
        B.QTb = [A.alloc([8, 128], BF16) for _ in range(2)]
        B.QIb = [A.alloc([8, 128], BF16) for _ in range(2)]
        B.WIb = [A.alloc([8], F32) for _ in range(2)]
        B.RL = [A.alloc([512], F32) for _ in range(2)]
        B.EX = [A.alloc([512], BF16) for _ in range(2)]
        B.AOt = [A.alloc([8, 128], BF16) for _ in range(2)]
        B.RD = [A.alloc([512], F32) for _ in range(2)]
        B.LO, B.MID, B.CNT, B.PRD = (A.alloc([1], F32) for _ in range(4))
        return B
    ao_d = C.sAO[0:1024, :].rearrange("(k p) t -> p k t", p=128)
    qt_d = QT.rearrange("(k p) t -> p k t", p=128)
    qit_d = QIT.rearrange("(h p) t -> p h t", p=64)
    kt_d = KT.rearrange("(g p) t -> p g t", p=128)
    SHARED = ("bta", "idn", "ones1")

    def a2_stream(si, seqs):
        B = mkset()
        KTs, Vs, KITs, SC, MK, MT, QTb, QIb, WIb, RL, EX, AOt, RD = (B.KTs, B.Vs, B.KITs, B.SC, B.MK, B.MT, B.QTb, B.QIb, B.WIb,
                                                                      B.RL, B.EX, B.AOt, B.RD)
        JUNK = MK
        LO, MID, CNT, PRD = B.LO, B.MID, B.CNT, B.PRD
        PL = [C.PSB[4 * si], C.PSB[4 * si + 3]]
        PLK = [f"ps{4 * si}", f"ps{4 * si + 3}"]

        def kx(k):
            if isinstance(k, str) and (k.startswith("ps") or k in SHARED):
                return k
            return (si, k)

        def op(eng, fn, reads=(), writes=(), dma=None):
            return P.op(eng, fn, reads=[kx(k) for k in reads], writes=[kx(k) for k in writes],
                        dma=(None if dma is None else f"{dma}_{si}"))
        qi_ = 0
        rl_i = 0
        ex_i = 0
        for (q0, k0, nqb, qw, ib0, nkeys) in seqs:
            nblk = (nkeys + 127) // 128
            yield op("sp", R.dma_start(out=KTs[:, :, :nkeys], in_=kt_d[:, :, k0:k0 + nkeys]), writes=["kts"], dma="kts")
            yield op("sp", R.dma_start(out=KITs[0:64, :nkeys], in_=KIT[:, k0:k0 + nkeys]), writes=["kits"], dma="kits")
            nfull = nkeys // 128
            yield op("sp", R.dma_start(out=Vs[:, :nfull, :], in_=V[k0:k0 + nfull * 128, :].rearrange("(j s) c -> s j c", s=128)),
                 writes=["vs"], dma="vs")
            if nkeys % 128:
                rem = nkeys % 128
                yield op("sp", R.dma_start(out=Vs[:rem, nfull, :], in_=V[k0 + nfull * 128:k0 + nkeys, :]), writes=["vs"], dma="vs")
            for ib in range(nqb):
                i = ib + ib0
                s2 = qi_ % 2
                qi_ += 1
                qcol = q0 + ib * 128
                nk = min(nkeys, 128 * (i + 1))
                nb = (nk + 127) // 128
                yield op("sp", R.dma_start(out=QTb[s2][:, :, :qw], in_=qt_d[:, :, qcol:qcol + qw]), writes=[("qtb", s2)], dma=f"qtb{s2}")
                yield op("sp", R.dma_start(out=QIb[s2][0:64, :, :qw], in_=qit_d[:, :, qcol:qcol + qw]), writes=[("qib", s2)], dma=f"qib{s2}")
                yield op("sp", R.dma_start(out=WIb[s2][:qw, :], in_=WI[qcol:qcol + qw, :]), writes=[("wib", s2)], dma=f"wib{s2}")
                for st in range((nk + 511) // 512):
                    c0 = st * 512
                    n_ = min(512, nk - c0)
                    for h in range(8):
                        b = (st * 8 + h) % 2
                        ps = PL[b]
                        yield op("pe", R.matmul(ps[:qw, :n_], lhsT=QIb[s2][0:64, h, :qw], rhs=KITs[0:64, c0:c0 + n_], start=True, stop=True),
                             reads=[("qib", s2), "kits"], writes=[PLK[b]])
                        rb = rl_i % 2
                        rl_i += 1
                        yield op("act", R.activation(out=RL[rb][:qw, :n_], in_=ps[:qw, :n_], func=AF.Relu, scale=C_IDX),
                             reads=[PLK[b]], writes=[("rl", rb)])
                        if h == 0:
                            yield op("dve", R.tensor_scalar(out=SC[:qw, c0:c0 + n_], in0=RL[rb][:qw, :n_], scalar1=WIb[s2][:qw, 0:1],
                                                        scalar2=None, op0=ALU.mult),
                                 reads=[("rl", rb), ("wib", s2)], writes=[("sc", st)])
                        else:
                            yield op("dve", R.scalar_tensor_tensor(out=SC[:qw, c0:c0 + n_], in0=RL[rb][:qw, :n_], scalar=WIb[s2][:qw, h:h + 1],
                                                               in1=SC[:qw, c0:c0 + n_], op0=ALU.mult, op1=ALU.add),
                                 reads=[("rl", rb), ("wib", s2), ("sc", st)], writes=[("sc", st)])
                sck = [("sc", st) for st in range((nk + 511) // 512)]
                if qw == 128:
                    yield op("pool", R.memset(SC[0:64, nk - 64:nk], -1.0e30), reads=sck, writes=sck)
                yield op("pool", R.memset(LO[:qw, :], -16.0), writes=["lo"])
                for it in range(NBIS):
                    step = 16.0 / (2 ** it)
                    yield op("dve", R.tensor_scalar(out=MID[:qw, :], in0=LO[:qw, :], scalar1=step, scalar2=None, op0=ALU.add),
                         reads=["lo"], writes=["mid"])
                    yield op("dve", R.tensor_scalar(out=JUNK[:qw, :nk], in0=SC[:qw, :nk], scalar1=MID[:qw, 0:1], scalar2=0.0,
                                                op0=ALU.is_ge, op1=ALU.add, accum_out=CNT[:qw, :]),
                         reads=sck + ["mid"], writes=["junk", "cnt"])
                    yield op("dve", R.tensor_scalar(out=PRD[:qw, :], in0=CNT[:qw, :], scalar1=255.5, scalar2=step,
                                                op0=ALU.is_ge, op1=ALU.mult),
                         reads=["cnt"], writes=["prd"])
                    yield op("dve", R.tensor_tensor(out=LO[:qw, :], in0=LO[:qw, :], in1=PRD[:qw, :], op=ALU.add),
                         reads=["lo", "prd"], writes=["lo"])
                yield op("dve", R.tensor_scalar(out=MK[:qw, :nk], in0=SC[:qw, :nk], scalar1=LO[:qw, 0:1], scalar2=NEG,
                                            op0=ALU.is_lt, op1=ALU.mult),
                     reads=sck + ["lo"], writes=["mk"])
                pT = C.PSALL.bitcast(BF16)[:, (4 * si + 3) * 1024:(4 * si + 3) * 1024 + 1024]
                for j0 in range(0, nb, 4):
                    jn = min(4, nb - j0)
                    for j in range(j0, j0 + jn):
                        w_ = min(128, nk - j * 128)
                        yield op("pe", R.transpose(out=pT[:w_, (j - j0) * 128:(j - j0) * 128 + qw], in_=MK[:qw, j * 128:j * 128 + w_],
                                               identity=IDN[:qw, :qw]),
                             reads=["mk", "idn"], writes=[PLK[1]])
                    if nk - j0 * 128 >= jn * 128:
                        yield op("act", R.activation(out=MT[:, j0:j0 + jn, :qw],
                                                 in_=pT[:, 0:jn * 128].rearrange("p (j t) -> p j t", j=jn)[:, :, :qw], func=AF.Copy),
                             reads=[PLK[1]], writes=[("mt", j0 // 4)])
                    else:
                        for j in range(j0, j0 + jn):
                            w_ = min(128, nk - j * 128)
                            yield op("act", R.activation(out=MT[:w_, j, :qw], in_=pT[:w_, (j - j0) * 128:(j - j0) * 128 + qw], func=AF.Copy),
                                 reads=[PLK[1]], writes=[("mt", j0 // 4)])
                for g in range(2):
                    po, pd = C.PSB[4 * si + 1], C.PSB[4 * si + 2]
                    pok, pdk = f"ps{4 * si + 1}", f"ps{4 * si + 2}"
                    NQ = 4 * qw
                    for j in range(nb):
                        w_ = min(128, nk - j * 128)
                        d = i - j
                        dc = min(d, 2)
                        b = j % 2
                        pl = PL[b]
                        plv = pl[:, :].rearrange("p (h t) -> p h t", h=4)
                        yield op("pe", R.matmul(plv[:w_, :, :qw], lhsT=KTs[:, g, j * 128:j * 128 + w_], rhs=QTb[s2][:, 4 * g:4 * g + 4, :qw],
                                            start=True, stop=False),
                             reads=["kts", ("qtb", s2)], writes=[PLK[b]])
                        yield op("pe", R.matmul(plv[:w_, :, :qw], lhsT=IDN[:w_, :w_], rhs=BTA[:w_, dc, 4 * g:4 * g + 4, :qw],
                                            start=False, stop=False),
                             reads=["idn", "bta"], writes=[PLK[b]])
                        for hh in range(4):
                            yield op("pe", R.matmul(plv[:w_, hh, :qw], lhsT=IDN[:w_, :w_], rhs=MT[:w_, j, :qw], start=False, stop=(hh == 3)),
                                 reads=["idn", ("mt", j // 4)], writes=[PLK[b]])
                        eb = ex_i % 2
                        ex_i += 1
                        EXv = EX[eb][:, :].rearrange("p (h t) -> p h t", h=4)
                        yield op("act", R.activation(out=EXv[:w_, :, :qw], in_=plv[:w_, :, :qw], func=AF.Exp, scale=SC_A),
                             reads=[PLK[b]], writes=[("ex", eb)])
                        pov = po[:, :].rearrange("p (h t) -> p h t", h=4)
                        pdv = pd[:, :].rearrange("p (h t) -> p h t", h=4)
                        yield op("pe", R.matmul(pov[:, :, :qw], lhsT=Vs[:w_, j, g * 128:(g + 1) * 128], rhs=EXv[:w_, :, :qw],
                                            start=(j == 0), stop=(j == nb - 1)),
                             reads=["vs", ("ex", eb)], writes=[pok])
                        yield op("pe", R.matmul(pdv[:, :, :qw], lhsT=ONES1[:w_, :], rhs=EXv[:w_, :, :qw],
                                            start=(j == 0), stop=(j == nb - 1)),
                             reads=["ones1", ("ex", eb)], writes=[pdk])
                    RDv = RD[g][:, :].rearrange("p (h t) -> p h t", h=4)
                    yield op("dve", R.reciprocal(out=RDv[:, :, :qw], in_=pdv[:, :, :qw]), reads=[pdk], writes=[("rd", g)])
                    yield op("dve", R.tensor_tensor(out=AOt[s2][:, 4 * g:4 * g + 4, :qw], in0=pov[:, :, :qw], in1=RDv[:, :, :qw], op=ALU.mult),
                         reads=[pok, ("rd", g)], writes=[("aot", s2)])
                yield op("sp", R.dma_start(out=ao_d[:, :, qcol:qcol + qw], in_=AOt[s2][:, :, :qw]), reads=[("aot", s2)], dma=f"aot{s2}")

    allseq = [(sq * S, sq * S, S // 128, 128, 0, S) for sq in range(NP_)]
    samp = (TPc, TPc, 1, TS, PAST // 128, PAST + TS)
    if NP_ >= 2:
        run_streams([a2_stream(0, [allseq[0], samp]), a2_stream(1, allseq[1:])])
    else:
        run_streams([a2_stream(0, allseq), a2_stream(1, [samp])])
    outproj_phase(P, C, "awout", C.sAO[0:1024, :], 8, C.d_a_w_out, Xin, Xout, lnidx, tiles)


def mixer_c(P, C, Xin, Xout, lnidx):
    S, NP_, TPc, TTc = C.SEQ, C.NPS, C.TP, C.TT
    tiles = C.tiles
    ZT, XC, XTM, DTS = C.sC_ZT, C.sC_XC, C.sC_XTM, C.sC_DTS
    P.barrier()
    C.A.reset()
    A = C.A
    st_z = Stager(P, C, "stz", [512], BF16, 3)
    st_t = Stager(P, C, "stt", [512], BF16, 3)
    HALO = A.alloc([32, 3], F32)
    CW = A.alloc([32, 4], F32)
    CB = A.alloc([32], F32)
    DTB = A.alloc([32], F32)
    ANEG = A.alloc([32], F32)
    IDN = A.alloc([128], BF16)
    XP = [A.alloc([515], F32) for _ in range(3)]
    ACC = [A.alloc([512], F32) for _ in range(2)]
    XS = [A.alloc([4, 512], BF16) for _ in range(2)]
    DTT = [A.alloc([96], F32) for _ in range(2)]
    P.op("sp", R.dma_start(out=CW[:], in_=C.d_c_cw), writes=["cw"], dma="cw")
    P.op("sp", R.dma_start(out=CB[:], in_=C.d_c_cb), writes=["cb"], dma="cb")
    P.op("sp", R.dma_start(out=DTB[:], in_=C.d_c_dtb), writes=["dtb"], dma="dtb")
    P.op("sp", R.dma_start(out=ANEG[:], in_=C.d_c_alog), writes=["aneg"], dma="aneg")
    P.op("act", R.activation(out=ANEG[:], in_=ANEG[:], func=AF.Exp), reads=["aneg"], writes=["aneg"])
    P.op("pool", R.tensor_scalar(out=ANEG[:], in0=ANEG[:], scalar1=-1.0, scalar2=None, op0=ALU.mult), reads=["aneg"], writes=["aneg"])
    P.op("pool", R.dma_start(out=IDN[:], in_=C.d_ident), writes=["idn"], dma="idn")
    st = {"xp": 0, "dt": 0, "tb": 0}
    pend = []
    pTall = C.PSALL.bitcast(BF16)

    def cb_z(c):
        return lambda ps, pk, t0, N: st_z.put(ps, pk, ZT[c * 128:(c + 1) * 128, t0:t0 + N], 128, N, func=AF.Silu)

    def cb_x(cc):
        def f(ps, pk, t0, N):
            if cc == 0:
                if t0 >= TPc:
                    P.op("sp", R.dma_start(out=HALO[:], in_=C.d_c_conv0.rearrange("(c p) j -> p c j", p=128)), writes=["halo"], dma="halo")
                elif t0 % S == 0:
                    P.op("pool", R.memset(HALO[:], 0.0), writes=["halo"])
            r = st["xp"] % 3
            st["xp"] += 1
            X_ = XP[r]
            xk = ("xp", r)
            P.op("act", R.activation(out=X_[:, 3:3 + N], in_=ps, func=AF.Copy), reads=[pk], writes=[xk])
            P.op("pool", R.tensor_copy(out=X_[:, 0:3], in_=HALO[:, cc, :]), reads=["halo"], writes=[xk])
            a = ACC[cc % 2]
            ak = ("acc", cc % 2)
            P.op("dve", R.tensor_scalar(out=a[:, :N], in0=X_[:, 0:N], scalar1=CW[:, cc, 0:1], scalar2=CB[:, cc:cc + 1],
                                        op0=ALU.mult, op1=ALU.add), reads=[xk, "cw", "cb"], writes=[ak])
            for j in range(1, 4):
                P.op("dve", R.scalar_tensor_tensor(out=a[:, :N], in0=X_[:, j:j + N], scalar=CW[:, cc, j:j + 1], in1=a[:, :N],
                                                   op0=ALU.mult, op1=ALU.add), reads=[xk, "cw", ak], writes=[ak])
            P.op("pool", R.tensor_copy(out=HALO[:, cc, :], in_=X_[:, N:N + 3]), reads=[xk], writes=["halo"])
            while pend:
                pend.pop(0)()
            pend.append(lambda: tail(cc, a, ak, t0, N))
            if cc == 31:
                while pend:
                    pend.pop(0)()

        def tail(cc, a, ak, t0, N):
            gi = (cc // 4) % 2
            xsk = ("xs", gi)
            P.op("act", R.activation(out=XS[gi][:, cc % 4, :N], in_=a[:, :N], func=AF.Silu), reads=[ak], writes=[xsk])
            P.op("sp", R.dma_start(out=XC[cc * 128:(cc + 1) * 128, t0:t0 + N], in_=XS[gi][:, cc % 4, :N]), reads=[xsk], dma=f"xc{gi}")
            if cc % 4 == 3 and cc < 24:
                cc0 = cc - 3
                for tb in range((N + 127) // 128):
                    nt = min(128, N - tb * 128)
                    b = 6 + st["tb"] % 2
                    st["tb"] += 1
                    pT = pTall[:, b * 1024:b * 1024 + 512]
                    for q in range(4):
                        P.op("pe", R.transpose(out=pT[:nt, q * 128:(q + 1) * 128], in_=XS[gi][:, q, tb * 128:tb * 128 + nt], identity=IDN[:, :]),
                             reads=[xsk, "idn"], writes=[f"ps{b}"])
                    st_t.put(pT[:nt, :], f"ps{b}", XTM[t0 + tb * 128:t0 + tb * 128 + nt, cc0 * 128:cc0 * 128 + 512], nt, 512, eng="dve")
            if cc == 31:
                if t0 >= TPc:
                    P.op("sp", R.dma_start(out=C.d_cconv_s.rearrange("(c p) j -> p c j", p=128), in_=HALO[:]), reads=["halo"], dma="halo_o")
                elif (t0 + N) % S == 0:
                    P.op("sp", R.dma_start(out=C.d_cconv_p[t0 // S].rearrange("(c p) j -> p c j", p=128), in_=HALO[:]), reads=["halo"], dma="halo_o")
        return f

    def cb_dt(ps, pk, t0, nt):
        r = st["dt"] % 2
        st["dt"] += 1
        T_ = DTT[r]
        k = ("dtt", r)
        P.op("dve", R.tensor_tensor(out=T_[:nt, 0:32], in0=ps, in1=DTB[:nt, :], op=ALU.add), reads=[pk, "dtb"], writes=[k])
        P.op("act", R.activation(out=T_[:nt, 0:32], in_=T_[:nt, 0:32], func=AF.Exp), reads=[k], writes=[k])
        P.op("act", R.activation(out=T_[:nt, 0:32], in_=T_[:nt, 0:32], func=AF.Ln, bias=C.onec[:nt, 0:1]), reads=[k, "onec"], writes=[k])
        P.op("act", R.activation(out=T_[:nt, 32:64], in_=T_[:nt, 0:32], func=AF.Ln), reads=[k], writes=[k])
        P.op("dve", R.tensor_tensor(out=T_[:nt, 64:96], in0=T_[:nt, 0:32], in1=ANEG[:nt, :], op=ALU.mult), reads=[k, "aneg"], writes=[k])
        P.op("sp", R.dma_start(out=DTS[t0:t0 + nt, :], in_=T_[:nt, :]), reads=[k], dma=f"dtt{r}")
    fm = [(c * 128, 128, cb_z(c)) for c in range(16)] + [(2048 + cc * 128, 128, cb_x(cc)) for cc in range(32)]
    tm = [(6144, 32, cb_dt)]
    inproj_generic(P, C, "cwin", Xin, C.d_c_w_in, 6176, tiles, fm, tm)

    if STOP_AFTER_INPROJ:
        return
    P.barrier()
    A.reset()
    UT = A.alloc([64], F32)
    NEG32 = A.alloc([32, 64], F32)
    NEGT = A.alloc([64], F32)
    ONESF = A.alloc([128], F32)
    ONESG = A.alloc([128], BF16)
    DSK = A.alloc([16], F32)
    P.op("sp", R.dma_start(out=UT[0:64, :], in_=C.d_ut), writes=["ut"], dma="ut")
    P.op("sp", R.dma_start(out=NEGT[0:64, :], in_=C.d_negt), writes=["negt"], dma="negt")
    P.op("sp", R.dma_start(out=DSK[:], in_=C.d_c_dsk), writes=["dsk"], dma="dsk")
    P.op("dve", R.tensor_copy(out=NEG32[0:64], in_=NEGT[0:64, :].rearrange("p (o t) -> p o t", o=1).to_broadcast([64, 32, 64])),
         reads=["negt"], writes=["neg32"])
    P.op("pool", R.memset(ONESF[:], 1.0), writes=["onesf"])
    P.op("pool", R.memset(ONESG[:], 1.0 / 256.0), writes=["onesg"])
    xc_d = XC.rearrange("(c p) t -> p c t", p=128)
    zt_d = ZT.rearrange("(c p) t -> p c t", p=128)
    ao_d = C.sAO.rearrange("(c p) t -> p c t", p=128)
    SHARED = ("ut", "neg32", "onesf", "onesg", "dsk", "epsc")

    def c2_stream(si, seqs):
        H = A.alloc([32, 64], F32)
        Hb = A.alloc([32, 64], BF16)
        DTc = [A.alloc([96], F32) for _ in range(2)]
        XHT = [A.alloc([3072], BF16) for _ in range(2)]
        XCF = [A.alloc([32, 64], BF16) for _ in range(2)]
        ZTc = [A.alloc([16, 64], BF16) for _ in range(2)]
        CML = A.alloc([32], F32)
        RR = A.alloc([32, 64], F32)
        ECB = A.alloc([32, 64], BF16)
        CEND = A.alloc([32], F32)
        SEG = A.alloc([32, 64], F32)
        DEC = A.alloc([32, 64], BF16)
        G = A.alloc([32, 64], BF16)
        CE = A.alloc([32, 64], BF16)
        T1 = A.alloc([16, 64], F32)
        YS = A.alloc([16, 64], F32)
        SQ = A.alloc([16, 64], BF16)
        RS = A.alloc([8, 64], F32)
        AOc = [A.alloc([16, 64], BF16) for _ in range(2)]
        W2 = A.alloc([32], F32)
        XW = A.alloc([32, 64], BF16)
        EE = A.alloc([32], F32)
        X0 = 4 * si
        XK = [f"ps{X0 + i}" for i in range(4)]

        def kx(k):
            if isinstance(k, str) and (k.startswith("ps") or k in SHARED):
                return k
            return (si, k)

        def op(eng, fn, reads=(), writes=(), dma=None):
            return P.op(eng, fn, reads=[kx(k) for k in reads], writes=[kx(k) for k in writes],
                        dma=(None if dma is None else f"{dma}_{si}"))
        ci_ = 0
        for (q0, nch, L, h0, hout) in seqs:
            if h0 is None:
                yield op("pool", R.memset(H[:], 0.0), writes=["h"])
            else:
                yield op("sp", R.dma_start(out=H[:].rearrange("p h q -> p (h q)"), in_=h0), writes=["h"], dma="h0")
            yield op("act", R.activation(out=Hb[:], in_=H[:], func=AF.Copy), reads=["h"], writes=["hb"])
            HPM = 512 // L
            for ci in range(nch):
                tc = q0 + ci * L
                s2 = ci_ % 2
                ci_ += 1
                dk, xhk, xck, ztk = ("dtc", s2), ("xht", s2), ("xcf", s2), ("ztc", s2)
                D_, XH_, XC_, ZT_ = DTc[s2], XHT[s2], XCF[s2], ZTc[s2]
                yield op("sp", R.dma_start(out=D_[:L, :], in_=DTS[tc:tc + L, :]), writes=[dk], dma=f"dtc{s2}")
                yield op("sp", R.dma_start(out=XH_[:L, :], in_=XTM[tc:tc + L, :]), writes=[xhk], dma=f"xht{s2}")
                yield op("sp", R.dma_start(out=XC_[:, :, :L], in_=xc_d[:, :, tc:tc + L]), writes=[xck], dma=f"xcf{s2}")
                yield op("sp", R.dma_start(out=ZT_[:, :, :L], in_=zt_d[:, :, tc:tc + L]), writes=[ztk], dma=f"ztc{s2}")
                pc = C.PSB[X0 + 1]
                yield op("pe", R.matmul(pc[:L, 0:32], lhsT=UT[:L, :L], rhs=D_[:L, 64:96], start=True, stop=True), reads=["ut", dk], writes=[XK[1]])
                yield op("dve", R.tensor_tensor(out=CML[:L, :], in0=pc[:L, 0:32], in1=D_[:L, 32:64], op=ALU.subtract), reads=[XK[1], dk], writes=["cml"])
                yield op("dve", R.tensor_tensor(out=RR[:L, :, :L], in0=D_[:L, 64:96].rearrange("p (h o) -> p h o", o=1).to_broadcast([L, 32, L]),
                                                in1=UT[:L, :L].rearrange("p (o t) -> p o t", o=1).to_broadcast([L, 32, L]), op=ALU.mult),
                         reads=[dk, "ut"], writes=["rr"])
                PBC = C.PSALL[:, X0 * 512:X0 * 512 + 16 * L].rearrange("p (h t) -> p h t", h=16)
                ubk = XK[0:(16 * L + 511) // 512]
                for hf in range(2):
                    h0_ = 16 * hf
                    for hb in range(0, 16, HPM):
                        yield op("pe", R.matmul(PBC[:, hb:hb + HPM, :], lhsT=ONESF[:L, :], rhs=RR[:L, h0_ + hb:h0_ + hb + HPM, :L], start=True, stop=True),
                                 reads=["onesf", "rr"], writes=[f"ps{X0 + (hb * L) // 512}"])
                    yield op("act", R.activation(out=ECB[:, h0_:h0_ + 16, :L], in_=PBC, func=AF.Exp), reads=ubk, writes=["ecb"])
                    yield op("act", R.activation(out=CEND[:, h0_:h0_ + 16], in_=PBC[:, :, L - 1], func=AF.Copy), reads=ubk, writes=["cend"])
                    yield op("dve", R.tensor_tensor(out=SEG[:L, h0_:h0_ + 16, :L], in0=PBC[:L],
                                                    in1=CML[:L, h0_:h0_ + 16].rearrange("p (h o) -> p h o", o=1).to_broadcast([L, 16, L]),
                                                    op=ALU.subtract), reads=ubk + ["cml"], writes=["seg"])
                yield op("pool", R.tensor_tensor(out=SEG[:L, :, :L], in0=SEG[:L, :, :L], in1=NEG32[:L, :, :L], op=ALU.add), reads=["seg", "neg32"], writes=["seg"])
                yield op("act", R.activation(out=DEC[:L, :, :L], in_=SEG[:L, :, :L], func=AF.Exp), reads=["seg"], writes=["dec"])
                pcb = C.PSB[X0 + 2][:, 0:8 * L].rearrange("p (g t) -> p g t", g=8)
                for g in range(8):
                    yield op("pe", R.matmul(pcb[:L, g, :], lhsT=XC_[:, 16 + g, :L], rhs=XC_[:, 24 + g, :L], start=True, stop=True),
                             reads=[xck], writes=[XK[2]])
                Gv = G[:, :, :].rearrange("p (g q) t -> p g q t", q=4)
                DECv = DEC[:, :, :].rearrange("p (g q) t -> p g q t", q=4)
                ECBv = ECB[:, :, :].rearrange("p (g q) t -> p g q t", q=4)
                CEv = CE[:, :, :].rearrange("p (g q) t -> p g q t", q=4)
                for hh in range(4):
                    yield op("dve", R.tensor_tensor(out=Gv[:L, :, hh, :L], in0=pcb[:L], in1=DECv[:L, :, hh, :L], op=ALU.mult),
                             reads=[XK[2], "dec"], writes=["g"])
                    yield op("pool", R.tensor_tensor(out=CEv[:, :, hh, :L], in0=ECBv[:, :, hh, :L], in1=XC_[:, 24:32, :L], op=ALU.mult),
                             reads=["ecb", xck], writes=["ce"])
                ybase = (X0 + 2) * 512
                pyv = C.PSALL[:, ybase:ybase + 16 * L].rearrange("p (c t) -> p c t", c=16)
                pyk = XK[2:2 + (16 * L + 511) // 512]
                for h in range(32):
                    c, par = divmod(h, 2)
                    o_ = pyv[par * 64:(par + 1) * 64, c, :]
                    bkk = [f"ps{X0 + 2 + (c * L) // 512}"]
                    yield op("pe", R.matmul(o_, lhsT=XH_[:L, h * 64:(h + 1) * 64], rhs=G[:L, h, :L], start=True, stop=False),
                             reads=[xhk, "g"], writes=bkk)
                    yield op("pe", R.matmul(o_, lhsT=Hb[:, h, :], rhs=CE[:, h, :L], start=False, stop=True), reads=["hb", "ce"], writes=bkk)
                yield op("pool", R.tensor_tensor(out=T1[:, :, :L], in0=XC_[:, 0:16, :L],
                                                 in1=DSK[:, :].rearrange("p (c o) -> p c o", o=1).to_broadcast([128, 16, L]),
                                                 op=ALU.mult), reads=[xck, "dsk"], writes=["t1"])
                yield op("dve", R.tensor_tensor(out=YS[:, :, :L], in0=pyv, in1=T1[:, :, :L], op=ALU.add), reads=pyk + ["t1"], writes=["ys"])
                yield op("dve", R.tensor_tensor(out=YS[:, :, :L], in0=YS[:, :, :L], in1=ZT_[:, :, :L], op=ALU.mult), reads=["ys", ztk], writes=["ys"])
                yield op("act", R.activation(out=SQ[:, :, :L], in_=YS[:, :, :L], func=AF.Square), reads=["ys"], writes=["sq"])
                pgn = C.PSB[X0][:, 0:8 * L].rearrange("p (g t) -> p g t", g=8)
                for g in range(8):
                    for q in range(2):
                        yield op("pe", R.matmul(pgn[:, g, :], lhsT=ONESG[:, :], rhs=SQ[:, 2 * g + q, :L], start=(q == 0), stop=(q == 1)),
                                 reads=["onesg", "sq"], writes=[XK[0]])
                yield op("act", R.activation(out=RS[:, :, :L], in_=pgn, func=AF.Sqrt, bias=C.epsc[:, 0:1]), reads=[XK[0], "epsc"], writes=["rs"])
                yield op("dve", R.reciprocal(out=RS[:, :, :L], in_=RS[:, :, :L]), reads=["rs"], writes=["rs"])
                AOv = AOc[s2][:, :, :].rearrange("p (g q) t -> p g q t", q=2)
                YSv = YS[:, :, :].rearrange("p (g q) t -> p g q t", q=2)
                for q in range(2):
                    yield op("dve", R.tensor_tensor(out=AOv[:, :, q, :L], in0=YSv[:, :, q, :L], in1=RS[:, :, :L], op=ALU.mult),
                             reads=["ys", "rs"], writes=[("aoc", s2)])
                yield op("sp", R.dma_start(out=ao_d[:, :, tc:tc + L], in_=AOc[s2][:, :, :L]), reads=[("aoc", s2)], dma=f"aoc{s2}")
                yield op("dve", R.tensor_tensor(out=W2[:L, :], in0=CEND[:L, :], in1=CML[:L, :], op=ALU.subtract), reads=["cend", "cml"], writes=["w2"])
                yield op("act", R.activation(out=W2[:L, :], in_=W2[:L, :], func=AF.Exp), reads=["w2"], writes=["w2"])
                yield op("dve", R.tensor_tensor(out=XW[:L, :, :], in0=XH_[:L, 0:2048].rearrange("p (h q) -> p h q", q=64),
                                                in1=W2[:L, :].rearrange("p (h o) -> p h o", o=1).to_broadcast([L, 32, 64]), op=ALU.mult),
                         reads=[xhk, "w2"], writes=["xw"])
                yield op("act", R.activation(out=EE[:, :], in_=CEND[:, :], func=AF.Exp), reads=["cend"], writes=["ee"])
                yield op("dve", R.tensor_tensor(out=H[:, :, :], in0=H[:, :, :], in1=EE[:, :].rearrange("p (h o) -> p h o", o=1).to_broadcast([128, 32, 64]),
                                                op=ALU.mult), reads=["h", "ee"], writes=["h"])
                pst = C.PSALL[:, X0 * 512:X0 * 512 + 1024].rearrange("p (h q) -> p h q", q=64)
                for hf in range(2):
                    for gl in range(4):
                        g = 4 * hf + gl
                        yield op("pe", R.matmul(pst[:, 4 * gl:4 * gl + 4, :], lhsT=XH_[:L, 2048 + g * 128:2048 + (g + 1) * 128], rhs=XW[:L, 4 * g:4 * g + 4, :],
                                                start=True, stop=True), reads=[xhk, "xw"], writes=[XK[gl // 2]])
                    yield op("dve", R.tensor_tensor(out=H[:, 16 * hf:16 * hf + 16, :], in0=pst, in1=H[:, 16 * hf:16 * hf + 16, :], op=ALU.add),
                             reads=XK[0:2] + ["h"], writes=["h"])
                yield op("act", R.activation(out=Hb[:], in_=H[:], func=AF.Copy), reads=["h"], writes=["hb"])
            yield op("sp", R.dma_start(out=hout, in_=H[:].rearrange("p h q -> p (h q)")), reads=["h"], dma="hout")

    allseq = [(sq * S, S // 64, 64, None, C.d_cssm_p[sq]) for sq in range(NP_)]
    samp = (TPc, 1, TS, C.d_c_ssm0, C.d_cssm_s)
    if NP_ >= 2:
        run_streams([c2_stream(0, [allseq[0], samp]), c2_stream(1, allseq[1:])])
    else:
        run_streams([c2_stream(0, allseq), c2_stream(1, [samp])])
    outproj_phase(P, C, "cwout", C.sAO, 16, C.d_c_w_out, Xin, Xout, lnidx, tiles, rowscale=C.d_c_ng)


SC_K = 512.0 ** -0.5


def mixer_d(P, C, Xin, Xout, lnidx):
    S, NP_, TPc, TTc = C.SEQ, C.NPS, C.TP, C.TT
    tiles = C.tiles
    QT, KT, KTM, VTM, OS, GS = C.sD_QT, C.sD_KT, C.sD_KTM, C.sD_VTM, C.sD_OS, C.sD_GS
    P.barrier()
    C.A.reset()
    A = C.A
    st_q = Stager(P, C, "stq", [512], BF16, 4)
    st_t = Stager(P, C, "stt", [512], BF16, 4)
    HALO = A.alloc([16, 3], F32)
    CW = A.alloc([16, 4], F32)
    CB = A.alloc([16], F32)
    GB = A.alloc([8], F32)
    WQ = A.alloc([16, 128], BF16)
    WK = A.alloc([16, 128], BF16)
    XP = [A.alloc([515], F32) for _ in range(3)]
    ACC = [A.alloc([512], F32) for _ in range(2)]
    XS = [A.alloc([4, 512], BF16) for _ in range(2)]
    GT = [A.alloc([8], F32) for _ in range(2)]
    P.op("sp", R.dma_start(out=CW[:], in_=C.d_d_cw), writes=["cw"], dma="cw")
    P.op("sp", R.dma_start(out=CB[:], in_=C.d_d_cb), writes=["cb"], dma="cb")
    P.op("sp", R.dma_start(out=GB[:], in_=C.d_d_gb), writes=["gb"], dma="gb")
    P.op("pool", R.dma_start(out=WQ[:], in_=C.d_d_wq), writes=["wq"], dma="wq")
    P.op("pool", R.dma_start(out=WK[:], in_=C.d_d_wk), writes=["wk"], dma="wk")
    st = {"xp": 0, "g": 0, "b": 0}
    pend = []

    def cb_x(cc):
        def f(ps, pk, t0, N):
            if cc == 0:
                if t0 >= TPc:
                    P.op("sp", R.dma_start(out=HALO[:], in_=C.d_d_conv0.rearrange("(c p) j -> p c j", p=128)), writes=["halo"], dma="halo")
                elif t0 % S == 0:
                    P.op("pool", R.memset(HALO[:], 0.0), writes=["halo"])
            r = st["xp"] % 3
            st["xp"] += 1
            X_ = XP[r]
            xk = ("xp", r)
            P.op("act", R.activation(out=X_[:, 3:3 + N], in_=ps, func=AF.Copy), reads=[pk], writes=[xk])
            P.op("pool", R.tensor_copy(out=X_[:, 0:3], in_=HALO[:, cc, :]), reads=["halo"], writes=[xk])
            a = ACC[cc % 2]
            ak = ("acc", cc % 2)
            P.op("dve", R.tensor_scalar(out=a[:, :N], in0=X_[:, 0:N], scalar1=CW[:, cc, 0:1], scalar2=CB[:, cc:cc + 1],
                                        op0=ALU.mult, op1=ALU.add), reads=[xk, "cw", "cb"], writes=[ak])
            for j in range(1, 4):
                P.op("dve", R.scalar_tensor_tensor(out=a[:, :N], in0=X_[:, j:j + N], scalar=CW[:, cc, j:j + 1], in1=a[:, :N],
                                                   op0=ALU.mult, op1=ALU.add), reads=[xk, "cw", ak], writes=[ak])
            P.op("pool", R.tensor_copy(out=HALO[:, cc, :], in_=X_[:, N:N + 3]), reads=[xk], writes=["halo"])
            while pend:
                pend.pop(0)()
            pend.append(lambda: tail(cc, a, ak, t0, N))
            if cc == 15:
                while pend:
                    pend.pop(0)()

        def tail(cc, a, ak, t0, N):
            gi = (cc // 4) % 2
            xsk = ("xs", gi)
            P.op("act", R.activation(out=XS[gi][:, cc % 4, :N], in_=a[:, :N], func=AF.Silu), reads=[ak], writes=[xsk])
            for (Wm, wkey, dst, sc) in ((WQ, "wq", QT, 1.0), (WK, "wk", KT, SC_K)):
                b = 6 + st["b"] % 2
                st["b"] += 1
                pq = C.PSB[b]
                P.op("pe", R.matmul(pq[:, :N], lhsT=Wm[:, cc, :], rhs=XS[gi][:, cc % 4, :N], start=True, stop=True),
                     reads=[wkey, xsk], writes=[f"ps{b}"])
                B_, k_ = st_q.bufs[st_q.i % st_q.n], (st_q.name, st_q.i % st_q.n)
                s_ = st_q.i % st_q.n
                st_q.i += 1
                P.op("act", R.activation(out=B_[:, :N], in_=pq[:, :N], func=AF.Copy, scale=sc), reads=[f"ps{b}"], writes=[k_])
                P.op("sp", R.dma_start(out=dst[cc * 128:(cc + 1) * 128, t0:t0 + N], in_=B_[:, :N]), reads=[k_], dma=f"stq{s_}")
            if cc % 4 == 3:
                cc0 = cc - 3
                for tb in range((N + 127) // 128):
                    nt = min(128, N - tb * 128)
                    b = 6 + st["b"] % 2
                    st["b"] += 1
                    pq = C.PSB[b]
                    for q in range(4):
                        P.op("pe", R.matmul(pq[:nt, q * 128:(q + 1) * 128], lhsT=XS[gi][:, q, tb * 128:tb * 128 + nt], rhs=WK[:, cc0 + q, :],
                                            start=True, stop=True), reads=[xsk, "wk"], writes=[f"ps{b}"])
                    B_, k_ = st_t.bufs[st_t.i % st_t.n], (st_t.name, st_t.i % st_t.n)
                    s_ = st_t.i % st_t.n
                    st_t.i += 1
                    P.op("act", R.activation(out=B_[:nt, :], in_=pq[:nt, :], func=AF.Copy, scale=SC_K), reads=[f"ps{b}"], writes=[k_])
                    P.op("sp", R.dma_start(out=KTM[t0 + tb * 128:t0 + tb * 128 + nt, cc0 * 128:cc0 * 128 + 512], in_=B_[:nt, :]),
                         reads=[k_], dma=f"stt{s_}")
            if cc == 15:
                if t0 >= TPc:
                    P.op("sp", R.dma_start(out=C.d_dconv_s.rearrange("(c p) j -> p c j", p=128), in_=HALO[:]), reads=["halo"], dma="halo_o")
                elif (t0 + N) % S == 0:
                    P.op("sp", R.dma_start(out=C.d_dconv_p[t0 // S].rearrange("(c p) j -> p c j", p=128), in_=HALO[:]), reads=["halo"], dma="halo_o")
        return f

    def cb_v(q):
        return lambda ps, pk, t0, nt: st_t.put(ps, pk, VTM[t0:t0 + nt, q * 512:(q + 1) * 512], nt, 512, eng="dve")

    def cb_o(q):
        return lambda ps, pk, t0, nt: st_t.put(ps, pk, OS[t0:t0 + nt, q * 512:(q + 1) * 512], nt, 512, func=AF.Sigmoid)

    def cb_g(ps, pk, t0, nt):
        r = st["g"] % 2
        st["g"] += 1
        T_ = GT[r]
        k = ("gt", r)
        P.op("dve", R.tensor_tensor(out=T_[:nt, :], in0=ps, in1=GB[:nt, :], op=ALU.add), reads=[pk, "gb"], writes=[k])
        P.op("act", R.activation(out=T_[:nt, 4:8], in_=T_[:nt, 4:8], func=AF.Exp, scale=-1.0), reads=[k], writes=[k])
        P.op("act", R.activation(out=T_[:nt, 4:8], in_=T_[:nt, 4:8], func=AF.Ln, bias=C.onec[:nt, 0:1]), reads=[k, "onec"], writes=[k])
        P.op("pool", R.tensor_scalar(out=T_[:nt, 4:8], in0=T_[:nt, 4:8], scalar1=-1.0, scalar2=None, op0=ALU.mult), reads=[k], writes=[k])
        P.op("sp", R.dma_start(out=GS[t0:t0 + nt, :], in_=T_[:nt, :]), reads=[k], dma=f"gt{r}")
    fm = [(cc * 128, 128, cb_x(cc)) for cc in range(16)]
    tm = [(2048 + q * 512, 512, cb_v(q)) for q in range(4)] + [(4096 + q * 512, 512, cb_o(q)) for q in range(4)] + [(6144, 8, cb_g)]
    inproj_generic(P, C, "dwin", Xin, C.d_d_w_in, 6152, tiles, fm, tm)

    DSTOP = 99
    if STOP_AFTER_INPROJ:
        return
    P.barrier()
    A.reset()
    UT = A.alloc([64], F32)
    IDF = A.alloc([64], F32)
    NEGU = A.alloc([64], F32)
    NEGU4 = A.alloc([4, 64], F32)
    SEL = {64: A.alloc([128], F32), 32: A.alloc([128], F32)}
    ONESF = A.alloc([64], F32)
    ONEB = A.alloc([2], BF16)
    IDN = A.alloc([128], BF16)
    for (t_, d_, nm) in ((UT, C.d_ut, "ut"), (IDF, C.d_identf, "idf"), (NEGU, C.d_negu, "negu")):
        P.op("sp", R.dma_start(out=t_[0:64, :], in_=d_), writes=[nm], dma=nm)
    P.op("sp", R.dma_start(out=SEL[64][0:64, :], in_=C.d_sel64), writes=["sel"], dma="sel")
    P.op("sp", R.dma_start(out=SEL[32][0:32, :], in_=C.d_sel32), writes=["sel"], dma="sel")
    P.op("pool", R.dma_start(out=IDN[:], in_=C.d_ident), writes=["idn"], dma="idn")
    P.op("dve", R.tensor_copy(out=NEGU4[0:64], in_=NEGU[0:64, :].rearrange("p (o t) -> p o t", o=1).to_broadcast([64, 4, 64])),
         reads=["negu"], writes=["negu4"])
    P.op("pool", R.memset(ONESF[:], 1.0), writes=["onesf"])
    P.op("pool", R.memset(ONEB[:], 1.0), writes=["oneb"])
    qt_d = QT.rearrange("(c p) t -> p c t", p=128)
    kt_d = KT.rearrange("(c p) t -> p c t", p=128)
    ao_d = C.sAO.rearrange("(c p) t -> p c t", p=128)
    SHARED = ("ut", "idf", "negu4", "sel", "onesf", "oneb", "idn", "epsc")

    def d2_stream(si, seqs):
        CST = A.alloc([4, 4, 512], F32)
        CBF = A.alloc([4, 4, 512], BF16)
        NS = A.alloc([4, 4], F32)
        NSB = A.alloc([4, 4], BF16)
        M0 = A.alloc([4], F32)
        GSc = [A.alloc([8], F32) for _ in range(1)]
        QTc = [A.alloc([16, 64], BF16) for _ in range(1)]
        KTc = [A.alloc([16, 64], BF16) for _ in range(1)]
        KMc = [A.alloc([2048], BF16) for _ in range(1)]
        VMc = [A.alloc([2048], BF16) for _ in range(1)]
        OSc = [A.alloc([2048], BF16) for _ in range(1)]
        FC = A.alloc([4], F32)
        RV = A.alloc([4], F32)
        RRr = A.alloc([4, 64], F32)
        DL = A.alloc([4, 64], F32)
        RMX = A.alloc([4], F32)
        INTER = A.alloc([4], F32)
        MM = A.alloc([8], F32)
        NEGM = A.alloc([4], F32)
        DEX = A.alloc([4, 64], F32)
        SM = A.alloc([4, 64], F32)
        SMB = A.alloc([4, 64], BF16)
        STT = A.alloc([4, 64], BF16)
        DENI = A.alloc([4], F32)
        WINT = A.alloc([4], F32)
        DEN = A.alloc([4], F32)
        EM = A.alloc([4], F32)
        RDEN = A.alloc([4], F32)
        WR = A.alloc([4], F32)
        TT_ = [A.alloc([512], F32) for _ in range(2)]
        HN = A.alloc([4, 512], F32)
        MEAN4 = A.alloc([4], F32)
        VAR4 = A.alloc([4], F32)
        HJ = A.alloc([512], BF16)
        HNB = A.alloc([2048], BF16)
        AOc = [A.alloc([16, 64], BF16) for _ in range(2)]
        BC8 = A.alloc([8], F32)
        DD = A.alloc([4], F32)
        WE = A.alloc([4], F32)
        DECAY = A.alloc([4], F32)
        KW = A.alloc([4, 512], BF16)
        Y0 = 4 * si
        YK = [f"ps{Y0 + i}" for i in range(4)]

        def kx(k):
            if isinstance(k, str) and (k.startswith("ps") or k in SHARED):
                return k
            return (si, k)

        def op(eng, fn, reads=(), writes=(), dma=None):
            return P.op(eng, fn, reads=[kx(k) for k in reads], writes=[kx(k) for k in writes],
                        dma=(None if dma is None else f"{dma}_{si}"))
        ci_ = 0
        for (q0, nch, L, init, sq) in seqs:
            if init is None:
                yield op("pool", R.memset(CST[:], 0.0), writes=["cst"])
                yield op("pool", R.memset(NS[:], 0.0), writes=["ns"])
                yield op("pool", R.memset(M0[:], 0.0), writes=["m0"])
            else:
                yield op("sp", R.dma_start(out=CST[:], in_=C.d_d_c0.rearrange("h (kc p) v -> p h kc v", p=128)), writes=["cst"], dma="c0")
                yield op("sp", R.dma_start(out=NS[:].rearrange("p h k -> p (h k)"), in_=C.d_d_n0), writes=["ns"], dma="n0")
                yield op("sp", R.dma_start(out=M0[:], in_=C.d_d_m0), writes=["m0"], dma="m0")
            yield op("act", R.activation(out=CBF[:], in_=CST[:], func=AF.Copy), reads=["cst"], writes=["cbf"])
            yield op("act", R.activation(out=NSB[:], in_=NS[:], func=AF.Copy), reads=["ns"], writes=["nsb"])
            for ci in range(nch):
                tc = q0 + ci * L
                s2 = 0
                ci_ += 1
                gk, qk, kk, kmk, vmk, osk = [(n_, s2) for n_ in ("gsc", "qtc", "ktc", "kmc", "vmc", "osc")]
                G_, Q_, K_, KM_, VM_, OS_ = GSc[s2], QTc[s2], KTc[s2], KMc[s2], VMc[s2], OSc[s2]
                yield op("sp", R.dma_start(out=G_[:L, :], in_=GS[tc:tc + L, :]), writes=[gk], dma=f"gsc{s2}")
                yield op("sp", R.dma_start(out=Q_[:, :, :L], in_=qt_d[:, :, tc:tc + L]), writes=[qk], dma=f"qtc{s2}")
                yield op("sp", R.dma_start(out=K_[:, :, :L], in_=kt_d[:, :, tc:tc + L]), writes=[kk], dma=f"ktc{s2}")
                yield op("sp", R.dma_start(out=KM_[:L, :], in_=KTM[tc:tc + L, :]), writes=[kmk], dma=f"kmc{s2}")
                yield op("sp", R.dma_start(out=VM_[:L, :], in_=VTM[tc:tc + L, :]), writes=[vmk], dma=f"vmc{s2}")
                yield op("sp", R.dma_start(out=OS_[:L, :], in_=OS[tc:tc + L, :]), writes=[osk], dma=f"osc{s2}")
                p7 = C.PSB[Y0 + 3][:, 256:512]
                yield op("pe", R.matmul(p7[:L, 0:4], lhsT=UT[:L, :L], rhs=G_[:L, 4:8], start=True, stop=True), reads=["ut", gk], writes=[YK[3]])
                yield op("act", R.activation(out=MM[:L, 4:8], in_=p7[:L, 0:4], func=AF.Copy), reads=[YK[3]], writes=["fc"])
                yield op("dve", R.tensor_tensor(out=RV[:L, :], in0=G_[:L, 0:4], in1=MM[:L, 4:8], op=ALU.subtract), reads=[gk, "fc"], writes=["rv"])
                yield op("dve", R.tensor_tensor(out=RRr[:L, :, :L], in0=RV[:L, :].rearrange("p (h o) -> p h o", o=1).to_broadcast([L, 4, L]),
                                            in1=IDF[:L, :L].rearrange("p (o t) -> p o t", o=1).to_broadcast([L, 4, L]), op=ALU.mult),
                     reads=["rv", "idf"], writes=["rrr"])
                p6 = C.PSB[Y0 + 3][:, 0:4 * L].rearrange("p (h t) -> p h t", h=4)
                yield op("pe", R.matmul(p6[:L], lhsT=ONESF[:L, :L], rhs=RRr[:L, :, :L], start=True, stop=True), reads=["onesf", "rrr"], writes=[YK[3]])
                yield op("dve", R.tensor_tensor(out=DL[:L, :, :L], in0=p6[:L], in1=MM[:L, 4:8].rearrange("p (h o) -> p h o", o=1).to_broadcast([L, 4, L]),
                                            op=ALU.add), reads=[YK[3], "fc"], writes=["dl"])
                yield op("pool", R.tensor_tensor(out=DL[:L, :, :L], in0=DL[:L, :, :L], in1=NEGU4[:L, :, :L], op=ALU.add), reads=["dl", "negu4"], writes=["dl"])
                yield op("dve", R.tensor_reduce(out=RMX[:L, :], in_=DL[:L, :, :L], axis=mybir.AxisListType.X, op=ALU.max), reads=["dl"], writes=["rmx"])
                yield op("dve", R.tensor_tensor(out=INTER[:L, :], in0=MM[:L, 4:8], in1=M0[:L, :], op=ALU.add), reads=["fc", "m0"], writes=["inter"])
                yield op("dve", R.tensor_tensor(out=MM[:L, 0:4], in0=RMX[:L, :], in1=INTER[:L, :], op=ALU.max), reads=["rmx", "inter"], writes=["mm"])
                yield op("pool", R.tensor_scalar(out=NEGM[:L, :], in0=MM[:L, 0:4], scalar1=-1.0, scalar2=None, op0=ALU.mult), reads=["mm"], writes=["negm"])
                for h in range(4):
                    yield op("act", R.activation(out=DEX[:L, h, :L], in_=DL[:L, h, :L], func=AF.Exp, bias=NEGM[:L, h:h + 1]),
                         reads=["dl", "negm"], writes=["dex"])
                if DSTOP <= 2:
                    continue
                p5 = C.PSB[Y0 + 2][:, 0:4 * L].rearrange("p (h t) -> p h t", h=4)
                for h in range(4):
                    for kc in range(4):
                        yield op("pe", R.matmul(p5[:L, h, :], lhsT=Q_[:, 4 * h + kc, :L], rhs=K_[:, 4 * h + kc, :L], start=(kc == 0), stop=(kc == 3)),
                             reads=[qk, kk], writes=[YK[2]])
                yield op("dve", R.tensor_tensor(out=SM[:L, :, :L], in0=p5[:L], in1=DEX[:L, :, :L], op=ALU.mult), reads=[YK[2], "dex"], writes=["sm"])
                yield op("dve", R.tensor_reduce(out=DENI[:L, :], in_=SM[:L, :, :L], axis=mybir.AxisListType.X, op=ALU.add), reads=["sm"], writes=["deni"])
                yield op("act", R.activation(out=SMB[:L, :, :L], in_=SM[:L, :, :L], func=AF.Copy), reads=["sm"], writes=["smb"])
                pT = C.PSALL.bitcast(BF16)[:, (Y0 + 2) * 1024:(Y0 + 2) * 1024 + 1024]
                pTv = pT[:, 0:4 * L].rearrange("p (h t) -> p h t", h=4)
                for h in range(4):
                    yield op("pe", R.transpose(out=pTv[:L, h, :], in_=SMB[:L, h, :L], identity=IDN[:L, :L]), reads=["smb", "idn"], writes=[YK[2]])
                yield op("act", R.activation(out=STT[:L, :, :L], in_=pTv[:L], func=AF.Copy), reads=[YK[2]], writes=["stt"])
                yield op("dve", R.tensor_tensor(out=WINT[:L, :], in0=INTER[:L, :], in1=MM[:L, 0:4], op=ALU.subtract), reads=["inter", "mm"], writes=["wint"])
                yield op("act", R.activation(out=WINT[:L, :], in_=WINT[:L, :], func=AF.Exp), reads=["wint"], writes=["wint"])
                for h in range(4):
                    for kc in range(4):
                        yield op("pe", R.matmul(p7[:L, 8 + h:9 + h], lhsT=Q_[:, 4 * h + kc, :L], rhs=NSB[:, h, kc:kc + 1], start=(kc == 0), stop=(kc == 3)),
                             reads=[qk, "nsb"], writes=[YK[3]])
                yield op("dve", R.tensor_tensor(out=DEN[:L, :], in0=p7[:L, 8:12], in1=WINT[:L, :], op=ALU.mult), reads=[YK[3], "wint"], writes=["den"])
                yield op("dve", R.tensor_tensor(out=DEN[:L, :], in0=DEN[:L, :], in1=DENI[:L, :], op=ALU.add), reads=["den", "deni"], writes=["den"])
                yield op("pool", R.tensor_scalar(out=WR[:L, :], in0=DEN[:L, :], scalar1=-1.0, scalar2=None, op0=ALU.mult), reads=["den"], writes=["wr"])
                yield op("dve", R.tensor_tensor(out=DEN[:L, :], in0=DEN[:L, :], in1=WR[:L, :], op=ALU.max), reads=["den", "wr"], writes=["den"])
                yield op("act", R.activation(out=EM[:L, :], in_=NEGM[:L, :], func=AF.Exp), reads=["negm"], writes=["em"])
                yield op("dve", R.tensor_tensor(out=DEN[:L, :], in0=DEN[:L, :], in1=EM[:L, :], op=ALU.max), reads=["den", "em"], writes=["den"])
                yield op("dve", R.reciprocal(out=RDEN[:L, :], in_=DEN[:L, :]), reads=["den"], writes=["rden"])
                yield op("dve", R.tensor_tensor(out=WR[:L, :], in0=WINT[:L, :], in1=RDEN[:L, :], op=ALU.mult), reads=["wint", "rden"], writes=["wr"])
                if DSTOP <= 3:
                    continue
                for hp in range(2):
                    for hl in range(2):
                        h = 2 * hp + hl
                        yield op("pe", R.matmul(C.PSB[Y0][:L, :], lhsT=STT[:L, h, :L], rhs=VM_[:L, h * 512:(h + 1) * 512], start=True, stop=True),
                             reads=["stt", vmk], writes=[YK[0]])
                        for kc in range(4):
                            yield op("pe", R.matmul(C.PSB[Y0 + 1][:L, :], lhsT=Q_[:, 4 * h + kc, :L], rhs=CBF[:, h, kc, :], start=(kc == 0), stop=(kc == 3)),
                                 reads=[qk, "cbf"], writes=[YK[1]])
                        T_ = TT_[hl]
                        yield op("act", R.activation(out=T_[:L, :], in_=C.PSB[Y0 + 1][:L, :], func=AF.Copy, scale=WR[:L, h:h + 1]),
                             reads=[YK[1], "wr"], writes=[("tt", hl)])
                        yield op("dve", R.scalar_tensor_tensor(out=HN[:L, h, :], in0=C.PSB[Y0][:L, :], scalar=RDEN[:L, h:h + 1], in1=T_[:L, :],
                                                           op0=ALU.mult, op1=ALU.add), reads=[YK[0], "rden", ("tt", hl)], writes=["hn"])
                if DSTOP <= 4:
                    continue
                yield op("dve", R.tensor_tensor(out=HN[:L, :, :], in0=HN[:L, :, :], in1=OS_[:L, :].rearrange("p (h v) -> p h v", h=4), op=ALU.mult),
                     reads=["hn", osk], writes=["hn"])
                yield op("dve", R.tensor_reduce(out=MEAN4[:L, :], in_=HN[:L, :, :], axis=mybir.AxisListType.X, op=ALU.add), reads=["hn"], writes=["mean4"])
                yield op("pool", R.tensor_scalar(out=MEAN4[:L, :], in0=MEAN4[:L, :], scalar1=-1.0 / 512.0, scalar2=None, op0=ALU.mult), reads=["mean4"], writes=["mean4"])
                yield op("dve", R.tensor_tensor(out=HN[:L, :, :], in0=HN[:L, :, :], in1=MEAN4[:L, :].rearrange("p (h o) -> p h o", o=1).to_broadcast([L, 4, 512]),
                                            op=ALU.add), reads=["hn", "mean4"], writes=["hn"])
                for h in range(4):
                    yield op("act", R.activation(out=HJ[:L, :], in_=HN[:L, h, :], func=AF.Square, accum_out=VAR4[:L, h:h + 1]),
                         reads=["hn"], writes=["hj", "var4"])
                yield op("act", R.activation(out=VAR4[:L, :], in_=VAR4[:L, :], func=AF.Sqrt, bias=C.epsc[:L, 0:1], scale=1.0 / 512.0),
                     reads=["var4", "epsc"], writes=["var4"])
                yield op("dve", R.reciprocal(out=VAR4[:L, :], in_=VAR4[:L, :]), reads=["var4"], writes=["var4"])
                yield op("dve", R.tensor_tensor(out=HNB[:L, :].rearrange("p (h v) -> p h v", h=4), in0=HN[:L, :, :],
                                            in1=VAR4[:L, :].rearrange("p (h o) -> p h o", o=1).to_broadcast([L, 4, 512]), op=ALU.mult),
                     reads=["hn", "var4"], writes=["hnb"])
                pA = C.PSALL.bitcast(BF16)[:, (Y0 + 2) * 1024:(Y0 + 2) * 1024 + 16 * L].rearrange("p (c t) -> p c t", c=16)
                for c in range(16):
                    yield op("pe", R.transpose(out=pA[:, c, :], in_=HNB[:L, c * 128:(c + 1) * 128], identity=IDN[:L, :L]), reads=["hnb", "idn"], writes=[YK[2]])
                yield op("act", R.activation(out=AOc[s2][:, :, :L], in_=pA, func=AF.Copy), reads=[YK[2]], writes=[("aoc", s2)])
                yield op("sp", R.dma_start(out=ao_d[:, :, tc:tc + L], in_=AOc[s2][:, :, :L]), reads=[("aoc", s2)], dma=f"aoc{s2}")
                if DSTOP <= 5:
                    continue
                yield op("pe", R.matmul(p7[:, 16:24], lhsT=SEL[L][:L, :], rhs=MM[:L, :], start=True, stop=True), reads=["sel", "mm", "fc"], writes=[YK[3]])
                yield op("act", R.activation(out=BC8[:, :], in_=p7[:, 16:24], func=AF.Copy), reads=[YK[3]], writes=["bc8"])
                yield op("dve", R.tensor_tensor(out=DD[:, :], in0=BC8[:, 4:8], in1=BC8[:, 0:4], op=ALU.subtract), reads=["bc8"], writes=["dd"])
                yield op("dve", R.tensor_tensor(out=WE[:L, :], in0=RV[:L, :], in1=DD[:L, :], op=ALU.add), reads=["rv", "dd"], writes=["we"])
                yield op("act", R.activation(out=WE[:L, :], in_=WE[:L, :], func=AF.Exp), reads=["we"], writes=["we"])
                yield op("dve", R.tensor_tensor(out=DECAY[:, :], in0=DD[:, :], in1=M0[:, :], op=ALU.add), reads=["dd", "m0"], writes=["decay"])
                yield op("act", R.activation(out=DECAY[:, :], in_=DECAY[:, :], func=AF.Exp), reads=["decay"], writes=["decay"])
                yield op("pool", R.tensor_copy(out=M0[:, :], in_=BC8[:, 0:4]), reads=["bc8"], writes=["m0"])
                yield op("dve", R.tensor_tensor(out=KW[:L, :, :], in0=KM_[:L, :].rearrange("p (h k) -> p h k", h=4),
                                            in1=WE[:L, :].rearrange("p (h o) -> p h o", o=1).to_broadcast([L, 4, 512]), op=ALU.mult),
                     reads=[kmk, "we"], writes=["kw"])
                if DSTOP <= 6:
                    continue
                bi = 0
                for h in range(4):
                    for kc in range(4):
                        b = Y0 + bi % 2
                        bi += 1
                        yield op("pe", R.matmul(C.PSB[b][:, :], lhsT=KW[:L, h, kc * 128:(kc + 1) * 128], rhs=VM_[:L, h * 512:(h + 1) * 512],
                                            start=True, stop=True), reads=["kw", vmk], writes=[f"ps{b}"])
                        yield op("dve", R.scalar_tensor_tensor(out=CST[:, h, kc, :], in0=CST[:, h, kc, :], scalar=DECAY[:, h:h + 1], in1=C.PSB[b][:, :],
                                                           op0=ALU.mult, op1=ALU.add), reads=[f"ps{b}", "decay", "cst"], writes=["cst"])
                        if DSTOP <= 7:
                            continue
                        yield op("pe", R.matmul(p7[:, 32 + 2 * (4 * h + kc):34 + 2 * (4 * h + kc)], lhsT=KW[:L, h, kc * 128:(kc + 1) * 128], rhs=ONEB[:L, :],
                                            start=True, stop=True), reads=["kw", "oneb"], writes=[YK[3]])
                yield op("dve", R.tensor_tensor(out=NS[:, :, :], in0=NS[:, :, :], in1=DECAY[:, :].rearrange("p (h o) -> p h o", o=1).to_broadcast([128, 4, 4]),
                                            op=ALU.mult), reads=["ns", "decay"], writes=["ns"])
                yield op("dve", R.tensor_tensor(out=NS[:, :, :], in0=p7[:, 32:64].rearrange("p (h k two) -> p h k two", h=4, two=2)[:, :, :, 0], in1=NS[:, :, :], op=ALU.add),
                     reads=[YK[3], "ns"], writes=["ns"])
                yield op("act", R.activation(out=CBF[:], in_=CST[:], func=AF.Copy), reads=["cst"], writes=["cbf"])
                yield op("act", R.activation(out=NSB[:], in_=NS[:], func=AF.Copy), reads=["ns"], writes=["nsb"])
            dc, dn, dm = (C.d_dc_s, C.d_dn_s, C.d_dm_s) if sq is None else (C.d_dc_p[sq], C.d_dn_p[sq], C.d_dm_p[sq])
            yield op("sp", R.dma_start(out=dc.rearrange("h (kc p) v -> p h kc v", p=128), in_=CST[:]), reads=["cst"], dma="dco")
            yield op("sp", R.dma_start(out=dn, in_=NS[:].rearrange("p h k -> p (h k)")), reads=["ns"], dma="dno")
            yield op("sp", R.dma_start(out=dm, in_=M0[0:1, :]), reads=["m0"], dma="dmo")

    allseq = [(sq * S, S // 64, 64, None, sq) for sq in range(NP_)]
    samp = (TPc, 1, TS, True, None)
    if NP_ >= 2:
        run_streams([d2_stream(0, [allseq[0], samp]), d2_stream(1, allseq[1:])])
    else:
        run_streams([d2_stream(0, allseq), d2_stream(1, [samp])])
    outproj_phase(P, C, "dwout", C.sAO, 16, C.d_d_w_out, Xin, Xout, lnidx, tiles, rowscale=C.d_d_ng)


def build(cfg):
    nc = bass.Bass("TRN2", target_bir_lowering=False)
    es = ExitStack()
    P = Prog(nc, es)
    C = Ctx()
    C.SEQ = cfg.get("seq", SEQ)
    C.NPS = cfg.get("nps", NPS)
    C.TP = C.SEQ * C.NPS
    C.TT = C.TP + TS
    S, TPc, tt = C.SEQ, C.TP, C.TT
    C.tiles = [(i * 512, 512) for i in range(TPc // 512)] + [(TPc, TS)]
    phases = cfg.get("phases", None)

    def din(name, shape, dt=F32):
        return nc.dram_tensor(name, list(shape), dt, kind="ExternalInput").ap()

    def dout(name, shape, dt=F32):
        return nc.dram_tensor(name, list(shape), dt, kind="ExternalOutput").ap()

    def dscr(name, shape, dt):
        return nc.dram_tensor(name, list(shape), dt).ap()

    C.d_x = din("x_in", [D, tt])
    C.d_lng = din("ln_g", [128, 12, KD])
    C.d_lnb = din("ln_b", [128, 12, KD])
    C.d_wg = [din(f"wg{i}", [D, DFF]) for i in range(8)]
    C.d_wu = [din(f"wu{i}", [D, DFF]) for i in range(8)]
    C.d_wd = [din(f"wd{i}", [DFF, D]) for i in range(8)]
    C.d_y = dout("y_out", [D, tt])
    C.sX = [dscr(f"sx{i}", [D, tt], F32) for i in range(2)]
    C.sAO = dscr("s_ao", [2048, tt], BF16)
    keep = min(512, S)
    C.d_b_w_in = din("b_w_in", [D, 3072])
    C.d_b_w_out = din("b_w_out", [D, D])
    C.d_b_bias = din("b_bias", [128, 16, 2, 128])
    C.d_b_far = din("b_far", [128, 16])
    C.d_cbkT = din("cache_b_kT", [1024, 512])
    C.d_cbk_k = din("cache_b_k", [512, 1024])
    C.d_cbk_v = din("cache_b_v", [512, 1024])
    C.d_bkp = dout("b_k_p", [C.NPS, keep, 1024])
    C.d_bvp = dout("b_v_p", [C.NPS, keep, 1024])
    C.d_bks = dout("b_k_s", [512, 1024])
    C.d_bvs = dout("b_v_s", [512, 1024])
    C.sB_QT = dscr("sb_qt", [1024, tt], BF16)
    C.sB_KT = dscr("sb_kt", [1024, tt + 512], BF16)
    C.sB_V = dscr("sb_v", [tt + 512, 1024], BF16)
    C.d_a_w_in = din("a_w_in", [D, 2120])
    C.d_a_w_out = din("a_w_out", [D, D])
    C.d_a_bias = din("a_bias", [128, 3, 8, 128])
    C.d_ident = din("ident", [128, 128])
    C.d_cakT = din("cache_a_kT", [256, 1024])
    C.d_cav = din("cache_a_v", [1024, 256])
    C.d_cakiT = din("cache_a_kiT", [64, 1024])
    C.d_akp = dout("a_k_p", [C.NPS, S, 256])
    C.d_avp = dout("a_v_p", [C.NPS, S, 256])
    C.d_aip = dout("a_kidx_p", [C.NPS, S, 64])
    C.d_aks = dout("a_k_s", [TS, 256])
    C.d_avs = dout("a_v_s", [TS, 256])
    C.d_ais = dout("a_kidx_s", [TS, 64])
    C.sA_QT = dscr("sa_qt", [1024, tt], BF16)
    C.sA_KT = dscr("sa_kt", [256, tt + 1024], BF16)
    C.sA_V = dscr("sa_v", [tt + 1024, 256], BF16)
    C.sA_QIT = dscr("sa_qit", [512, tt], BF16)
    C.sA_KIT = dscr("sa_kit", [64, tt + 1024], BF16)
    C.sA_WI = dscr("sa_wi", [tt, 8], F32)
    C.d_c_w_in = din("c_w_in", [D, 6176])
    C.d_c_w_out = din("c_w_out", [2048, D])
    C.d_c_cw = din("c_cw", [128, 32, 4])
    C.d_c_cb = din("c_cb", [128, 32])
    C.d_c_dtb = din("c_dtb", [128, 32])
    C.d_c_alog = din("c_alog", [128, 32])
    C.d_c_dsk = din("c_dsk", [128, 16])
    C.d_c_ng = din("c_ng", [128, 16])
    C.d_ut = din("ut", [64, 64])
    C.d_negt = din("negt", [64, 64])
    C.d_c_conv0 = din("c_conv0", [4096, 3])
    C.d_c_ssm0 = din("c_ssm0", [128, 2048])
    C.d_cssm_p = dout("c_ssm_p", [C.NPS, 128, 2048])
    C.d_cssm_s = dout("c_ssm_s", [128, 2048])
    C.d_cconv_p = dout("c_conv_p", [C.NPS, 4096, 3])
    C.d_cconv_s = dout("c_conv_s", [4096, 3])
    C.sC_ZT = dscr("sc_zt", [2048, tt], BF16)
    C.sC_XC = dscr("sc_xc", [4096, tt], BF16)
    C.sC_XTM = dscr("sc_xtm", [tt, 3072], BF16)
    C.sC_DTS = dscr("sc_dts", [tt, 96], F32)
    C.d_d_w_in = din("d_w_in", [D, 6152])
    C.d_d_w_out = din("d_w_out", [2048, D])
    C.d_d_cw = din("d_cw", [128, 16, 4])
    C.d_d_cb = din("d_cb", [128, 16])
    C.d_d_gb = din("d_gb", [128, 8])
    C.d_d_wq = din("d_wq", [128, 16, 128])
    C.d_d_wk = din("d_wk", [128, 16, 128])
    C.d_d_ng = din("d_ng", [128, 16])
    C.d_identf = din("identf", [64, 64])
    C.d_negu = din("negu", [64, 64])
    C.d_sel64 = din("sel64", [64, 128])
    C.d_sel32 = din("sel32", [32, 128])
    C.d_d_conv0 = din("d_conv0", [2048, 3])
    C.d_d_c0 = din("d_c0", [4, 512, 512])
    C.d_d_n0 = din("d_n0", [128, 16])
    C.d_d_m0 = din("d_m0", [128, 4])
    C.d_dc_p = dout("d_c_p", [C.NPS, 4, 512, 512])
    C.d_dn_p = dout("d_n_p", [C.NPS, 128, 16])
    C.d_dm_p = dout("d_m_p", [C.NPS, 1, 4])
    C.d_dconv_p = dout("d_conv_p", [C.NPS, 2048, 3])
    C.d_dc_s = dout("d_c_s", [4, 512, 512])
    C.d_dn_s = dout("d_n_s", [128, 16])
    C.d_dm_s = dout("d_m_s", [1, 4])
    C.d_dconv_s = dout("d_conv_s", [2048, 3])
    C.sD_QT = dscr("sd_qt", [2048, tt], BF16)
    C.sD_KT = dscr("sd_kt", [2048, tt], BF16)
    C.sD_KTM = dscr("sd_ktm", [tt, 2048], BF16)
    C.sD_VTM = dscr("sd_vtm", [tt, 2048], BF16)
    C.sD_OS = dscr("sd_os", [tt, 2048], BF16)
    C.sD_GS = dscr("sd_gs", [tt, 8], F32)
    with es:
        setup_common(P, C)
        if phases is None:
            phases = []
            for l in range(4):
                phases += [("ffn", 2 * l), ("mix", l), ("ffn", 2 * l + 1)]
        cur = C.d_x
        for pi, ph in enumerate(phases):
            nxt = C.d_y if pi == len(phases) - 1 else C.sX[pi % 2]
            if ph[0] == "ffn":
                i = ph[1]
                l, which = divmod(i, 2)
                ffn_phase(P, C, i, cur, nxt, C.d_wg[i], C.d_wu[i], C.d_wd[i], 3 * l + (0 if which == 0 else 2), C.tiles)
            elif ph[0] == "mix":
                l = ph[1]
                [mixer_a, mixer_b, mixer_c, mixer_d][l](P, C, cur, nxt, 3 * l + 1)
            cur = nxt
        P.emit()
    return nc, P


def _fm(a):
    return np.ascontiguousarray(a.T)


def prep_core_inputs(inp, c, S=SEQ, nps=NPS):
    f = np.float32
    m = {}
    xp = inp["x_prompt"][c * nps:(c + 1) * nps].reshape(nps * S, D)
    xs = inp["x_sample"][c]
    m["x_in"] = np.ascontiguousarray(np.concatenate([xp, xs], 0).T)
    m["ln_g"] = np.ascontiguousarray(inp["ln_g"].reshape(12, KD, 128).transpose(2, 0, 1))
    m["ln_b"] = np.ascontiguousarray(inp["ln_b"].reshape(12, KD, 128).transpose(2, 0, 1))
    for l in range(4):
        for w, nm in ((0, "ffn1"), (1, "ffn2")):
            m[f"wg{2 * l + w}"] = inp[f"{nm}_wg"][l]
            m[f"wu{2 * l + w}"] = inp[f"{nm}_wu"][l]
            m[f"wd{2 * l + w}"] = inp[f"{nm}_wd"][l]
    m["b_w_in"] = inp["b_w_in"]
    m["b_w_out"] = inp["b_w_out"]
    tab = inp["b_rel_table"]
    ss = np.arange(128)[:, None]
    tt_ = np.arange(128)[None, :]
    bt = np.empty((128, 16, 2, 128), f)
    for d in range(2):
        idx = np.minimum(128 * (1 + d) + tt_ - ss, 256)
        bt[:, :, d, :] = tab[:, idx].transpose(1, 0, 2)
    m["b_bias"] = bt
    m["b_far"] = np.ascontiguousarray(np.broadcast_to(tab[:, 256][None, :], (128, 16)))
    ck = inp["cache_b_k"][c].reshape(512, 1024)
    m["cache_b_kT"] = _fm(ck)
    m["cache_b_k"] = np.ascontiguousarray(ck)
    m["cache_b_v"] = np.ascontiguousarray(inp["cache_b_v"][c].reshape(512, 1024))
    m["a_w_in"] = inp["a_w_in"]
    m["a_w_out"] = inp["a_w_out"]
    t5 = inp["t5_table"]
    ab = np.empty((128, 3, 8, 128), f)
    for d in range(2):
        rel = ss - tt_ - 128 * d
        ab[:, d, :, :] = t5[_t5_bucket(rel)].transpose(0, 2, 1)
    ab[:, 2, :, :] = t5[15][None, :, None]
    m["a_bias"] = ab
    m["ident"] = np.eye(128, dtype=f)
    m["d_w_in"] = inp["d_w_in"]
    m["d_w_out"] = inp["d_w_out"]
    m["d_cw"] = np.ascontiguousarray(inp["d_conv_w"].reshape(4, 16, 128).transpose(2, 1, 0))
    m["d_cb"] = np.ascontiguousarray(inp["d_conv_b"].reshape(16, 128).T)
    m["d_gb"] = np.ascontiguousarray(np.broadcast_to(inp["d_gate_b"][None, :], (128, 8)))
    for nm, src in (("d_wq", inp["d_wq_blk"]), ("d_wk", inp["d_wk_blk"])):
        bd = np.zeros((16, 32, 4, 32, 4), f)
        blk = src.reshape(16, 32, 4, 4)
        for b_ in range(32):
            bd[:, b_, :, b_, :] = blk[:, b_]
        m[nm] = np.ascontiguousarray(bd.reshape(16, 128, 128).transpose(1, 0, 2))
    m["d_ng"] = np.ascontiguousarray(inp["d_norm_g"].reshape(16, 128).T)
    m["identf"] = np.eye(64, dtype=f)
    m["negu"] = np.triu(np.full((64, 64), NEG, f), 1)
    s64 = np.zeros((64, 128), f); s64[63] = 1.0
    s32 = np.zeros((32, 128), f); s32[31] = 1.0
    m["sel64"], m["sel32"] = s64, s32
    m["d_conv0"] = _fm(inp["state_d_conv"][c])
    m["d_c0"] = np.ascontiguousarray(inp["state_d_c"][c])
    m["d_n0"] = np.ascontiguousarray(inp["state_d_n"][c].reshape(16, 128).T)
    m["d_m0"] = np.ascontiguousarray(np.broadcast_to(inp["state_d_m"][c][None, :], (128, 4)))
    m["c_w_in"] = inp["c_w_in"]
    m["c_w_out"] = inp["c_w_out"]
    m["c_cw"] = np.ascontiguousarray(inp["c_conv_w"].reshape(4, 32, 128).transpose(2, 1, 0))
    m["c_cb"] = np.ascontiguousarray(inp["c_conv_b"].reshape(32, 128).T)
    m["c_dtb"] = np.ascontiguousarray(np.broadcast_to(inp["c_dt_bias"][None, :], (128, 32)))
    m["c_alog"] = np.ascontiguousarray(np.broadcast_to(inp["c_a_log"][None, :], (128, 32)))
    m["c_dsk"] = np.ascontiguousarray(np.repeat(inp["c_d_skip"].reshape(16, 2), 64, axis=1).T)
    m["c_ng"] = np.ascontiguousarray(inp["c_norm_g"].reshape(16, 128).T)
    m["ut"] = np.triu(np.ones((64, 64), f))
    m["negt"] = np.tril(np.full((64, 64), NEG, f), -1)
    m["c_conv0"] = _fm(inp["state_c_conv"][c])
    m["c_ssm0"] = np.ascontiguousarray(inp["state_c_ssm"][c].reshape(2048, 128).T)
    m["cache_a_kT"] = _fm(inp["cache_a_k"][c].reshape(1024, 256))
    m["cache_a_v"] = np.ascontiguousarray(inp["cache_a_v"][c].reshape(1024, 256))
    m["cache_a_kiT"] = _fm(inp["cache_a_kidx"][c])
    return m


def _t5_bucket(rel):
    half, max_exact = 16, 8
    ret = np.where(rel > 0, half, 0)
    n = np.abs(rel)
    nf = np.maximum(n, 1).astype(np.float32)
    large = max_exact + (np.log(nf / np.float32(max_exact)) / np.float32(np.log(128 / 8)) * np.float32(half - max_exact)).astype(np.int32)
    large = np.minimum(large, half - 1)
    return ret + np.where(n < max_exact, n, large)


_OUT_ORDER = ("y_prompt", "y_sample", "a_k_p", "a_v_p", "a_kidx_p", "b_k_p", "b_v_p", "c_ssm_p", "c_conv_p",
              "d_c_p", "d_n_p", "d_m_p", "d_conv_p", "a_k_s", "a_v_s", "a_kidx_s", "b_k_s", "b_v_s",
              "c_ssm_s", "c_conv_s", "d_c_s", "d_n_s", "d_m_s", "d_conv_s")


def assemble(results, S=SEQ, nps=NPS):
    o = {k: [] for k in _OUT_ORDER}
    keep = min(512, S)
    for r in results:
        y = r["y_out"].T
        o["y_prompt"].append(y[:nps * S].reshape(nps, S, D))
        o["y_sample"].append(y[nps * S:].reshape(1, TS, D))
        o["a_k_p"].append(r["a_k_p"].reshape(nps, S, 2, 128))
        o["a_v_p"].append(r["a_v_p"].reshape(nps, S, 2, 128))
        o["a_kidx_p"].append(r["a_kidx_p"].reshape(nps, S, 64))
        o["b_k_p"].append(r["b_k_p"].reshape(nps, keep, 16, 64))
        o["b_v_p"].append(r["b_v_p"].reshape(nps, keep, 16, 64))
        o["c_ssm_p"].append(r["c_ssm_p"].reshape(nps, 128, 32, 64).transpose(0, 2, 3, 1))
        o["c_conv_p"].append(r["c_conv_p"].transpose(0, 2, 1))
        o["d_c_p"].append(r["d_c_p"])
        o["d_n_p"].append(r["d_n_p"].transpose(0, 2, 1).reshape(nps, 4, 512))
        o["d_m_p"].append(r["d_m_p"].reshape(nps, 4))
        o["d_conv_p"].append(r["d_conv_p"].transpose(0, 2, 1))
        o["a_k_s"].append(r["a_k_s"].reshape(1, TS, 2, 128))
        o["a_v_s"].append(r["a_v_s"].reshape(1, TS, 2, 128))
        o["a_kidx_s"].append(r["a_kidx_s"].reshape(1, TS, 64))
        o["b_k_s"].append(r["b_k_s"].reshape(1, 512, 16, 64))
        o["b_v_s"].append(r["b_v_s"].reshape(1, 512, 16, 64))
        o["c_ssm_s"].append(r["c_ssm_s"].reshape(1, 128, 32, 64).transpose(0, 2, 3, 1))
        o["c_conv_s"].append(r["c_conv_s"].T[None])
        o["d_c_s"].append(r["d_c_s"][None])
        o["d_n_s"].append(r["d_n_s"].T.reshape(1, 4, 512))
        o["d_m_s"].append(r["d_m_s"].reshape(1, 4))
        o["d_conv_s"].append(r["d_conv_s"].T[None])
    return tuple(np.ascontiguousarray(np.concatenate(o[k], 0), dtype=np.float32) for k in _OUT_ORDER)


def kernel(**inputs):
    inp = {k: np.asarray(v) for k, v in inputs.items()}
    nc, _ = build({})
    in_maps = [prep_core_inputs(inp, c) for c in range(NCORES)]
    res = run_bass_kernel_spmd(nc, in_maps, core_ids=list(range(NCORES)))
    return assemble(res.results)
```

```python
import numpy as np
from contextlib import ExitStack
import concourse.bass as bass
import concourse.mybir as mybir
from concourse.bass_utils import run_bass_kernel_spmd

F32 = mybir.dt.float32
BF16 = mybir.dt.bfloat16
ALU = mybir.AluOpType
AF = mybir.ActivationFunctionType

D = 1024
KD = 8
DFF = 2816
KF = 22
SEQ = 4096
NPS = 2
TS = 32
TP = NPS * SEQ
TT = TP + TS
ALPHA = (2.0 * 4) ** 0.25
EPS = 1e-5
NCORES = 8

ENGS = ["pe", "act", "dve", "pool", "sp"]


class _Rec:
    def __getattr__(self, name):
        def f(*a, **k):
            return (name, a, k)
        return f


R = _Rec()


class Tok:
    __slots__ = ("sem", "val", "eng", "idx", "snap", "dma")

    def __init__(self, sem, val, eng, idx, snap, dma):
        self.sem, self.val, self.eng, self.idx, self.snap, self.dma = sem, val, eng, idx, snap, dma


class Prog:
    def __init__(self, nc, es):
        self.nc, self.es = nc, es
        self.streams = {e: [] for e in ENGS}
        self.n = {e: 0 for e in ENGS}
        self.clock = {e: {} for e in ENGS}
        self.lastw = {}
        self.readers = {}
        self.sems = {}
        self.semcnt = {}
        self.last_tok = {e: None for e in ENGS}
        self.dma_last = {}
        self.lmap = {}
        self.free_phys = []
        self.all_phys = []
        self.n_wait = 0

    def sem(self, name):
        if name not in self.sems:
            self.sems[name] = self.es.enter_context(self.nc.semaphore(name))
            self.semcnt[name] = 0
        return self.sems[name]

    def sb(self, name, shape, dt):
        return self.es.enter_context(self.nc.sbuf_tensor(name, list(shape), dt))

    def ps(self, name, shape, dt=F32):
        return self.es.enter_context(self.nc.psum_tensor(name, list(shape), dt))

    def _need(self, eng, tk, waits):
        if tk is None:
            return
        if (not tk.dma) and tk.eng == eng:
            if eng == "pe" or tk.idx < self.n[eng] - 3:
                return
        ck = self.clock[eng]
        if ck.get(tk.sem, 0) >= tk.val:
            return
        waits[tk.sem] = max(waits.get(tk.sem, 0), tk.val)
        for s, v in tk.snap.items():
            if ck.get(s, 0) < v:
                ck[s] = v
        ck[tk.sem] = tk.val

    def op(self, eng, fn, reads=(), writes=(), dma=None, extra=()):
        waits = {}
        psr = [k for k in reads if isinstance(k, str) and k.startswith("ps")]
        if psr:
            reads = [k for k in reads if k not in psr]
            writes = list(writes) + psr
        for k in reads:
            self._need(eng, self.lastw.get(k), waits)
        for k in writes:
            self._need(eng, self.lastw.get(k), waits)
            for tk in self.readers.get(k, {}).values():
                self._need(eng, tk, waits)
        for tk in extra:
            self._need(eng, tk, waits)
        idx = self.n[eng]
        if dma is not None:
            lname = "d_" + dma
            if lname not in self.lmap:
                if self.free_phys:
                    self.lmap[lname] = self.free_phys.pop()
                else:
                    self.lmap[lname] = f"dq{len(self.all_phys)}"
                    self.all_phys.append(self.lmap[lname])
            sname = self.lmap[lname]
            self.sem(sname)
            self.semcnt[sname] += 16
            tk = Tok(sname, self.semcnt[sname], eng, idx, dict(self.clock[eng]), True)
            self.dma_last[sname] = tk
            inc = (sname, 16)
        else:
            sname = "e_" + eng
            self.sem(sname)
            self.n[eng] += 1
            self.semcnt[sname] = self.n[eng]
            tk = Tok(sname, self.n[eng], eng, idx, dict(self.clock[eng]), False)
            inc = (sname, 1)
        self.n_wait += len(waits)
        self.streams[eng].append((list(waits.items()), fn, inc))
        for k in writes:
            self.lastw[k] = tk
            self.readers[k] = {}
        for k in reads:
            self.readers.setdefault(k, {})[(eng, sname)] = tk
        if dma is None:
            self.last_tok[eng] = tk
        return tk

    def barrier(self):
        toks = [t for t in self.last_tok.values() if t is not None] + list(self.dma_last.values())
        for e in ENGS:
            self.op(e, R.nop(), extra=toks)
        self.lastw.clear()
        self.readers.clear()
        self.free_phys = list(self.all_phys)
        self.lmap.clear()

    def emit(self):
        nc = self.nc
        finals = [(s, c) for s, c in self.semcnt.items() if c > 0]
        with nc.Block() as block:
            def run(ename, engine):
                for waits, fn, inc in self.streams[ename]:
                    for s, v in waits:
                        engine.wait_ge(self.sems[s], v)
                    try:
                        ins = getattr(engine, fn[0])(*fn[1], **fn[2])
                    except Exception:
                        print("FAILED OP:", ename, fn[0], fn[1], fn[2])
                        raise
                    ins.then_inc(self.sems[inc[0]], inc[1])
                if ename == "sp":
                    for s, c in finals:
                        engine.wait_ge(self.sems[s], c)

            @block.tensor
            def _(e):
                run("pe", e)

            @block.scalar
            def _(e):
                run("act", e)

            @block.vector
            def _(e):
                run("dve", e)

            @block.gpsimd
            def _(e):
                run("pool", e)

            @block.sync
            def _(e):
                run("sp", e)


class Ctx:
    pass


class Arena:
    def __init__(self, t, nelem):
        self.t, self.nelem, self.off = t, nelem, 0

    def reset(self):
        self.off = 0

    def alloc(self, shape, dt):
        n = int(np.prod(shape))
        ne = n * (2 if dt == F32 else 1)
        ne = (ne + 15) // 16 * 16
        assert self.off + ne <= self.nelem, ("arena overflow", self.off, ne, self.nelem)
        v = self.t[:, self.off:self.off + (n * 2 if dt == F32 else n)]
        self.off += ne
        if dt == F32:
            v = v.bitcast(F32)
        if len(shape) == 2:
            return v.rearrange("p (a b) -> p a b", a=shape[0])
        if len(shape) == 3:
            return v.rearrange("p (a b c) -> p a b c", a=shape[0], b=shape[1])
        return v


def setup_common(P, C):
    C.A = Arena(P.sb("arena", [128, ARENA_N], BF16), ARENA_N)
    C.ones = P.sb("ones_bf", [128, 128], BF16)
    C.lng = P.sb("lng", [128, 12, KD], F32)
    C.lnb = P.sb("lnb", [128, 12, KD], F32)
    C.epsc = P.sb("epsc", [128, 1], F32)
    C.PSALL = P.ps("psall", [128, 4096], F32)
    C.PSB = [C.PSALL[:, i * 512:(i + 1) * 512] for i in range(8)]
    C.onec = P.sb("onec", [128, 1], F32)
    P.op("pool", R.memset(C.onec[:], 1.0), writes=["onec"])
    P.op("pool", R.memset(C.ones[:], 1.0 / D), writes=["ones"])
    P.op("pool", R.memset(C.epsc[:], EPS), writes=["epsc"])
    P.op("sp", R.dma_start(out=C.lng[:], in_=C.d_lng),
         writes=["lng"], dma="lng")
    P.op("sp", R.dma_start(out=C.lnb[:], in_=C.d_lnb),
         writes=["lnb"], dma="lnb")


def load_w(P, C, name, dram_ap, kin, nout, issue=True):
    view = C.A.alloc([kin, nout], BF16)
    src = dram_ap.rearrange("(k p) n -> p k n", p=128)
    jobs = []
    for c0 in range(0, nout, WSTEP):
        c1 = min(nout, c0 + WSTEP)
        blk = []
        for k in range(kin):
            blk.append((P, view, src, name, k, c0, c1))
        jobs.append(blk)
    if issue:
        for blk in jobs:
            for j in blk:
                issue_w(*j)
        return view
    return view, jobs


def issue_w(P, view, src, name, k, c0, c1):
    P.op("pool", R.dma_start(out=view[:, k, c0:c1], in_=src[:, k, c0:c1]), writes=[(name, k, c0)], dma=f"w_{name[:2]}_{k % 4}")


WSTEP = 1024


def wkeys(name, k, c0, c1):
    return [(name, k, c) for c in range((c0 // WSTEP) * WSTEP, c1, WSTEP)]


def ln_bufs(P, C):
    L = Ctx()
    L.MEAN = C.A.alloc([512], F32)
    L.MSQ = C.A.alloc([512], F32)
    L.RSTD = C.A.alloc([512], F32)
    L.TMP = [C.A.alloc([512], F32) for _ in range(2)]
    return L


def postnorm_stats(P, C, L, zk, Z, ZB, zbk, ZQ, zqk, N):
    ps1, ps2 = C.PSB[6], C.PSB[7]
    P.op("act", R.activation(out=ZB[:, :, :N], in_=Z[:, :, :N], func=AF.Copy), reads=[zk], writes=zbk)
    P.op("act", R.activation(out=ZQ[:, :, :N], in_=Z[:, :, :N], func=AF.Square), reads=[zk], writes=zqk)
    for m in range(KD):
        P.op("pe", R.matmul(ps1[:, :N], lhsT=C.ones[:], rhs=ZB[:, m, :N], start=(m == 0), stop=(m == KD - 1)),
             reads=["ones"] + zbk, writes=["ps6"])
    for m in range(KD):
        P.op("pe", R.matmul(ps2[:, :N], lhsT=C.ones[:], rhs=ZQ[:, m, :N], start=(m == 0), stop=(m == KD - 1)),
             reads=["ones"] + zqk, writes=["ps7"])


def postnorm_apply(P, C, L, lnidx, zk, Z, N):
    ps1, ps2 = C.PSB[6], C.PSB[7]
    yield P.op("act", R.activation(out=L.MEAN[:, :N], in_=ps1[:, :N], func=AF.Copy), reads=["ps6"], writes=["mean"])
    yield P.op("act", R.activation(out=L.MSQ[:, :N], in_=ps1[:, :N], func=AF.Square), reads=["ps6"], writes=["msq"])
    yield P.op("dve", R.tensor_tensor(out=L.RSTD[:, :N], in0=ps2[:, :N], in1=L.MSQ[:, :N], op=ALU.subtract),
               reads=["ps7", "msq"], writes=["rstd"])
    yield P.op("act", R.activation(out=L.RSTD[:, :N], in_=L.RSTD[:, :N], func=AF.Sqrt, bias=C.epsc[:, 0:1]),
               reads=["rstd", "epsc"], writes=["rstd"])
    yield P.op("dve", R.reciprocal(out=L.RSTD[:, :N], in_=L.RSTD[:, :N]), reads=["rstd"], writes=["rstd"])
    for m in range(KD):
        T = L.TMP[m % 2]
        tk = ("lntmp", m % 2)
        yield P.op("dve", R.tensor_tensor(out=T[:, :N], in0=Z[:, m, :N], in1=L.MEAN[:, :N], op=ALU.subtract),
                   reads=[zk, "mean"], writes=[tk])
        yield P.op("dve", R.tensor_tensor(out=T[:, :N], in0=T[:, :N], in1=L.RSTD[:, :N], op=ALU.mult),
                   reads=[tk, "rstd"], writes=[tk])
        yield P.op("act", R.activation(out=Z[:, m, :N], in_=T[:, :N], func=AF.Identity,
                                       bias=C.lnb[:, lnidx, m:m + 1], scale=C.lng[:, lnidx, m:m + 1]),
                   reads=[tk, "lng", "lnb"], writes=[zk])


def postnorm_tile(P, C, L, lnidx, zk, Z, ZB, zbk, ZQ, zqk, N):
    postnorm_stats(P, C, L, zk, Z, ZB, zbk, ZQ, zqk, N)
    for _ in postnorm_apply(P, C, L, lnidx, zk, Z, N):
        pass


def run_streams(gens):
    gens = list(gens)
    while gens:
        for g in list(gens):
            try:
                next(g)
            except StopIteration:
                gens.remove(g)


def tiles_of(with_sample=True):
    t = [(i * 512, 512) for i in range(TP // 512)]
    if with_sample:
        t.append((TP, TS))
    return t


def ffn_phase(P, C, pid, Xin, Xout, wg, wu, wd, lnidx, tiles):
    P.barrier()
    C.A.reset()
    WG, jg = load_w(P, C, f"wg{pid}", wg, KD, DFF, issue=False)
    WU, ju = load_w(P, C, f"wu{pid}", wu, KD, DFF, issue=False)
    for bg, bu in zip(jg, ju):
        for j in bg + bu:
            issue_w(*j)
    WD = load_w(P, C, f"wd{pid}", wd, KF, D)
    X = C.A.alloc([KD, 512], F32)
    XBs = [C.A.alloc([KD, 512], BF16) for _ in range(2)]
    H = C.A.alloc([KF, 512], BF16)
    SGs = [C.A.alloc([512], BF16) for _ in range(2)]
    L = ln_bufs(P, C)
    xin = Xin.rearrange("(k p) t -> p k t", p=128)
    xout = Xout.rearrange("(k p) t -> p k t", p=128)
    xk = "x"
    for ti, (t0, N) in enumerate(tiles):
        s = ti % 2
        XB = XBs[s]
        xbk = ("xb", s)
        P.op("pool", R.dma_start(out=XB[:, :, :N], in_=xin[:, :, t0:t0 + N]),
             writes=[xbk], dma=f"xb{s}")
        for j in range(KF):
            b = j % 2
            pg, pu = C.PSB[b], C.PSB[2 + b]
            for k in range(KD):
                P.op("pe", R.matmul(
                    pg[:, :N], lhsT=WG[:, k, j * 128:(j + 1) * 128], rhs=XB[:, k, :N], start=(k == 0), stop=(k == KD - 1)),
                     reads=[xbk] + wkeys(f"wg{pid}", k, j * 128, (j + 1) * 128), writes=[f"ps{b}"])
            for k in range(KD):
                P.op("pe", R.matmul(
                    pu[:, :N], lhsT=WU[:, k, j * 128:(j + 1) * 128], rhs=XB[:, k, :N], start=(k == 0), stop=(k == KD - 1)),
                     reads=[xbk] + wkeys(f"wu{pid}", k, j * 128, (j + 1) * 128), writes=[f"ps{2 + b}"])
            SG = SGs[b]
            P.op("act", R.activation(out=SG[:, :N], in_=pg[:, :N], func=AF.Silu),
                 reads=[f"ps{b}"], writes=[("sg", b)])
            P.op("dve", R.tensor_tensor(out=H[:, j, :N], in0=pu[:, :N],
                                                                        in1=SG[:, :N], op=ALU.mult),
                 reads=[f"ps{2 + b}", ("sg", b)], writes=[("h", j)])
        P.op("sp", R.dma_start(out=X[:, :, :N], in_=xin[:, :, t0:t0 + N]),
             writes=[xk], dma="x")
        P.op("pool", R.tensor_scalar(out=X[:, :, :N], in0=X[:, :, :N], scalar1=ALPHA,
                                                    scalar2=None, op0=ALU.mult),
             reads=[xk], writes=[xk])
        for m in range(KD):
            b = m % 2
            py = C.PSB[4 + b]
            for j in range(KF):
                P.op("pe", R.matmul(
                    py[:, :N], lhsT=WD[:, j, m * 128:(m + 1) * 128], rhs=H[:, j, :N], start=(j == 0), stop=(j == KF - 1)),
                     reads=[("h", j)] + wkeys(f"wd{pid}", j, m * 128, (m + 1) * 128), writes=[f"ps{4 + b}"])
            P.op("dve", R.scalar_tensor_tensor(
                out=X[:, m, :N], in0=py[:, :N], scalar=0.5, in1=X[:, m, :N], op0=ALU.mult, op1=ALU.add),
                 reads=[f"ps{4 + b}", xk], writes=[xk])
        postnorm_tile(P, C, L, lnidx, xk, X, XB, [xbk], H[:, 0:KD, :], [("h", j) for j in range(KD)], N)
        P.op("sp", R.dma_start(out=xout[:, :, t0:t0 + N], in_=X[:, :, :N]),
             reads=[xk], dma="xst")


ARENA_N = 102400
STOP_AFTER_INPROJ = False
SC_B = 64.0 ** -0.5
NEG = -30000.0


def outproj_phase(P, C, name, AO, kin, w_out, Xin, Xout, lnidx, tiles, rowscale=None):
    P.barrier()
    C.A.reset()
    W = load_w(P, C, name, w_out, kin, D)
    if rowscale is not None:
        RSC = C.A.alloc([kin], F32)
        P.op("sp", R.dma_start(out=RSC[:], in_=rowscale), writes=["rsc"], dma="rsc")
        for j in range(kin):
            P.op("pool", R.tensor_scalar(out=W[:, j, :], in0=W[:, j, :], scalar1=RSC[:, j:j + 1], scalar2=None, op0=ALU.mult),
                 reads=["rsc"] + wkeys(name, j, 0, D), writes=wkeys(name, j, 0, D))
    Xs = [C.A.alloc([KD, 512], F32) for _ in range(2)]
    AOs = [C.A.alloc([kin, 512], BF16) for _ in range(2)]
    ZB = C.A.alloc([KD, 512], BF16)
    ZQ = C.A.alloc([KD, 512], BF16)
    L = ln_bufs(P, C)
    xin = Xin.rearrange("(k p) t -> p k t", p=128)
    xout = Xout.rearrange("(k p) t -> p k t", p=128)
    ao = AO.rearrange("(k p) t -> p k t", p=128)
    pending = [None]

    def advance(n):
        g = pending[0]
        if g is None:
            return
        for _ in range(n):
            try:
                next(g)
            except StopIteration:
                pending[0] = None
                return
    for ti, (t0, N) in enumerate(tiles):
        s = ti % 2
        A_, X = AOs[s], Xs[s]
        ak, xk = ("ao", s), ("x", s)
        P.op("sp", R.dma_start(out=A_[:, :, :N], in_=ao[:, :, t0:t0 + N]), writes=[ak], dma=f"ao{s}")
        P.op("sp", R.dma_start(out=X[:, :, :N], in_=xin[:, :, t0:t0 + N]), writes=[xk], dma=f"x{s}")
        P.op("pool", R.tensor_scalar(out=X[:, :, :N], in0=X[:, :, :N], scalar1=ALPHA, scalar2=None, op0=ALU.mult),
             reads=[xk], writes=[xk])
        for m in range(KD):
            b = m % 2
            py = C.PSB[4 + b]
            for j in range(kin):
                P.op("pe", R.matmul(py[:, :N], lhsT=W[:, j, m * 128:(m + 1) * 128], rhs=A_[:, j, :N], start=(j == 0), stop=(j == kin - 1)),
                     reads=[ak] + wkeys(name, j, m * 128, (m + 1) * 128), writes=[f"ps{4 + b}"])
            P.op("dve", R.tensor_tensor(out=X[:, m, :N], in0=py[:, :N], in1=X[:, m, :N], op=ALU.add),
                 reads=[f"ps{4 + b}", xk], writes=[xk])
            advance(4)
        advance(10 ** 6)
        postnorm_stats(P, C, L, xk, X, ZB, ["zb"], ZQ, ["zq"], N)

        def tail_gen(t0=t0, N=N, X=X, xk=xk, s=s):
            yield from postnorm_apply(P, C, L, lnidx, xk, X, N)
            yield P.op("sp", R.dma_start(out=xout[:, :, t0:t0 + N], in_=X[:, :, :N]), reads=[xk], dma=f"xst{s}")
        pending[0] = tail_gen()
    advance(10 ** 6)


def inproj_generic(P, C, name, Xin, w_in, nout, tiles, fm_specs, tm_specs):
    W = load_w(P, C, name, w_in, KD, nout)
    XBs = [C.A.alloc([KD, 512], BF16) for _ in range(2)]
    xin = Xin.rearrange("(k p) t -> p k t", p=128)
    bank = [0]
    for ti, (t0, N) in enumerate(tiles):
        s = ti % 2
        XB = XBs[s]
        xbk = ("xb", s)
        P.op("pool", R.dma_start(out=XB[:, :, :N], in_=xin[:, :, t0:t0 + N]),
             writes=[xbk], dma=f"xb{s}")
        for (c0, nc_, cb) in fm_specs:
            b = bank[0] % 6
            bank[0] += 1
            ps = C.PSB[b]
            for k in range(KD):
                P.op("pe", R.matmul(
                    ps[:nc_, :N], lhsT=W[:, k, c0:c0 + nc_], rhs=XB[:, k, :N], start=(k == 0), stop=(k == KD - 1)),
                     reads=[xbk] + wkeys(name, k, c0, c0 + nc_), writes=[f"ps{b}"])
            cb(ps[:nc_, :N], f"ps{b}", t0, N)
        for tb in range((N + 127) // 128):
            nt = min(128, N - tb * 128)
            for (c0, nc_, cb) in tm_specs:
                b = bank[0] % 6
                bank[0] += 1
                ps = C.PSB[b]
                for k in range(KD):
                    P.op("pe", R.matmul(
                        ps[:nt, :nc_], lhsT=XB[:, k, tb * 128:tb * 128 + nt], rhs=W[:, k, c0:c0 + nc_],
                        start=(k == 0), stop=(k == KD - 1)),
                         reads=[xbk] + wkeys(name, k, c0, c0 + nc_), writes=[f"ps{b}"])
                cb(ps[:nt, :nc_], f"ps{b}", t0 + tb * 128, nt)


class Stager:
    def __init__(self, P, C, name, shape, dt, n=3):
        self.P, self.name, self.n, self.i = P, name, n, 0
        self.bufs = [C.A.alloc(shape, dt) for _ in range(n)]

    def put(self, ps_ap, pskey, dst_ap, rows, cols, eng="act", func=None):
        P = self.P
        s = self.i % self.n
        self.i += 1
        B = self.bufs[s]
        k = (self.name, s)
        if eng == "act":
            P.op("act", R.activation(out=B[:rows, :cols], in_=ps_ap, func=(func or AF.Copy)), reads=[pskey], writes=[k])
        else:
            P.op("dve", R.tensor_copy(out=B[:rows, :cols], in_=ps_ap), reads=[pskey], writes=[k])
        P.op("sp", R.dma_start(out=dst_ap, in_=B[:rows, :cols]), reads=[k], dma=f"{self.name}{s}")
        return B, k


def mixer_b(P, C, Xin, Xout, lnidx):
    nc = P.nc
    S, NP_, TPc, TTc = C.SEQ, C.NPS, C.TP, C.TT
    keep = min(512, S)
    tiles = C.tiles
    P.barrier()
    C.A.reset()
    QT, KT, V = C.sB_QT, C.sB_KT, C.sB_V
    st_fm = Stager(P, C, "stfm", [512], BF16, 4)
    st_tm = Stager(P, C, "sttm", [512], BF16, 4)
    st_o = Stager(P, C, "sto", [512], F32, 4)

    def kcol(t0):
        return t0 + 512 if t0 >= TPc else t0

    def out_rows(t0, nt):
        if t0 >= TPc:
            return C.d_bks[480:480 + nt, :], C.d_bvs[480:480 + nt, :]
        sq, tl = divmod(t0, S)
        if tl >= S - keep:
            r = tl - (S - keep)
            return C.d_bkp[sq, r:r + nt, :], C.d_bvp[sq, r:r + nt, :]
        return None

    fm = []
    for c in range(8):
        fm.append((c * 128, 128, lambda ps, pk, t0, N, c=c: st_fm.put(ps, pk, QT[c * 128:(c + 1) * 128, t0:t0 + N], 128, N)))
    for c in range(8):
        fm.append((1024 + c * 128, 128, lambda ps, pk, t0, N, c=c: st_fm.put(
            ps, pk, KT[c * 128:(c + 1) * 128, kcol(t0):kcol(t0) + N], 128, N, eng="dve")))
    tm = []
    for hf in range(2):
        def cbv(ps, pk, t0, nt, hf=hf):
            st_tm.put(ps, pk, V[kcol(t0):kcol(t0) + nt, hf * 512:(hf + 1) * 512], nt, 512, eng="dve")
            o = out_rows(t0, nt)
            if o is not None:
                st_o.put(ps, pk, o[1][:, hf * 512:(hf + 1) * 512], nt, 512)
        tm.append((2048 + hf * 512, 512, cbv))

        def cbk(ps, pk, t0, nt, hf=hf):
            o = out_rows(t0, nt)
            if o is not None:
                st_o.put(ps, pk, o[0][:, hf * 512:(hf + 1) * 512], nt, 512)
        tm.append((1024 + hf * 512, 512, cbk))
    inproj_generic(P, C, "bwin", Xin, C.d_b_w_in, 3072, tiles, fm, tm)
    P.op("pool", R.dma_start(out=KT[:, TPc:TPc + 512], in_=C.d_cbkT), dma="cbk")
    P.op("pool", R.dma_start(out=V[TPc:TPc + 512, :], in_=C.d_cbk_v), dma="cbv")
    P.op("sp", R.dma_start(out=C.d_bks[0:480, :], in_=C.d_cbk_k[32:512, :]), dma="cbk2")
    P.op("sp", R.dma_start(out=C.d_bvs[0:480, :], in_=C.d_cbk_v[32:512, :]), dma="cbv2")

    P.barrier()
    C.A.reset()
    A = C.A
    BT = A.alloc([16, 2, 128], F32)
    FAR = A.alloc([16], F32)
    ONE_E = A.alloc([128], BF16)
    ONE_O = A.alloc([128], BF16)
    NR = 6
    P.op("sp", R.dma_start(out=BT[:], in_=C.d_b_bias), writes=["bt"], dma="bt")
    P.op("sp", R.dma_start(out=FAR[:], in_=C.d_b_far), writes=["far"], dma="far")
    P.op("pool", R.memset(BT[64:128, :, 0, 0:64], NEG), reads=["bt"], writes=["bt"])
    P.op("pool", R.memset(ONE_E[:, 0:64], 1.0), writes=["one_e"])
    P.op("pool", R.memset(ONE_E[:, 64:128], 0.0), writes=["one_e"])
    P.op("pool", R.memset(ONE_O[:, 0:64], 0.0), writes=["one_o"])
    P.op("pool", R.memset(ONE_O[:, 64:128], 1.0), writes=["one_o"])
    ao_d = C.sAO[0:1024, :].rearrange("(k p) t -> p k t", p=128)
    qt_d = QT.rearrange("(k p) t -> p k t", p=128)
    kt_d = KT.rearrange("(k p) t -> p k t", p=128)
    SHARED = ("bt", "far", "one_e", "one_o")

    def b2_stream(si, seqs):
        KB = [A.alloc([8, 128], BF16) for _ in range(NR)]
        VB = [A.alloc([16, 128], BF16) for _ in range(NR)]
        QE = [A.alloc([8, 128], BF16) for _ in range(2)]
        QO = [A.alloc([8, 128], BF16) for _ in range(2)]
        AOt = [A.alloc([8, 128], BF16) for _ in range(2)]
        EN = [A.alloc([2, 128], BF16) for _ in range(2)]
        EF = [A.alloc([3, 128], BF16) for _ in range(2)]
        LG = [A.alloc([2, 128], F32) for _ in range(2)]
        RD = [A.alloc([128], F32) for _ in range(2)]
        Z = 4 * si

        def kx(k):
            if isinstance(k, str) and (k.startswith("ps") or k in SHARED):
                return k
            return (si, k)

        def op(eng, fn, reads=(), writes=(), dma=None):
            return P.op(eng, fn, reads=[kx(k) for k in reads], writes=[kx(k) for k in writes],
                        dma=(None if dma is None else f"{dma}_{si}"))
        for r in range(NR):
            yield op("pool", R.memset(VB[r][:], 0.0), writes=[("vb", r)])
        for r in range(2):
            yield op("pool", R.memset(QE[r][:], 0.0), writes=[("qe", r)])
            yield op("pool", R.memset(QO[r][:], 0.0), writes=[("qo", r)])
            yield op("pool", R.memset(EF[r][:], 0.0), writes=[("ef", r)])
        bank = 0
        qi_ = 0
        for (q0, k0, nqb, qw, ib0) in seqs:
            loaded = {}

            def load_kv(j, nk):
                r = j % NR
                yield op("sp", R.dma_start(out=KB[r][:, :, :nk], in_=kt_d[:, :, k0 + j * 128:k0 + j * 128 + nk]),
                     writes=[("kb", r)], dma=f"kb{r}")
                for par in range(2):
                    src = V[k0 + j * 128:k0 + j * 128 + nk, :].rearrange("s (h p d) -> s h p d", p=2, d=64)[:, :, par, :]
                    yield op("sp", R.dma_start(out=VB[r].rearrange("p (c q) d -> p c q d", q=2)[:nk, :, par, par * 64:(par + 1) * 64], in_=src),
                         writes=[("vb", r)], dma=f"vb{r}")
            for ib in range(nqb):
                i = ib + ib0
                s2 = qi_ % 2
                qi_ += 1
                tq = q0 + ib * 128 if qw == 128 else q0 + 512
                qcol = q0 + ib * 128
                yield op("sp", R.dma_start(out=QE[s2][0:64, :, :qw], in_=qt_d[0:64, :, qcol:qcol + qw]),
                     writes=[("qe", s2)], dma=f"qe{s2}")
                yield op("sp", R.dma_start(out=QO[s2][64:128, :, :qw], in_=qt_d[64:128, :, qcol:qcol + qw]),
                     writes=[("qo", s2)], dma=f"qo{s2}")
                jlist = list(range(max(0, i - 4), i + 1))
                for j in jlist:
                    if j not in loaded:
                        nk = TS if (qw != 128 and j == 4) else 128
                        yield from load_kv(j, nk)
                        loaded[j] = nk
                for c in range(8):
                    po = C.PSB[Z + 2]
                    pd = C.PSB[Z + 3]
                    pok, pdk = f"ps{Z + 2}", f"ps{Z + 3}"
                    first = True
                    for par in range(2):
                        h = 2 * c + par
                        Qh = (QE if par == 0 else QO)[s2]
                        qk = ("qe" if par == 0 else "qo", s2)
                        near = [j for j in jlist if i - j <= 1]
                        far = [j for j in jlist if i - j >= 2]
                        pn, pf = C.PSB[Z], C.PSB[Z + 1]
                        pnk, pfk = f"ps{Z}", f"ps{Z + 1}"
                        pnv = pn[:, 0:256].rearrange("p (d t) -> p d t", d=2)
                        pfv = pf[:, 0:384].rearrange("p (d t) -> p d t", d=3)
                        for j in jlist:
                            d = i - j
                            nk = loaded[j]
                            dst = pnv[:nk, d, :qw] if d <= 1 else pfv[:nk, d - 2, :qw]
                            yield op("pe", R.matmul(
                                dst, lhsT=KB[j % NR][:, c, :nk], rhs=Qh[:, c, :qw], start=True, stop=True),
                                 reads=[("kb", j % NR), qk], writes=[pnk if d <= 1 else pfk])
                        ENp, LGp, EFp = EN[par], LG[par], EF[par]
                        for j in near:
                            d = i - j
                            nk = loaded[j]
                            yield op("dve", R.scalar_tensor_tensor(
                                out=LGp[:nk, d, :qw], in0=pnv[:nk, d, :qw], scalar=SC_B,
                                in1=BT[:nk, h, d, :qw], op0=ALU.mult, op1=ALU.add),
                                 reads=[pnk, "bt"], writes=[("lg", par)])
                            yield op("act", R.activation(
                                out=ENp[:nk, d, :qw], in_=LGp[:nk, d, :qw], func=AF.Exp),
                                 reads=[("lg", par)], writes=[("en", par)])
                        if far:
                            dlo, dhi = min(i - j for j in far), max(i - j for j in far)
                            if dhi == 4 and qw == 128:
                                yield op("act", R.activation(
                                    out=EFp[:, 0:2, :], in_=pfv[:, 0:2, :], func=AF.Exp,
                                    bias=FAR[:, h:h + 1], scale=SC_B), reads=[pfk, "far"], writes=[("ef", par)])
                                yield op("act", R.activation(
                                    out=EFp[:, 2, 0:64], in_=pfv[:, 2, 0:64], func=AF.Exp,
                                    bias=FAR[:, h:h + 1], scale=SC_B), reads=[pfk, "far"], writes=[("ef", par)])
                                yield op("act", R.activation(
                                    out=EFp[64:128, 2, 64:128], in_=pfv[64:128, 2, 64:128], func=AF.Exp,
                                    bias=FAR[64:128, h:h + 1], scale=SC_B), reads=[pfk, "far"], writes=[("ef", par)])
                            else:
                                yield op("act", R.activation(
                                    out=EFp[:, dlo - 2:dhi - 1, :qw], in_=pfv[:, dlo - 2:dhi - 1, :qw],
                                    func=AF.Exp, bias=FAR[:, h:h + 1], scale=SC_B), reads=[pfk, "far"], writes=[("ef", par)])
                        ONE = ONE_E if par == 0 else ONE_O
                        onek = "one_e" if par == 0 else "one_o"
                        blocks = [(j, ENp, ("en", par), i - j) for j in near] + [(j, EFp, ("ef", par), i - j - 2) for j in far]
                        for bi, (j, E_, ek, slot) in enumerate(blocks):
                            nk = loaded[j]
                            last = (par == 1 and bi == len(blocks) - 1)
                            yield op("pe", R.matmul(
                                po[:, :qw], lhsT=VB[j % NR][:nk, h, :], rhs=E_[:nk, slot, :qw], start=first, stop=last),
                                 reads=[("vb", j % NR), ek], writes=[pok])
                            yield op("pe", R.matmul(
                                pd[:, :qw], lhsT=ONE[:nk, :], rhs=E_[:nk, slot, :qw], start=first, stop=last),
                                 reads=[onek, ek], writes=[pdk])
                            first = False
                    r2 = c % 2
                    yield op("dve", R.reciprocal(out=RD[r2][:, :qw], in_=pd[:, :qw]),
                         reads=[pdk], writes=[("rd", r2)])
                    yield op("dve", R.tensor_tensor(
                        out=AOt[s2][:, c, :qw], in0=po[:, :qw], in1=RD[r2][:, :qw], op=ALU.mult),
                         reads=[pok, ("rd", r2)], writes=[("aot", s2)])
                yield op("sp", R.dma_start(out=ao_d[:, :, qcol:qcol + qw], in_=AOt[s2][:, :, :qw]),
                     reads=[("aot", s2)], dma=f"aot{s2}")

    allseq = [(sq * S, sq * S, S // 128, 128, 0) for sq in range(NP_)]
    samp = (TPc, TPc, 1, TS, 4)
    if NP_ >= 2:
        run_streams([b2_stream(0, [allseq[0], samp]), b2_stream(1, allseq[1:])])
    else:
        run_streams([b2_stream(0, allseq), b2_stream(1, [samp])])
    outproj_phase(P, C, "bwout", C.sAO[0:1024, :], 8, C.d_b_w_out, Xin, Xout, lnidx, tiles)


SC_A = 128.0 ** -0.5
C_IDX = (64.0 ** -0.5) * (8.0 ** -0.5)
NBIS = 20


def mixer_a(P, C, Xin, Xout, lnidx):
    S, NP_, TPc, TTc = C.SEQ, C.NPS, C.TP, C.TT
    tiles = C.tiles
    PAST = 1024
    P.barrier()
    C.A.reset()
    QT, KT, V, QIT, KIT, WI = C.sA_QT, C.sA_KT, C.sA_V, C.sA_QIT, C.sA_KIT, C.sA_WI
    st_fm = Stager(P, C, "stfm", [512], BF16, 4)
    st_tm = Stager(P, C, "sttm", [256], BF16, 3)
    st_o = Stager(P, C, "sto", [512], F32, 4)

    def kcol(t0):
        return t0 + PAST if t0 >= TPc else t0

    def orow(t0, nt, dp, ds):
        if t0 >= TPc:
            return ds[0:nt, :]
        sq, tl = divmod(t0, S)
        return dp[sq, tl:tl + nt, :]

    fm = []
    for c in range(8):
        fm.append((c * 128, 128, lambda ps, pk, t0, N, c=c: st_fm.put(ps, pk, QT[c * 128:(c + 1) * 128, t0:t0 + N], 128, N)))
    for c in range(2):
        fm.append((1024 + c * 128, 128, lambda ps, pk, t0, N, c=c: st_fm.put(
            ps, pk, KT[c * 128:(c + 1) * 128, kcol(t0):kcol(t0) + N], 128, N, eng="dve")))
    for c in range(4):
        fm.append((1536 + c * 128, 128, lambda ps, pk, t0, N, c=c: st_fm.put(ps, pk, QIT[c * 128:(c + 1) * 128, t0:t0 + N], 128, N)))
    fm.append((2048, 64, lambda ps, pk, t0, N: st_fm.put(ps, pk, KIT[:, kcol(t0):kcol(t0) + N], 64, N, eng="dve")))

    def cb_kv(ps, pk, t0, nt):
        st_o.put(ps[:, 0:256], pk, orow(t0, nt, C.d_akp, C.d_aks), nt, 256)
        st_o.put(ps[:, 256:512], pk, orow(t0, nt, C.d_avp, C.d_avs), nt, 256, eng="dve")
        st_tm.put(ps[:, 256:512], pk, V[kcol(t0):kcol(t0) + nt, :], nt, 256)

    def cb_iw(ps, pk, t0, nt):
        st_o.put(ps[:, 0:64], pk, orow(t0, nt, C.d_aip, C.d_ais), nt, 64)
        st_o.put(ps[:, 64:72], pk, WI[t0:t0 + nt, :], nt, 8, eng="dve")
    tm = [(1024, 512, cb_kv), (2048, 72, cb_iw)]
    inproj_generic(P, C, "awin", Xin, C.d_a_w_in, 2120, tiles, fm, tm)
    P.op("pool", R.dma_start(out=KT[:, TPc:TPc + PAST], in_=C.d_cakT), dma="cak")
    P.op("pool", R.dma_start(out=V[TPc:TPc + PAST, :], in_=C.d_cav), dma="cav")
    P.op("pool", R.dma_start(out=KIT[:, TPc:TPc + PAST], in_=C.d_cakiT), dma="caki")

    P.barrier()
    C.A.reset()
    A = C.A
    NBmax = max(S, PAST + 128) // 128
    BTA = A.alloc([3, 8, 128], BF16)
    IDN = A.alloc([128], BF16)
    ONES1 = A.alloc([128], BF16)
    off0 = A.off
    BTF = A.alloc([3, 8, 128], F32)
    P.op("sp", R.dma_start(out=BTF[:], in_=C.d_a_bias), writes=["btf"], dma="bt")
    P.op("act", R.activation(out=BTA[:], in_=BTF[:], func=AF.Copy, scale=1.0 / SC_A), reads=["btf"], writes=["bta"])
    P.op("pool", R.dma_start(out=IDN[:], in_=C.d_ident), writes=["idn"], dma="idn")
    P.op("pool", R.memset(ONES1[:], 1.0), writes=["ones1"])
    P.barrier()
    A.off = off0

    def mkset():
        B = Ctx()
        B.KTs = A.alloc([2, NBmax * 128], BF16)
        B.Vs = A.alloc([NBmax, 256], BF16)
        B.KITs = A.alloc([NBmax * 128], BF16)
        B.SC = A.alloc([NBmax * 128], F32)
        B.MK = A.alloc([NBmax * 128], BF16)
        B.MT = A.alloc([NBmax, 128], BF16)
        B.QTb = [A.alloc([8, 128], BF16) for _ in range(2)]
        B.QIb = [A.alloc([8, 128], BF16) for _ in range(2)]
        B.WIb = [A.alloc([8], F32) for _ in range(2)]
        B.RL = [A.alloc([512], F32) for _ in range(2)]
        B.EX = [A.alloc([512], BF16) for _ in range(2)]
        B.AOt = [A.alloc([8, 128], BF16) for _ in range(2)]
        B.RD = [A.alloc([512], F32) for _ in range(2)]
        B.LO, B.MID, B.CNT, B.PRD = (A.alloc([1], F32) for _ in range(4))
        return B
    ao_d = C.sAO[0:1024, :].rearrange("(k p) t -> p k t", p=128)
    qt_d = QT.rearrange("(k p) t -> p k t", p=128)
    qit_d = QIT.rearrange("(h p) t -> p h t", p=64)
    kt_d = KT.rearrange("(g p) t -> p g t", p=128)
    SHARED = ("bta", "idn", "ones1")

    def a2_stream(si, seqs):
        B = mkset()
        KTs, Vs, KITs, SC, MK, MT, QTb, QIb, WIb, RL, EX, AOt, RD = (B.KTs, B.Vs, B.KITs, B.SC, B.MK, B.MT, B.QTb, B.QIb, B.WIb,
                                                                      B.RL, B.EX, B.AOt, B.RD)
        JUNK = MK
        LO, MID, CNT, PRD = B.LO, B.MID, B.CNT, B.PRD
        PL = [C.PSB[4 * si], C.PSB[4 * si + 3]]
        PLK = [f"ps{4 * si}", f"ps{4 * si + 3}"]

        def kx(k):
            if isinstance(k, str) and (k.startswith("ps") or k in SHARED):
                return k
            return (si, k)

        def op(eng, fn, reads=(), writes=(), dma=None):
            return P.op(eng, fn, reads=[kx(k) for k in reads], writes=[kx(k) for k in writes],
                        dma=(None if dma is None else f"{dma}_{si}"))
        qi_ = 0
        rl_i = 0
        ex_i = 0
        for (q0, k0, nqb, qw, ib0, nkeys) in seqs:
            nblk = (nkeys + 127) // 128
            yield op("sp", R.dma_start(out=KTs[:, :, :nkeys], in_=kt_d[:, :, k0:k0 + nkeys]), writes=["kts"], dma="kts")
            yield op("sp", R.dma_start(out=KITs[0:64, :nkeys], in_=KIT[:, k0:k0 + nkeys]), writes=["kits"], dma="kits")
            nfull = nkeys // 128
            yield op("sp", R.dma_start(out=Vs[:, :nfull, :], in_=V[k0:k0 + nfull * 128, :].rearrange("(j s) c -> s j c", s=128)),
                 writes=["vs"], dma="vs")
            if nkeys % 128:
                rem = nkeys % 128
                yield op("sp", R.dma_start(out=Vs[:rem, nfull, :], in_=V[k0 + nfull * 128:k0 + nkeys, :]), writes=["vs"], dma="vs")
            for ib in range(nqb):
                i = ib + ib0
                s2 = qi_ % 2
                qi_ += 1
                qcol = q0 + ib * 128
                nk = min(nkeys, 128 * (i + 1))
                nb = (nk + 127) // 128
                yield op("sp", R.dma_start(out=QTb[s2][:, :, :qw], in_=qt_d[:, :, qcol:qcol + qw]), writes=[("qtb", s2)], dma=f"qtb{s2}")
                yield op("sp", R.dma_start(out=QIb[s2][0:64, :, :qw], in_=qit_d[:, :, qcol:qcol + qw]), writes=[("qib", s2)], dma=f"qib{s2}")
                yield op("sp", R.dma_start(out=WIb[s2][:qw, :], in_=WI[qcol:qcol + qw, :]), writes=[("wib", s2)], dma=f"wib{s2}")
                for st in range((nk + 511) // 512):
                    c0 = st * 512
                    n_ = min(512, nk - c0)
                    for h in range(8):
                        b = (st * 8 + h) % 2
                        ps = PL[b]
                        yield op("pe", R.matmul(ps[:qw, :n_], lhsT=QIb[s2][0:64, h, :qw], rhs=KITs[0:64, c0:c0 + n_], start=True, stop=True),
                             reads=[("qib", s2), "kits"], writes=[PLK[b]])
                        rb = rl_i % 2
                        rl_i += 1
                        yield op("act", R.activation(out=RL[rb][:qw, :n_], in_=ps[:qw, :n_], func=AF.Relu, scale=C_IDX),
                             reads=[PLK[b]], writes=[("rl", rb)])
                        if h == 0:
                            yield op("dve", R.tensor_scalar(out=SC[:qw, c0:c0 + n_], in0=RL[rb][:qw, :n_], scalar1=WIb[s2][:qw, 0:1],
                                                        scalar2=None, op0=ALU.mult),
                                 reads=[("rl", rb), ("wib", s2)], writes=[("sc", st)])
                        else:
                            yield op("dve", R.scalar_tensor_tensor(out=SC[:qw, c0:c0 + n_], in0=RL[rb][:qw, :n_], scalar=WIb[s2][:qw, h:h + 1],
                                                               in1=SC[:qw, c0:c0 + n_], op0=ALU.mult, op1=ALU.add),
                                 reads=[("rl", rb), ("wib", s2), ("sc", st)], writes=[("sc", st)])
                sck = [("sc", st) for st in range((nk + 511) // 512)]
                if qw == 128:
                    yield op("pool", R.memset(SC[0:64, nk - 64:nk], -1.0e30), reads=sck, writes=sck)
                yield op("pool", R.memset(LO[:qw, :], -16.0), writes=["lo"])
                for it in range(NBIS):
                    step = 16.0 / (2 ** it)
                    yield op("dve", R.tensor_scalar(out=MID[:qw, :], in0=LO[:qw, :], scalar1=step, scalar2=None, op0=ALU.add),
                         reads=["lo"], writes=["mid"])
                    yield op("dve", R.tensor_scalar(out=JUNK[:qw, :nk], in0=SC[:qw, :nk], scalar1=MID[:qw, 0:1], scalar2=0.0,
                                                op0=ALU.is_ge, op1=ALU.add, accum_out=CNT[:qw, :]),
                         reads=sck + ["mid"], writes=["junk", "cnt"])
                    yield op("dve", R.tensor_scalar(out=PRD[:qw, :], in0=CNT[:qw, :], scalar1=255.5, scalar2=step,
                                                op0=ALU.is_ge, op1=ALU.mult),
                         reads=["cnt"], writes=["prd"])
                    yield op("dve", R.tensor_tensor(out=LO[:qw, :], in0=LO[:qw, :], in1=PRD[:qw, :], op=ALU.add),
                         reads=["lo", "prd"], writes=["lo"])
                yield op("dve", R.tensor_scalar(out=MK[:qw, :nk], in0=SC[:qw, :nk], scalar1=LO[:qw, 0:1], scalar2=NEG,
                                            op0=ALU.is_lt, op1=ALU.mult),
                     reads=sck + ["lo"], writes=["mk"])
                pT = C.PSALL.bitcast(BF16)[:, (4 * si + 3) * 1024:(4 * si + 3) * 1024 + 1024]
                for j0 in range(0, nb, 4):
                    jn = min(4, nb - j0)
                    for j in range(j0, j0 + jn):
                        w_ = min(128, nk - j * 128)
                        yield op("pe", R.transpose(out=pT[:w_, (j - j0) * 128:(j - j0) * 128 + qw], in_=MK[:qw, j * 128:j * 128 + w_],
                                               identity=IDN[:qw, :qw]),
                             reads=["mk", "idn"], writes=[PLK[1]])
                    if nk - j0 * 128 >= jn * 128:
                        yield op("act", R.activation(out=MT[:, j0:j0 + jn, :qw],
                                                 in_=pT[:, 0:jn * 128].rearrange("p (j t) -> p j t", j=jn)[:, :, :qw], func=AF.Copy),
                             reads=[PLK[1]], writes=[("mt", j0 // 4)])
                    else:
                        for j in range(j0, j0 + jn):
                            w_ = min(128, nk - j * 128)
                            yield op("act", R.activation(out=MT[:w_, j, :qw], in_=pT[:w_, (j - j0) * 128:(j - j0) * 128 + qw], func=AF.Copy),
                                 reads=[PLK[1]], writes=[("mt", j0 // 4)])
                for g in range(2):
                    po, pd = C.PSB[4 * si + 1], C.PSB[4 * si + 2]
                    pok, pdk = f"ps{4 * si + 1}", f"ps{4 * si + 2}"
                    NQ = 4 * qw
                    for j in range(nb):
                        w_ = min(128, nk - j * 128)
                        d = i - j
                        dc = min(d, 2)
                        b = j % 2
                        pl = PL[b]
                        plv = pl[:, :].rearrange("p (h t) -> p h t", h=4)
                        yield op("pe", R.matmul(plv[:w_, :, :qw], lhsT=KTs[:, g, j * 128:j * 128 + w_], rhs=QTb[s2][:, 4 * g:4 * g + 4, :qw],
                                            start=True, stop=False),
                             reads=["kts", ("qtb", s2)], writes=[PLK[b]])
                        yield op("pe", R.matmul(plv[:w_, :, :qw], lhsT=IDN[:w_, :w_], rhs=BTA[:w_, dc, 4 * g:4 * g + 4, :qw],
                                            start=False, stop=False),
                             reads=["idn", "bta"], writes=[PLK[b]])
                        for hh in range(4):
                            yield op("pe", R.matmul(plv[:w_, hh, :qw], lhsT=IDN[:w_, :w_], rhs=MT[:w_, j, :qw], start=False, stop=(hh == 3)),
                                 reads=["idn", ("mt", j // 4)], writes=[PLK[b]])
                        eb = ex_i % 2
                        ex_i += 1
                        EXv = EX[eb][:, :].rearrange("p (h t) -> p h t", h=4)
                        yield op("act", R.activation(out=EXv[:w_, :, :qw], in_=plv[:w_, :, :qw], func=AF.Exp, scale=SC_A),
                             reads=[PLK[b]], writes=[("ex", eb)])
                        pov = po[:, :].rearrange("p (h t) -> p h t", h=4)
                        pdv = pd[:, :].rearrange("p (h t) -> p h t", h=4)
                        yield op("pe", R.matmul(pov[:, :, :qw], lhsT=Vs[:w_, j, g * 128:(g + 1) * 128], rhs=EXv[:w_, :, :qw],
                                            start=(j == 0), stop=(j == nb - 1)),
                             reads=["vs", ("ex", eb)], writes=[pok])
                        yield op("pe", R.matmul(pdv[:, :, :qw], lhsT=ONES1[:w_, :], rhs=EXv[:w_, :, :qw],
                                            start=(j == 0), stop=(j == nb - 1)),
                             reads=["ones1", ("ex", eb)], writes=[pdk])
                    RDv = RD[g][:, :].rearrange("p (h t) -> p h t", h=4)
                    yield op("dve", R.reciprocal(out=RDv[:, :, :qw], in_=pdv[:, :, :qw]), reads=[pdk], writes=[("rd", g)])
                    yield op("dve", R.tensor_tensor(out=AOt[s2][:, 4 * g:4 * g + 4, :qw], in0=pov[:, :, :qw], in1=RDv[:, :, :qw], op=ALU.mult),
                         reads=[pok, ("rd", g)], writes=[("aot", s2)])
                yield op("sp", R.dma_start(out=ao_d[:, :, qcol:qcol + qw], in_=AOt[s2][:, :, :qw]), reads=[("aot", s2)], dma=f"aot{s2}")

    allseq = [(sq * S, sq * S, S // 128, 128, 0, S) for sq in range(NP_)]
    samp = (TPc, TPc, 1, TS, PAST // 128, PAST + TS)
    if NP_ >= 2:
        run_streams([a2_stream(0, [allseq[0], samp]), a2_stream(1, allseq[1:])])
    else:
        run_streams([a2_stream(0, allseq), a2_stream(1, [samp])])
    outproj_phase(P, C, "awout", C.sAO[0:1024, :], 8, C.d_a_w_out, Xin, Xout, lnidx, tiles)


def mixer_c(P, C, Xin, Xout, lnidx):
    S, NP_, TPc, TTc = C.SEQ, C.NPS, C.TP, C.TT
    tiles = C.tiles
    ZT, XC, XTM, DTS = C.sC_ZT, C.sC_XC, C.sC_XTM, C.sC_DTS
    P.barrier()
    C.A.reset()
    A = C.A
    st_z = Stager(P, C, "stz", [512], BF16, 3)
    st_t = Stager(P, C, "stt", [512], BF16, 3)
    HALO = A.alloc([32, 3], F32)
    CW = A.alloc([32, 4], F32)
    CB = A.alloc([32], F32)
    DTB = A.alloc([32], F32)
    ANEG = A.alloc([32], F32)
    IDN = A.alloc([128], BF16)
    XP = [A.alloc([515], F32) for _ in range(3)]
    ACC = [A.alloc([512], F32) for _ in range(2)]
    XS = [A.alloc([4, 512], BF16) for _ in range(2)]
    DTT = [A.alloc([96], F32) for _ in range(2)]
    P.op("sp", R.dma_start(out=CW[:], in_=C.d_c_cw), writes=["cw"], dma="cw")
    P.op("sp", R.dma_start(out=CB[:], in_=C.d_c_cb), writes=["cb"], dma="cb")
    P.op("sp", R.dma_start(out=DTB[:], in_=C.d_c_dtb), writes=["dtb"], dma="dtb")
    P.op("sp", R.dma_start(out=ANEG[:], in_=C.d_c_alog), writes=["aneg"], dma="aneg")
    P.op("act", R.activation(out=ANEG[:], in_=ANEG[:], func=AF.Exp), reads=["aneg"], writes=["aneg"])
    P.op("pool", R.tensor_scalar(out=ANEG[:], in0=ANEG[:], scalar1=-1.0, scalar2=None, op0=ALU.mult), reads=["aneg"], writes=["aneg"])
    P.op("pool", R.dma_start(out=IDN[:], in_=C.d_ident), writes=["idn"], dma="idn")
    st = {"xp": 0, "dt": 0, "tb": 0}
    pend = []
    pTall = C.PSALL.bitcast(BF16)

    def cb_z(c):
        return lambda ps, pk, t0, N: st_z.put(ps, pk, ZT[c * 128:(c + 1) * 128, t0:t0 + N], 128, N, func=AF.Silu)

    def cb_x(cc):
        def f(ps, pk, t0, N):
            if cc == 0:
                if t0 >= TPc:
                    P.op("sp", R.dma_start(out=HALO[:], in_=C.d_c_conv0.rearrange("(c p) j -> p c j", p=128)), writes=["halo"], dma="halo")
                elif t0 % S == 0:
                    P.op("pool", R.memset(HALO[:], 0.0), writes=["halo"])
            r = st["xp"] % 3
            st["xp"] += 1
            X_ = XP[r]
            xk = ("xp", r)
            P.op("act", R.activation(out=X_[:, 3:3 + N], in_=ps, func=AF.Copy), reads=[pk], writes=[xk])
            P.op("pool", R.tensor_copy(out=X_[:, 0:3], in_=HALO[:, cc, :]), reads=["halo"], writes=[xk])
            a = ACC[cc % 2]
            ak = ("acc", cc % 2)
            P.op("dve", R.tensor_scalar(out=a[:, :N], in0=X_[:, 0:N], scalar1=CW[:, cc, 0:1], scalar2=CB[:, cc:cc + 1],
                                        op0=ALU.mult, op1=ALU.add), reads=[xk, "cw", "cb"], writes=[ak])
            for j in range(1, 4):
                P.op("dve", R.scalar_tensor_tensor(out=a[:, :N], in0=X_[:, j:j + N], scalar=CW[:, cc, j:j + 1], in1=a[:, :N],
                                                   op0=ALU.mult, op1=ALU.add), reads=[xk, "cw", ak], writes=[ak])
            P.op("pool", R.tensor_copy(out=HALO[:, cc, :], in_=X_[:, N:N + 3]), reads=[xk], writes=["halo"])
            while pend:
                pend.pop(0)()
            pend.append(lambda: tail(cc, a, ak, t0, N))
            if cc == 31:
                while pend:
                    pend.pop(0)()

        def tail(cc, a, ak, t0, N):
            gi = (cc // 4) % 2
            xsk = ("xs", gi)
            P.op("act", R.activation(out=XS[gi][:, cc % 4, :N], in_=a[:, :N], func=AF.Silu), reads=[ak], writes=[xsk])
            P.op("sp", R.dma_start(out=XC[cc * 128:(cc + 1) * 128, t0:t0 + N], in_=XS[gi][:, cc % 4, :N]), reads=[xsk], dma=f"xc{gi}")
            if cc % 4 == 3 and cc < 24:
                cc0 = cc - 3
                for tb in range((N + 127) // 128):
                    nt = min(128, N - tb * 128)
                    b = 6 + st["tb"] % 2
                    st["tb"] += 1
                    pT = pTall[:, b * 1024:b * 1024 + 512]
                    for q in range(4):
                        P.op("pe", R.transpose(out=pT[:nt, q * 128:(q + 1) * 128], in_=XS[gi][:, q, tb * 128:tb * 128 + nt], identity=IDN[:, :]),
                             reads=[xsk, "idn"], writes=[f"ps{b}"])
                    st_t.put(pT[:nt, :], f"ps{b}", XTM[t0 + tb * 128:t0 + tb * 128 + nt, cc0 * 128:cc0 * 128 + 512], nt, 512, eng="dve")
            if cc == 31:
                if t0 >= TPc:
                    P.op("sp", R.dma_start(out=C.d_cconv_s.rearrange("(c p) j -> p c j", p=128), in_=HALO[:]), reads=["halo"], dma="halo_o")
                elif (t0 + N) % S == 0:
                    P.op("sp", R.dma_start(out=C.d_cconv_p[t0 // S].rearrange("(c p) j -> p c j", p=128), in_=HALO[:]), reads=["halo"], dma="halo_o")
        return f

    def cb_dt(ps, pk, t0, nt):
        r = st["dt"] % 2
        st["dt"] += 1
        T_ = DTT[r]
        k = ("dtt", r)
        P.op("dve", R.tensor_tensor(out=T_[:nt, 0:32], in0=ps, in1=DTB[:nt, :], op=ALU.add), reads=[pk, "dtb"], writes=[k])
        P.op("act", R.activation(out=T_[:nt, 0:32], in_=T_[:nt, 0:32], func=AF.Exp), reads=[k], writes=[k])
        P.op("act", R.activation(out=T_[:nt, 0:32], in_=T_[:nt, 0:32], func=AF.Ln, bias=C.onec[:nt, 0:1]), reads=[k, "onec"], writes=[k])
        P.op("act", R.activation(out=T_[:nt, 32:64], in_=T_[:nt, 0:32], func=AF.Ln), reads=[k], writes=[k])
        P.op("dve", R.tensor_tensor(out=T_[:nt, 64:96], in0=T_[:nt, 0:32], in1=ANEG[:nt, :], op=ALU.mult), reads=[k, "aneg"], writes=[k])
        P.op("sp", R.dma_start(out=DTS[t0:t0 + nt, :], in_=T_[:nt, :]), reads=[k], dma=f"dtt{r}")
    fm = [(c * 128, 128, cb_z(c)) for c in range(16)] + [(2048 + cc * 128, 128, cb_x(cc)) for cc in range(32)]
    tm = [(6144, 32, cb_dt)]
    inproj_generic(P, C, "cwin", Xin, C.d_c_w_in, 6176, tiles, fm, tm)

    if STOP_AFTER_INPROJ:
        return
    P.barrier()
    A.reset()
    UT = A.alloc([64], F32)
    NEG32 = A.alloc([32, 64], F32)
    NEGT = A.alloc([64], F32)
    ONESF = A.alloc([128], F32)
    ONESG = A.alloc([128], BF16)
    DSK = A.alloc([16], F32)
    P.op("sp", R.dma_start(out=UT[0:64, :], in_=C.d_ut), writes=["ut"], dma="ut")
    P.op("sp", R.dma_start(out=NEGT[0:64, :], in_=C.d_negt), writes=["negt"], dma="negt")
    P.op("sp", R.dma_start(out=DSK[:], in_=C.d_c_dsk), writes=["dsk"], dma="dsk")
    P.op("dve", R.tensor_copy(out=NEG32[0:64], in_=NEGT[0:64, :].rearrange("p (o t) -> p o t", o=1).to_broadcast([64, 32, 64])),
         reads=["negt"], writes=["neg32"])
    P.op("pool", R.memset(ONESF[:], 1.0), writes=["onesf"])
    P.op("pool", R.memset(ONESG[:], 1.0 / 256.0), writes=["onesg"])
    xc_d = XC.rearrange("(c p) t -> p c t", p=128)
    zt_d = ZT.rearrange("(c p) t -> p c t", p=128)
    ao_d = C.sAO.rearrange("(c p) t -> p c t", p=128)
    SHARED = ("ut", "neg32", "onesf", "onesg", "dsk", "epsc")

    def c2_stream(si, seqs):
        H = A.alloc([32, 64], F32)
        Hb = A.alloc([32, 64], BF16)
        DTc = [A.alloc([96], F32) for _ in range(2)]
        XHT = [A.alloc([3072], BF16) for _ in range(2)]
        XCF = [A.alloc([32, 64], BF16) for _ in range(2)]
        ZTc = [A.alloc([16, 64], BF16) for _ in range(2)]
        CML = A.alloc([32], F32)
        RR = A.alloc([32, 64], F32)
        ECB = A.alloc([32, 64], BF16)
        CEND = A.alloc([32], F32)
        SEG = A.alloc([32, 64], F32)
        DEC = A.alloc([32, 64], BF16)
        G = A.alloc([32, 64], BF16)
        CE = A.alloc([32, 64], BF16)
        T1 = A.alloc([16, 64], F32)
        YS = A.alloc([16, 64], F32)
        SQ = A.alloc([16, 64], BF16)
        RS = A.alloc([8, 64], F32)
        AOc = [A.alloc([16, 64], BF16) for _ in range(2)]
        W2 = A.alloc([32], F32)
        XW = A.alloc([32, 64], BF16)
        EE = A.alloc([32], F32)
        X0 = 4 * si
        XK = [f"ps{X0 + i}" for i in range(4)]

        def kx(k):
            if isinstance(k, str) and (k.startswith("ps") or k in SHARED):
                return k
            return (si, k)

        def op(eng, fn, reads=(), writes=(), dma=None):
            return P.op(eng, fn, reads=[kx(k) for k in reads], writes=[kx(k) for k in writes],
                        dma=(None if dma is None else f"{dma}_{si}"))
        ci_ = 0
        for (q0, nch, L, h0, hout) in seqs:
            if h0 is None:
                yield op("pool", R.memset(H[:], 0.0), writes=["h"])
            else:
                yield op("sp", R.dma_start(out=H[:].rearrange("p h q -> p (h q)"), in_=h0), writes=["h"], dma="h0")
            yield op("act", R.activation(out=Hb[:], in_=H[:], func=AF.Copy), reads=["h"], writes=["hb"])
            HPM = 512 // L
            for ci in range(nch):
                tc = q0 + ci * L
                s2 = ci_ % 2
                ci_ += 1
                dk, xhk, xck, ztk = ("dtc", s2), ("xht", s2), ("xcf", s2), ("ztc", s2)
                D_, XH_, XC_, ZT_ = DTc[s2], XHT[s2], XCF[s2], ZTc[s2]
                yield op("sp", R.dma_start(out=D_[:L, :], in_=DTS[tc:tc + L, :]), writes=[dk], dma=f"dtc{s2}")
                yield op("sp", R.dma_start(out=XH_[:L, :], in_=XTM[tc:tc + L, :]), writes=[xhk], dma=f"xht{s2}")
                yield op("sp", R.dma_start(out=XC_[:, :, :L], in_=xc_d[:, :, tc:tc + L]), writes=[xck], dma=f"xcf{s2}")
                yield op("sp", R.dma_start(out=ZT_[:, :, :L], in_=zt_d[:, :, tc:tc + L]), writes=[ztk], dma=f"ztc{s2}")
                pc = C.PSB[X0 + 1]
                yield op("pe", R.matmul(pc[:L, 0:32], lhsT=UT[:L, :L], rhs=D_[:L, 64:96], start=True, stop=True), reads=["ut", dk], writes=[XK[1]])
                yield op("dve", R.tensor_tensor(out=CML[:L, :], in0=pc[:L, 0:32], in1=D_[:L, 32:64], op=ALU.subtract), reads=[XK[1], dk], writes=["cml"])
                yield op("dve", R.tensor_tensor(out=RR[:L, :, :L], in0=D_[:L, 64:96].rearrange("p (h o) -> p h o", o=1).to_broadcast([L, 32, L]),
                                                in1=UT[:L, :L].rearrange("p (o t) -> p o t", o=1).to_broadcast([L, 32, L]), op=ALU.mult),
                         reads=[dk, "ut"], writes=["rr"])
                PBC = C.PSALL[:, X0 * 512:X0 * 512 + 16 * L].rearrange("p (h t) -> p h t", h=16)
                ubk = XK[0:(16 * L + 511) // 512]
                for hf in range(2):
                    h0_ = 16 * hf
                    for hb in range(0, 16, HPM):
                        yield op("pe", R.matmul(PBC[:, hb:hb + HPM, :], lhsT=ONESF[:L, :], rhs=RR[:L, h0_ + hb:h0_ + hb + HPM, :L], start=True, stop=True),
                                 reads=["onesf", "rr"], writes=[f"ps{X0 + (hb * L) // 512}"])
                    yield op("act", R.activation(out=ECB[:, h0_:h0_ + 16, :L], in_=PBC, func=AF.Exp), reads=ubk, writes=["ecb"])
                    yield op("act", R.activation(out=CEND[:, h0_:h0_ + 16], in_=PBC[:, :, L - 1], func=AF.Copy), reads=ubk, writes=["cend"])
                    yield op("dve", R.tensor_tensor(out=SEG[:L, h0_:h0_ + 16, :L], in0=PBC[:L],
                                                    in1=CML[:L, h0_:h0_ + 16].rearrange("p (h o) -> p h o", o=1).to_broadcast([L, 16, L]),
                                                    op=ALU.subtract), reads=ubk + ["cml"], writes=["seg"])
                yield op("pool", R.tensor_tensor(out=SEG[:L, :, :L], in0=SEG[:L, :, :L], in1=NEG32[:L, :, :L], op=ALU.add), reads=["seg", "neg32"], writes=["seg"])
                yield op("act", R.activation(out=DEC[:L, :, :L], in_=SEG[:L, :, :L], func=AF.Exp), reads=["seg"], writes=["dec"])
                pcb = C.PSB[X0 + 2][:, 0:8 * L].rearrange("p (g t) -> p g t", g=8)
                for g in range(8):
                    yield op("pe", R.matmul(pcb[:L, g, :], lhsT=XC_[:, 16 + g, :L], rhs=XC_[:, 24 + g, :L], start=True, stop=True),
                             reads=[xck], writes=[XK[2]])
                Gv = G[:, :, :].rearrange("p (g q) t -> p g q t", q=4)
                DECv = DEC[:, :, :].rearrange("p (g q) t -> p g q t", q=4)
                ECBv = ECB[:, :, :].rearrange("p (g q) t -> p g q t", q=4)
                CEv = CE[:, :, :].rearrange("p (g q) t -> p g q t", q=4)
                for hh in range(4):
                    yield op("dve", R.tensor_tensor(out=Gv[:L, :, hh, :L], in0=pcb[:L], in1=DECv[:L, :, hh, :L], op=ALU.mult),
                             reads=[XK[2], "dec"], writes=["g"])
                    yield op("pool", R.tensor_tensor(out=CEv[:, :, hh, :L], in0=ECBv[:, :, hh, :L], in1=XC_[:, 24:32, :L], op=ALU.mult),
                             reads=["ecb", xck], writes=["ce"])
                ybase = (X0 + 2) * 512
                pyv = C.PSALL[:, ybase:ybase + 16 * L].rearrange("p (c t) -> p c t", c=16)
                pyk = XK[2:2 + (16 * L + 511) // 512]
                for h in range(32):
                    c, par = divmod(h, 2)
                    o_ = pyv[par * 64:(par + 1) * 64, c, :]
                    bkk = [f"ps{X0 + 2 + (c * L) // 512}"]
                    yield op("pe", R.matmul(o_, lhsT=XH_[:L, h * 64:(h + 1) * 64], rhs=G[:L, h, :L], start=True, stop=False),
                             reads=[xhk, "g"], writes=bkk)
                    yield op("pe", R.matmul(o_, lhsT=Hb[:, h, :], rhs=CE[:, h, :L], start=False, stop=True), reads=["hb", "ce"], writes=bkk)
                yield op("pool", R.tensor_tensor(out=T1[:, :, :L], in0=XC_[:, 0:16, :L],
                                                 in1=DSK[:, :].rearrange("p (c o) -> p c o", o=1).to_broadcast([128, 16, L]),
                                                 op=ALU.mult), reads=[xck, "dsk"], writes=["t1"])
                yield op("dve", R.tensor_tensor(out=YS[:, :, :L], in0=pyv, in1=T1[:, :, :L], op=ALU.add), reads=pyk + ["t1"], writes=["ys"])
                yield op("dve", R.tensor_tensor(out=YS[:, :, :L], in0=YS[:, :, :L], in1=ZT_[:, :, :L], op=ALU.mult), reads=["ys", ztk], writes=["ys"])
                yield op("act", R.activation(out=SQ[:, :, :L], in_=YS[:, :, :L], func=AF.Square), reads=["ys"], writes=["sq"])
                pgn = C.PSB[X0][:, 0:8 * L].rearrange("p (g t) -> p g t", g=8)
                for g in range(8):
                    for q in range(2):
                        yield op("pe", R.matmul(pgn[:, g, :], lhsT=ONESG[:, :], rhs=SQ[:, 2 * g + q, :L], start=(q == 0), stop=(q == 1)),
                                 reads=["onesg", "sq"], writes=[XK[0]])
                yield op("act", R.activation(out=RS[:, :, :L], in_=pgn, func=AF.Sqrt, bias=C.epsc[:, 0:1]), reads=[XK[0], "epsc"], writes=["rs"])
                yield op("dve", R.reciprocal(out=RS[:, :, :L], in_=RS[:, :, :L]), reads=["rs"], writes=["rs"])
                AOv = AOc[s2][:, :, :].rearrange("p (g q) t -> p g q t", q=2)
                YSv = YS[:, :, :].rearrange("p (g q) t -> p g q t", q=2)
                for q in range(2):
                    yield op("dve", R.tensor_tensor(out=AOv[:, :, q, :L], in0=YSv[:, :, q, :L], in1=RS[:, :, :L], op=ALU.mult),
                             reads=["ys", "rs"], writes=[("aoc", s2)])
                yield op("sp", R.dma_start(out=ao_d[:, :, tc:tc + L], in_=AOc[s2][:, :, :L]), reads=[("aoc", s2)], dma=f"aoc{s2}")
                yield op("dve", R.tensor_tensor(out=W2[:L, :], in0=CEND[:L, :], in1=CML[:L, :], op=ALU.subtract), reads=["cend", "cml"], writes=["w2"])
                yield op("act", R.activation(out=W2[:L, :], in_=W2[:L, :], func=AF.Exp), reads=["w2"], writes=["w2"])
                yield op("dve", R.tensor_tensor(out=XW[:L, :, :], in0=XH_[:L, 0:2048].rearrange("p (h q) -> p h q", q=64),
                                                in1=W2[:L, :].rearrange("p (h o) -> p h o", o=1).to_broadcast([L, 32, 64]), op=ALU.mult),
                         reads=[xhk, "w2"], writes=["xw"])
                yield op("act", R.activation(out=EE[:, :], in_=CEND[:, :], func=AF.Exp), reads=["cend"], writes=["ee"])
                yield op("dve", R.tensor_tensor(out=H[:, :, :], in0=H[:, :, :], in1=EE[:, :].rearrange("p (h o) -> p h o", o=1).to_broadcast([128, 32, 64]),
                                                op=ALU.mult), reads=["h", "ee"], writes=["h"])
                pst = C.PSALL[:, X0 * 512:X0 * 512 + 1024].rearrange("p (h q) -> p h q", q=64)
                for hf in range(2):
                    for gl in range(4):
                        g = 4 * hf + gl
                        yield op("pe", R.matmul(pst[:, 4 * gl:4 * gl + 4, :], lhsT=XH_[:L, 2048 + g * 128:2048 + (g + 1) * 128], rhs=XW[:L, 4 * g:4 * g + 4, :],
                                                start=True, stop=True), reads=[xhk, "xw"], writes=[XK[gl // 2]])
                    yield op("dve", R.tensor_tensor(out=H[:, 16 * hf:16 * hf + 16, :], in0=pst, in1=H[:, 16 * hf:16 * hf + 16, :], op=ALU.add),
                             reads=XK[0:2] + ["h"], writes=["h"])
                yield op("act", R.activation(out=Hb[:], in_=H[:], func=AF.Copy), reads=["h"], writes=["hb"])
            yield op("sp", R.dma_start(out=hout, in_=H[:].rearrange("p h q -> p (h q)")), reads=["h"], dma="hout")

    allseq = [(sq * S, S // 64, 64, None, C.d_cssm_p[sq]) for sq in range(NP_)]
    samp = (TPc, 1, TS, C.d_c_ssm0, C.d_cssm_s)
    if NP_ >= 2:
        run_streams([c2_stream(0, [allseq[0], samp]), c2_stream(1, allseq[1:])])
    else:
        run_streams([c2_stream(0, allseq), c2_stream(1, [samp])])
    outproj_phase(P, C, "cwout", C.sAO, 16, C.d_c_w_out, Xin, Xout, lnidx, tiles, rowscale=C.d_c_ng)


SC_K = 512.0 ** -0.5


def mixer_d(P, C, Xin, Xout, lnidx):
    S, NP_, TPc, TTc = C.SEQ, C.NPS, C.TP, C.TT
    tiles = C.tiles
    QT, KT, KTM, VTM, OS, GS = C.sD_QT, C.sD_KT, C.sD_KTM, C.sD_VTM, C.sD_OS, C.sD_GS
    P.barrier()
    C.A.reset()
    A = C.A
    st_q = Stager(P, C, "stq", [512], BF16, 4)
    st_t = Stager(P, C, "stt", [512], BF16, 4)
    HALO = A.alloc([16, 3], F32)
    CW = A.alloc([16, 4], F32)
    CB = A.alloc([16], F32)
    GB = A.alloc([8], F32)
    WQ = A.alloc([16, 128], BF16)
    WK = A.alloc([16, 128], BF16)
    XP = [A.alloc([515], F32) for _ in range(3)]
    ACC = [A.alloc([512], F32) for _ in range(2)]
    XS = [A.alloc([4, 512], BF16) for _ in range(2)]
    GT = [A.alloc([8], F32) for _ in range(2)]
    P.op("sp", R.dma_start(out=CW[:], in_=C.d_d_cw), writes=["cw"], dma="cw")
    P.op("sp", R.dma_start(out=CB[:], in_=C.d_d_cb), writes=["cb"], dma="cb")
    P.op("sp", R.dma_start(out=GB[:], in_=C.d_d_gb), writes=["gb"], dma="gb")
    P.op("pool", R.dma_start(out=WQ[:], in_=C.d_d_wq), writes=["wq"], dma="wq")
    P.op("pool", R.dma_start(out=WK[:], in_=C.d_d_wk), writes=["wk"], dma="wk")
    st = {"xp": 0, "g": 0, "b": 0}
    pend = []

    def cb_x(cc):
        def f(ps, pk, t0, N):
            if cc == 0:
                if t0 >= TPc:
                    P.op("sp", R.dma_start(out=HALO[:], in_=C.d_d_conv0.rearrange("(c p) j -> p c j", p=128)), writes=["halo"], dma="halo")
                elif t0 % S == 0:
                    P.op("pool", R.memset(HALO[:], 0.0), writes=["halo"])
            r = st["xp"] % 3
            st["xp"] += 1
            X_ = XP[r]
            xk = ("xp", r)
            P.op("act", R.activation(out=X_[:, 3:3 + N], in_=ps, func=AF.Copy), reads=[pk], writes=[xk])
            P.op("pool", R.tensor_copy(out=X_[:, 0:3], in_=HALO[:, cc, :]), reads=["halo"], writes=[xk])
            a = ACC[cc % 2]
            ak = ("acc", cc % 2)
            P.op("dve", R.tensor_scalar(out=a[:, :N], in0=X_[:, 0:N], scalar1=CW[:, cc, 0:1], scalar2=CB[:, cc:cc + 1],
                                        op0=ALU.mult, op1=ALU.add), reads=[xk, "cw", "cb"], writes=[ak])
            for j in range(1, 4):
                P.op("dve", R.scalar_tensor_tensor(out=a[:, :N], in0=X_[:, j:j + N], scalar=CW[:, cc, j:j + 1], in1=a[:, :N],
                                                   op0=ALU.mult, op1=ALU.add), reads=[xk, "cw", ak], writes=[ak])
            P.op("pool", R.tensor_copy(out=HALO[:, cc, :], in_=X_[:, N:N + 3]), reads=[xk], writes=["halo"])
            while pend:
                pend.pop(0)()
            pend.append(lambda: tail(cc, a, ak, t0, N))
            if cc == 15:
                while pend:
                    pend.pop(0)()

        def tail(cc, a, ak, t0, N):
            gi = (cc // 4) % 2
            xsk = ("xs", gi)
            P.op("act", R.activation(out=XS[gi][:, cc % 4, :N], in_=a[:, :N], func=AF.Silu), reads=[ak], writes=[xsk])
            for (Wm, wkey, dst, sc) in ((WQ, "wq", QT, 1.0), (WK, "wk", KT, SC_K)):
                b = 6 + st["b"] % 2
                st["b"] += 1
                pq = C.PSB[b]
                P.op("pe", R.matmul(pq[:, :N], lhsT=Wm[:, cc, :], rhs=XS[gi][:, cc % 4, :N], start=True, stop=True),
                     reads=[wkey, xsk], writes=[f"ps{b}"])
                B_, k_ = st_q.bufs[st_q.i % st_q.n], (st_q.name, st_q.i % st_q.n)
                s_ = st_q.i % st_q.n
                st_q.i += 1
                P.op("act", R.activation(out=B_[:, :N], in_=pq[:, :N], func=AF.Copy, scale=sc), reads=[f"ps{b}"], writes=[k_])
                P.op("sp", R.dma_start(out=dst[cc * 128:(cc + 1) * 128, t0:t0 + N], in_=B_[:, :N]), reads=[k_], dma=f"stq{s_}")
            if cc % 4 == 3:
                cc0 = cc - 3
                for tb in range((N + 127) // 128):
                    nt = min(128, N - tb * 128)
                    b = 6 + st["b"] % 2
                    st["b"] += 1
                    pq = C.PSB[b]
                    for q in range(4):
                        P.op("pe", R.matmul(pq[:nt, q * 128:(q + 1) * 128], lhsT=XS[gi][:, q, tb * 128:tb * 128 + nt], rhs=WK[:, cc0 + q, :],
                                            start=True, stop=True), reads=[xsk, "wk"], writes=[f"ps{b}"])
                    B_, k_ = st_t.bufs[st_t.i % st_t.n], (st_t.name, st_t.i % st_t.n)
                    s_ = st_t.i % st_t.n
                    st_t.i += 1
                    P.op("act", R.activation(out=B_[:nt, :], in_=pq[:nt, :], func=AF.Copy, scale=SC_K), reads=[f"ps{b}"], writes=[k_])
                    P.op("sp", R.dma_start(out=KTM[t0 + tb * 128:t0 + tb * 128 + nt, cc0 * 128:cc0 * 128 + 512], in_=B_[:nt, :]),
                         reads=[k_], dma=f"stt{s_}")
            if cc == 15:
                if t0 >= TPc:
                    P.op("sp", R.dma_start(out=C.d_dconv_s.rearrange("(c p) j -> p c j", p=128), in_=HALO[:]), reads=["halo"], dma="halo_o")
                elif (t0 + N) % S == 0:
                    P.op("sp", R.dma_start(out=C.d_dconv_p[t0 // S].rearrange("(c p) j -> p c j", p=128), in_=HALO[:]), reads=["halo"], dma="halo_o")
        return f

    def cb_v(q):
        return lambda ps, pk, t0, nt: st_t.put(ps, pk, VTM[t0:t0 + nt, q * 512:(q + 1) * 512], nt, 512, eng="dve")

    def cb_o(q):
        return lambda ps, pk, t0, nt: st_t.put(ps, pk, OS[t0:t0 + nt, q * 512:(q + 1) * 512], nt, 512, func=AF.Sigmoid)

    def cb_g(ps, pk, t0, nt):
        r = st["g"] % 2
        st["g"] += 1
        T_ = GT[r]
        k = ("gt", r)
        P.op("dve", R.tensor_tensor(out=T_[:nt, :], in0=ps, in1=GB[:nt, :], op=ALU.add), reads=[pk, "gb"], writes=[k])
        P.op("act", R.activation(out=T_[:nt, 4:8], in_=T_[:nt, 4:8], func=AF.Exp, scale=-1.0), reads=[k], writes=[k])
        P.op("act", R.activation(out=T_[:nt, 4:8], in_=T_[:nt, 4:8], func=AF.Ln, bias=C.onec[:nt, 0:1]), reads=[k, "onec"], writes=[k])
        P.op("pool", R.tensor_scalar(out=T_[:nt, 4:8], in0=T_[:nt, 4:8], scalar1=-1.0, scalar2=None, op0=ALU.mult), reads=[k], writes=[k])
        P.op("sp", R.dma_start(out=GS[t0:t0 + nt, :], in_=T_[:nt, :]), reads=[k], dma=f"gt{r}")
    fm = [(cc * 128, 128, cb_x(cc)) for cc in range(16)]
    tm = [(2048 + q * 512, 512, cb_v(q)) for q in range(4)] + [(4096 + q * 512, 512, cb_o(q)) for q in range(4)] + [(6144, 8, cb_g)]
    inproj_generic(P, C, "dwin", Xin, C.d_d_w_in, 6152, tiles, fm, tm)

    DSTOP = 99
    if STOP_AFTER_INPROJ:
        return
    P.barrier()
    A.reset()
    UT = A.alloc([64], F32)
    IDF = A.alloc([64], F32)
    NEGU = A.alloc([64], F32)
    NEGU4 = A.alloc([4, 64], F32)
    SEL = {64: A.alloc([128], F32), 32: A.alloc([128], F32)}
    ONESF = A.alloc([64], F32)
    ONEB = A.alloc([2], BF16)
    IDN = A.alloc([128], BF16)
    for (t_, d_, nm) in ((UT, C.d_ut, "ut"), (IDF, C.d_identf, "idf"), (NEGU, C.d_negu, "negu")):
        P.op("sp", R.dma_start(out=t_[0:64, :], in_=d_), writes=[nm], dma=nm)
    P.op("sp", R.dma_start(out=SEL[64][0:64, :], in_=C.d_sel64), writes=["sel"], dma="sel")
    P.op("sp", R.dma_start(out=SEL[32][0:32, :], in_=C.d_sel32), writes=["sel"], dma="sel")
    P.op("pool", R.dma_start(out=IDN[:], in_=C.d_ident), writes=["idn"], dma="idn")
    P.op("dve", R.tensor_copy(out=NEGU4[0:64], in_=NEGU[0:64, :].rearrange("p (o t) -> p o t", o=1).to_broadcast([64, 4, 64])),
         reads=["negu"], writes=["negu4"])
    P.op("pool", R.memset(ONESF[:], 1.0), writes=["onesf"])
    P.op("pool", R.memset(ONEB[:], 1.0), writes=["oneb"])
    qt_d = QT.rearrange("(c p) t -> p c t", p=128)
    kt_d = KT.rearrange("(c p) t -> p c t", p=128)
    ao_d = C.sAO.rearrange("(c p) t -> p c t", p=128)
    SHARED = ("ut", "idf", "negu4", "sel", "onesf", "oneb", "idn", "epsc")

    def d2_stream(si, seqs):
        CST = A.alloc([4, 4, 512], F32)
        CBF = A.alloc([4, 4, 512], BF16)
        NS = A.alloc([4, 4], F32)
        NSB = A.alloc([4, 4], BF16)
        M0 = A.alloc([4], F32)
        GSc = [A.alloc([8], F32) for _ in range(1)]
        QTc = [A.alloc([16, 64], BF16) for _ in range(1)]
        KTc = [A.alloc([16, 64], BF16) for _ in range(1)]
        KMc = [A.alloc([2048], BF16) for _ in range(1)]
        VMc = [A.alloc([2048], BF16) for _ in range(1)]
        OSc = [A.alloc([2048], BF16) for _ in range(1)]
        FC = A.alloc([4], F32)
        RV = A.alloc([4], F32)
        RRr = A.alloc([4, 64], F32)
        DL = A.alloc([4, 64], F32)
        RMX = A.alloc([4], F32)
        INTER = A.alloc([4], F32)
        MM = A.alloc([8], F32)
        NEGM = A.alloc([4], F32)
        DEX = A.alloc([4, 64], F32)
        SM = A.alloc([4, 64], F32)
        SMB = A.alloc([4, 64], BF16)
        STT = A.alloc([4, 64], BF16)
        DENI = A.alloc([4], F32)
        WINT = A.alloc([4], F32)
        DEN = A.alloc([4], F32)
        EM = A.alloc([4], F32)
        RDEN = A.alloc([4], F32)
        WR = A.alloc([4], F32)
        TT_ = [A.alloc([512], F32) for _ in range(2)]
        HN = A.alloc([4, 512], F32)
        MEAN4 = A.alloc([4], F32)
        VAR4 = A.alloc([4], F32)
        HJ = A.alloc([512], BF16)
        HNB = A.alloc([2048], BF16)
        AOc = [A.alloc([16, 64], BF16) for _ in range(2)]
        BC8 = A.alloc([8], F32)
        DD = A.alloc([4], F32)
        WE = A.alloc([4], F32)
        DECAY = A.alloc([4], F32)
        KW = A.alloc([4, 512], BF16)
        Y0 = 4 * si
        YK = [f"ps{Y0 + i}" for i in range(4)]

        def kx(k):
            if isinstance(k, str) and (k.startswith("ps") or k in SHARED):
                return k
            return (si, k)

        def op(eng, fn, reads=(), writes=(), dma=None):
            return P.op(eng, fn, reads=[kx(k) for k in reads], writes=[kx(k) for k in writes],
                        dma=(None if dma is None else f"{dma}_{si}"))
        ci_ = 0
        for (q0, nch, L, init, sq) in seqs:
            if init is None:
                yield op("pool", R.memset(CST[:], 0.0), writes=["cst"])
                yield op("pool", R.memset(NS[:], 0.0), writes=["ns"])
                yield op("pool", R.memset(M0[:], 0.0), writes=["m0"])
            else:
                yield op("sp", R.dma_start(out=CST[:], in_=C.d_d_c0.rearrange("h (kc p) v -> p h kc v", p=128)), writes=["cst"], dma="c0")
                yield op("sp", R.dma_start(out=NS[:].rearrange("p h k -> p (h k)"), in_=C.d_d_n0), writes=["ns"], dma="n0")
                yield op("sp", R.dma_start(out=M0[:], in_=C.d_d_m0), writes=["m0"], dma="m0")
            yield op("act", R.activation(out=CBF[:], in_=CST[:], func=AF.Copy), reads=["cst"], writes=["cbf"])
            yield op("act", R.activation(out=NSB[:], in_=NS[:], func=AF.Copy), reads=["ns"], writes=["nsb"])
            for ci in range(nch):
                tc = q0 + ci * L
                s2 = 0
                ci_ += 1
                gk, qk, kk, kmk, vmk, osk = [(n_, s2) for n_ in ("gsc", "qtc", "ktc", "kmc", "vmc", "osc")]
                G_, Q_, K_, KM_, VM_, OS_ = GSc[s2], QTc[s2], KTc[s2], KMc[s2], VMc[s2], OSc[s2]
                yield op("sp", R.dma_start(out=G_[:L, :], in_=GS[tc:tc + L, :]), writes=[gk], dma=f"gsc{s2}")
                yield op("sp", R.dma_start(out=Q_[:, :, :L], in_=qt_d[:, :, tc:tc + L]), writes=[qk], dma=f"qtc{s2}")
                yield op("sp", R.dma_start(out=K_[:, :, :L], in_=kt_d[:, :, tc:tc + L]), writes=[kk], dma=f"ktc{s2}")
                yield op("sp", R.dma_start(out=KM_[:L, :], in_=KTM[tc:tc + L, :]), writes=[kmk], dma=f"kmc{s2}")
                yield op("sp", R.dma_start(out=VM_[:L, :], in_=VTM[tc:tc + L, :]), writes=[vmk], dma=f"vmc{s2}")
                yield op("sp", R.dma_start(out=OS_[:L, :], in_=OS[tc:tc + L, :]), writes=[osk], dma=f"osc{s2}")
                p7 = C.PSB[Y0 + 3][:, 256:512]
                yield op("pe", R.matmul(p7[:L, 0:4], lhsT=UT[:L, :L], rhs=G_[:L, 4:8], start=True, stop=True), reads=["ut", gk], writes=[YK[3]])
                yield op("act", R.activation(out=MM[:L, 4:8], in_=p7[:L, 0:4], func=AF.Copy), reads=[YK[3]], writes=["fc"])
                yield op("dve", R.tensor_tensor(out=RV[:L, :], in0=G_[:L, 0:4], in1=MM[:L, 4:8], op=ALU.subtract), reads=[gk, "fc"], writes=["rv"])
                yield op("dve", R.tensor_tensor(out=RRr[:L, :, :L], in0=RV[:L, :].rearrange("p (h o) -> p h o", o=1).to_broadcast([L, 4, L]),
                                            in1=IDF[:L, :L].rearrange("p (o t) -> p o t", o=1).to_broadcast([L, 4, L]), op=ALU.mult),
                     reads=["rv", "idf"], writes=["rrr"])
                p6 = C.PSB[Y0 + 3][:, 0:4 * L].rearrange("p (h t) -> p h t", h=4)
                yield op("pe", R.matmul(p6[:L], lhsT=ONESF[:L, :L], rhs=RRr[:L, :, :L], start=True, stop=True), reads=["onesf", "rrr"], writes=[YK[3]])
                yield op("dve", R.tensor_tensor(out=DL[:L, :, :L], in0=p6[:L], in1=MM[:L, 4:8].rearrange("p (h o) -> p h o", o=1).to_broadcast([L, 4, L]),
                                            op=ALU.add), reads=[YK[3], "fc"], writes=["dl"])
                yield op("pool", R.tensor_tensor(out=DL[:L, :, :L], in0=DL[:L, :, :L], in1=NEGU4[:L, :, :L], op=ALU.add), reads=["dl", "negu4"], writes=["dl"])
                yield op("dve", R.tensor_reduce(out=RMX[:L, :], in_=DL[:L, :, :L], axis=mybir.AxisListType.X, op=ALU.max), reads=["dl"], writes=["rmx"])
                yield op("dve", R.tensor_tensor(out=INTER[:L, :], in0=MM[:L, 4:8], in1=M0[:L, :], op=ALU.add), reads=["fc", "m0"], writes=["inter"])
                yield op("dve", R.tensor_tensor(out=MM[:L, 0:4], in0=RMX[:L, :], in1=INTER[:L, :], op=ALU.max), reads=["rmx", "inter"], writes=["mm"])
                yield op("pool", R.tensor_scalar(out=NEGM[:L, :], in0=MM[:L, 0:4], scalar1=-1.0, scalar2=None, op0=ALU.mult), reads=["mm"], writes=["negm"])
                for h in range(4):
                    yield op("act", R.activation(out=DEX[:L, h, :L], in_=DL[:L, h, :L], func=AF.Exp, bias=NEGM[:L, h:h + 1]),
                         reads=["dl", "negm"], writes=["dex"])
                if DSTOP <= 2:
                    continue
                p5 = C.PSB[Y0 + 2][:, 0:4 * L].rearrange("p (h t) -> p h t", h=4)
                for h in range(4):
                    for kc in range(4):
                        yield op("pe", R.matmul(p5[:L, h, :], lhsT=Q_[:, 4 * h + kc, :L], rhs=K_[:, 4 * h + kc, :L], start=(kc == 0), stop=(kc == 3)),
                             reads=[qk, kk], writes=[YK[2]])
                yield op("dve", R.tensor_tensor(out=SM[:L, :, :L], in0=p5[:L], in1=DEX[:L, :, :L], op=ALU.mult), reads=[YK[2], "dex"], writes=["sm"])
                yield op("dve", R.tensor_reduce(out=DENI[:L, :], in_=SM[:L, :, :L], axis=mybir.AxisListType.X, op=ALU.add), reads=["sm"], writes=["deni"])
                yield op("act", R.activation(out=SMB[:L, :, :L], in_=SM[:L, :, :L], func=AF.Copy), reads=["sm"], writes=["smb"])
                pT = C.PSALL.bitcast(BF16)[:, (Y0 + 2) * 1024:(Y0 + 2) * 1024 + 1024]
                pTv = pT[:, 0:4 * L].rearrange("p (h t) -> p h t", h=4)
                for h in range(4):
                    yield op("pe", R.transpose(out=pTv[:L, h, :], in_=SMB[:L, h, :L], identity=IDN[:L, :L]), reads=["smb", "idn"], writes=[YK[2]])
                yield op("act", R.activation(out=STT[:L, :, :L], in_=pTv[:L], func=AF.Copy), reads=[YK[2]], writes=["stt"])
                yield op("dve", R.tensor_tensor(out=WINT[:L, :], in0=INTER[:L, :], in1=MM[:L, 0:4], op=ALU.subtract), reads=["inter", "mm"], writes=["wint"])
                yield op("act", R.activation(out=WINT[:L, :], in_=WINT[:L, :], func=AF.Exp), reads=["wint"], writes=["wint"])
                for h in range(4):
                    for kc in range(4):
                        yield op("pe", R.matmul(p7[:L, 8 + h:9 + h], lhsT=Q_[:, 4 * h + kc, :L], rhs=NSB[:, h, kc:kc + 1], start=(kc == 0), stop=(kc == 3)),
                             reads=[qk, "nsb"], writes=[YK[3]])
                yield op("dve", R.tensor_tensor(out=DEN[:L, :], in0=p7[:L, 8:12], in1=WINT[:L, :], op=ALU.mult), reads=[YK[3], "wint"], writes=["den"])
                yield op("dve", R.tensor_tensor(out=DEN[:L, :], in0=DEN[:L, :], in1=DENI[:L, :], op=ALU.add), reads=["den", "deni"], writes=["den"])
                yield op("pool", R.tensor_scalar(out=WR[:L, :], in0=DEN[:L, :], scalar1=-1.0, scalar2=None, op0=ALU.mult), reads=["den"], writes=["wr"])
                yield op("dve", R.tensor_tensor(out=DEN[:L, :], in0=DEN[:L, :], in1=WR[:L, :], op=ALU.max), reads=["den", "wr"], writes=["den"])
                yield op("act", R.activation(out=EM[:L, :], in_=NEGM[:L, :], func=AF.Exp), reads=["negm"], writes=["em"])
                yield op("dve", R.tensor_tensor(out=DEN[:L, :], in0=DEN[:L, :], in1=EM[:L, :], op=ALU.max), reads=["den", "em"], writes=["den"])
                yield op("dve", R.reciprocal(out=RDEN[:L, :], in_=DEN[:L, :]), reads=["den"], writes=["rden"])
                yield op("dve", R.tensor_tensor(out=WR[:L, :], in0=WINT[:L, :], in1=RDEN[:L, :], op=ALU.mult), reads=["wint", "rden"], writes=["wr"])
                if DSTOP <= 3:
                    continue
                for hp in range(2):
                    for hl in range(2):
                        h = 2 * hp + hl
                        yield op("pe", R.matmul(C.PSB[Y0][:L, :], lhsT=STT[:L, h, :L], rhs=VM_[:L, h * 512:(h + 1) * 512], start=True, stop=True),
                             reads=["stt", vmk], writes=[YK[0]])
                        for kc in range(4):
                            yield op("pe", R.matmul(C.PSB[Y0 + 1][:L, :], lhsT=Q_[:, 4 * h + kc, :L], rhs=CBF[:, h, kc, :], start=(kc == 0), stop=(kc == 3)),
                                 reads=[qk, "cbf"], writes=[YK[1]])
                        T_ = TT_[hl]
                        yield op("act", R.activation(out=T_[:L, :], in_=C.PSB[Y0 + 1][:L, :], func=AF.Copy, scale=WR[:L, h:h + 1]),
                             reads=[YK[1], "wr"], writes=[("tt", hl)])
                        yield op("dve", R.scalar_tensor_tensor(out=HN[:L, h, :], in0=C.PSB[Y0][:L, :], scalar=RDEN[:L, h:h + 1], in1=T_[:L, :],
                                                           op0=ALU.mult, op1=ALU.add), reads=[YK[0], "rden", ("tt", hl)], writes=["hn"])
                if DSTOP <= 4:
                    continue
                yield op("dve", R.tensor_tensor(out=HN[:L, :, :], in0=HN[:L, :, :], in1=OS_[:L, :].rearrange("p (h v) -> p h v", h=4), op=ALU.mult),
                     reads=["hn", osk], writes=["hn"])
                yield op("dve", R.tensor_reduce(out=MEAN4[:L, :], in_=HN[:L, :, :], axis=mybir.AxisListType.X, op=ALU.add), reads=["hn"], writes=["mean4"])
                yield op("pool", R.tensor_scalar(out=MEAN4[:L, :], in0=MEAN4[:L, :], scalar1=-1.0 / 512.0, scalar2=None, op0=ALU.mult), reads=["mean4"], writes=["mean4"])
                yield op("dve", R.tensor_tensor(out=HN[:L, :, :], in0=HN[:L, :, :], in1=MEAN4[:L, :].rearrange("p (h o) -> p h o", o=1).to_broadcast([L, 4, 512]),
                                            op=ALU.add), reads=["hn", "mean4"], writes=["hn"])
                for h in range(4):
                    yield op("act", R.activation(out=HJ[:L, :], in_=HN[:L, h, :], func=AF.Square, accum_out=VAR4[:L, h:h + 1]),
                         reads=["hn"], writes=["hj", "var4"])
                yield op("act", R.activation(out=VAR4[:L, :], in_=VAR4[:L, :], func=AF.Sqrt, bias=C.epsc[:L, 0:1], scale=1.0 / 512.0),
                     reads=["var4", "epsc"], writes=["var4"])
                yield op("dve", R.reciprocal(out=VAR4[:L, :], in_=VAR4[:L, :]), reads=["var4"], writes=["var4"])
                yield op("dve", R.tensor_tensor(out=HNB[:L, :].rearrange("p (h v) -> p h v", h=4), in0=HN[:L, :, :],
                                            in1=VAR4[:L, :].rearrange("p (h o) -> p h o", o=1).to_broadcast([L, 4, 512]), op=ALU.mult),
                     reads=["hn", "var4"], writes=["hnb"])
                pA = C.PSALL.bitcast(BF16)[:, (Y0 + 2) * 1024:(Y0 + 2) * 1024 + 16 * L].rearrange("p (c t) -> p c t", c=16)
                for c in range(16):
                    yield op("pe", R.transpose(out=pA[:, c, :], in_=HNB[:L, c * 128:(c + 1) * 128], identity=IDN[:L, :L]), reads=["hnb", "idn"], writes=[YK[2]])
                yield op("act", R.activation(out=AOc[s2][:, :, :L], in_=pA, func=AF.Copy), reads=[YK[2]], writes=[("aoc", s2)])
                yield op("sp", R.dma_start(out=ao_d[:, :, tc:tc + L], in_=AOc[s2][:, :, :L]), reads=[("aoc", s2)], dma=f"aoc{s2}")
                if DSTOP <= 5:
                    continue
                yield op("pe", R.matmul(p7[:, 16:24], lhsT=SEL[L][:L, :], rhs=MM[:L, :], start=True, stop=True), reads=["sel", "mm", "fc"], writes=[YK[3]])
                yield op("act", R.activation(out=BC8[:, :], in_=p7[:, 16:24], func=AF.Copy), reads=[YK[3]], writes=["bc8"])
                yield op("dve", R.tensor_tensor(out=DD[:, :], in0=BC8[:, 4:8], in1=BC8[:, 0:4], op=ALU.subtract), reads=["bc8"], writes=["dd"])
                yield op("dve", R.tensor_tensor(out=WE[:L, :], in0=RV[:L, :], in1=DD[:L, :], op=ALU.add), reads=["rv", "dd"], writes=["we"])
                yield op("act", R.activation(out=WE[:L, :], in_=WE[:L, :], func=AF.Exp), reads=["we"], writes=["we"])
                yield op("dve", R.tensor_tensor(out=DECAY[:, :], in0=DD[:, :], in1=M0[:, :], op=ALU.add), reads=["dd", "m0"], writes=["decay"])
                yield op("act", R.activation(out=DECAY[:, :], in_=DECAY[:, :], func=AF.Exp), reads=["decay"], writes=["decay"])
                yield op("pool", R.tensor_copy(out=M0[:, :], in_=BC8[:, 0:4]), reads=["bc8"], writes=["m0"])
                yield op("dve", R.tensor_tensor(out=KW[:L, :, :], in0=KM_[:L, :].rearrange("p (h k) -> p h k", h=4),
                                            in1=WE[:L, :].rearrange("p (h o) -> p h o", o=1).to_broadcast([L, 4, 512]), op=ALU.mult),
                     reads=[kmk, "we"], writes=["kw"])
                if DSTOP <= 6:
                    continue
                bi = 0
                for h in range(4):
                    for kc in range(4):
                        b = Y0 + bi % 2
                        bi += 1
                        yield op("pe", R.matmul(C.PSB[b][:, :], lhsT=KW[:L, h, kc * 128:(kc + 1) * 128], rhs=VM_[:L, h * 512:(h + 1) * 512],
                                            start=True, stop=True), reads=["kw", vmk], writes=[f"ps{b}"])
                        yield op("dve", R.scalar_tensor_tensor(out=CST[:, h, kc, :], in0=CST[:, h, kc, :], scalar=DECAY[:, h:h + 1], in1=C.PSB[b][:, :],
                                                           op0=ALU.mult, op1=ALU.add), reads=[f"ps{b}", "decay", "cst"], writes=["cst"])
                        if DSTOP <= 7:
                            continue
                        yield op("pe", R.matmul(p7[:, 32 + 2 * (4 * h + kc):34 + 2 * (4 * h + kc)], lhsT=KW[:L, h, kc * 128:(kc + 1) * 128], rhs=ONEB[:L, :],
                                            start=True, stop=True), reads=["kw", "oneb"], writes=[YK[3]])
                yield op("dve", R.tensor_tensor(out=NS[:, :, :], in0=NS[:, :, :], in1=DECAY[:, :].rearrange("p (h o) -> p h o", o=1).to_broadcast([128, 4, 4]),
                                            op=ALU.mult), reads=["ns", "decay"], writes=["ns"])
                yield op("dve", R.tensor_tensor(out=NS[:, :, :], in0=p7[:, 32:64].rearrange("p (h k two) -> p h k two", h=4, two=2)[:, :, :, 0], in1=NS[:, :, :], op=ALU.add),
                     reads=[YK[3], "ns"], writes=["ns"])
                yield op("act", R.activation(out=CBF[:], in_=CST[:], func=AF.Copy), reads=["cst"], writes=["cbf"])
                yield op("act", R.activation(out=NSB[:], in_=NS[:], func=AF.Copy), reads=["ns"], writes=["nsb"])
            dc, dn, dm = (C.d_dc_s, C.d_dn_s, C.d_dm_s) if sq is None else (C.d_dc_p[sq], C.d_dn_p[sq], C.d_dm_p[sq])
            yield op("sp", R.dma_start(out=dc.rearrange("h (kc p) v -> p h kc v", p=128), in_=CST[:]), reads=["cst"], dma="dco")
            yield op("sp", R.dma_start(out=dn, in_=NS[:].rearrange("p h k -> p (h k)")), reads=["ns"], dma="dno")
            yield op("sp", R.dma_start(out=dm, in_=M0[0:1, :]), reads=["m0"], dma="dmo")

    allseq = [(sq * S, S // 64, 64, None, sq) for sq in range(NP_)]
    samp = (TPc, 1, TS, True, None)
    if NP_ >= 2:
        run_streams([d2_stream(0, [allseq[0], samp]), d2_stream(1, allseq[1:])])
    else:
        run_streams([d2_stream(0, allseq), d2_stream(1, [samp])])
    outproj_phase(P, C, "dwout", C.sAO, 16, C.d_d_w_out, Xin, Xout, lnidx, tiles, rowscale=C.d_d_ng)


def build(cfg):
    nc = bass.Bass("TRN2", target_bir_lowering=False)
    es = ExitStack()
    P = Prog(nc, es)
    C = Ctx()
    C.SEQ = cfg.get("seq", SEQ)
    C.NPS = cfg.get("nps", NPS)
    C.TP = C.SEQ * C.NPS
    C.TT = C.TP + TS
    S, TPc, tt = C.SEQ, C.TP, C.TT
    C.tiles = [(i * 512, 512) for i in range(TPc // 512)] + [(TPc, TS)]
    phases = cfg.get("phases", None)

    def din(name, shape, dt=F32):
        return nc.dram_tensor(name, list(shape), dt, kind="ExternalInput").ap()

    def dout(name, shape, dt=F32):
        return nc.dram_tensor(name, list(shape), dt, kind="ExternalOutput").ap()

    def dscr(name, shape, dt):
        return nc.dram_tensor(name, list(shape), dt).ap()

    C.d_x = din("x_in", [D, tt])
    C.d_lng = din("ln_g", [128, 12, KD])
    C.d_lnb = din("ln_b", [128, 12, KD])
    C.d_wg = [din(f"wg{i}", [D, DFF]) for i in range(8)]
    C.d_wu = [din(f"wu{i}", [D, DFF]) for i in range(8)]
    C.d_wd = [din(f"wd{i}", [DFF, D]) for i in range(8)]
    C.d_y = dout("y_out", [D, tt])
    C.sX = [dscr(f"sx{i}", [D, tt], F32) for i in range(2)]
    C.sAO = dscr("s_ao", [2048, tt], BF16)
    keep = min(512, S)
    C.d_b_w_in = din("b_w_in", [D, 3072])
    C.d_b_w_out = din("b_w_out", [D, D])
    C.d_b_bias = din("b_bias", [128, 16, 2, 128])
    C.d_b_far = din("b_far", [128, 16])
    C.d_cbkT = din("cache_b_kT", [1024, 512])
    C.d_cbk_k = din("cache_b_k", [512, 1024])
    C.d_cbk_v = din("cache_b_v", [512, 1024])
    C.d_bkp = dout("b_k_p", [C.NPS, keep, 1024])
    C.d_bvp = dout("b_v_p", [C.NPS, keep, 1024])
    C.d_bks = dout("b_k_s", [512, 1024])
    C.d_bvs = dout("b_v_s", [512, 1024])
    C.sB_QT = dscr("sb_qt", [1024, tt], BF16)
    C.sB_KT = dscr("sb_kt", [1024, tt + 512], BF16)
    C.sB_V = dscr("sb_v", [tt + 512, 1024], BF16)
    C.d_a_w_in = din("a_w_in", [D, 2120])
    C.d_a_w_out = din("a_w_out", [D, D])
    C.d_a_bias = din("a_bias", [128, 3, 8, 128])
    C.d_ident = din("ident", [128, 128])
    C.d_cakT = din("cache_a_kT", [256, 1024])
    C.d_cav = din("cache_a_v", [1024, 256])
    C.d_cakiT = din("cache_a_kiT", [64, 1024])
    C.d_akp = dout("a_k_p", [C.NPS, S, 256])
    C.d_avp = dout("a_v_p", [C.NPS, S, 256])
    C.d_aip = dout("a_kidx_p", [C.NPS, S, 64])
    C.d_aks = dout("a_k_s", [TS, 256])
    C.d_avs = dout("a_v_s", [TS, 256])
    C.d_ais = dout("a_kidx_s", [TS, 64])
    C.sA_QT = dscr("sa_qt", [1024, tt], BF16)
    C.sA_KT = dscr("sa_kt", [256, tt + 1024], BF16)
    C.sA_V = dscr("sa_v", [tt + 1024, 256], BF16)
    C.sA_QIT = dscr("sa_qit", [512, tt], BF16)
    C.sA_KIT = dscr("sa_kit", [64, tt + 1024], BF16)
    C.sA_WI = dscr("sa_wi", [tt, 8], F32)
    C.d_c_w_in = din("c_w_in", [D, 6176])
    C.d_c_w_out = din("c_w_out", [2048, D])
    C.d_c_cw = din("c_cw", [128, 32, 4])
    C.d_c_cb = din("c_cb", [128, 32])
    C.d_c_dtb = din("c_dtb", [128, 32])
    C.d_c_alog = din("c_alog", [128, 32])
    C.d_c_dsk = din("c_dsk", [128, 16])
    C.d_c_ng = din("c_ng", [128, 16])
    C.d_ut = din("ut", [64, 64])
    C.d_negt = din("negt", [64, 64])
    C.d_c_conv0 = din("c_conv0", [4096, 3])
    C.d_c_ssm0 = din("c_ssm0", [128, 2048])
    C.d_cssm_p = dout("c_ssm_p", [C.NPS, 128, 2048])
    C.d_cssm_s = dout("c_ssm_s", [128, 2048])
    C.d_cconv_p = dout("c_conv_p", [C.NPS, 4096, 3])
    C.d_cconv_s = dout("c_conv_s", [4096, 3])
    C.sC_ZT = dscr("sc_zt", [2048, tt], BF16)
    C.sC_XC = dscr("sc_xc", [4096, tt], BF16)
    C.sC_XTM = dscr("sc_xtm", [tt, 3072], BF16)
    C.sC_DTS = dscr("sc_dts", [tt, 96], F32)
    C.d_d_w_in = din("d_w_in", [D, 6152])
    C.d_d_w_out = din("d_w_out", [2048, D])
    C.d_d_cw = din("d_cw", [128, 16, 4])
    C.d_d_cb = din("d_cb", [128, 16])
    C.d_d_gb = din("d_gb", [128, 8])
    C.d_d_wq = din("d_wq", [128, 16, 128])
    C.d_d_wk = din("d_wk", [128, 16, 128])
    C.d_d_ng = din("d_ng", [128, 16])
    C.d_identf = din("identf", [64, 64])
    C.d_negu = din("negu", [64, 64])
    C.d_sel64 = din("sel64", [64, 128])
    C.d_sel32 = din("sel32", [32, 128])
    C.d_d_conv0 = din("d_conv0", [2048, 3])
    C.d_d_c0 = din("d_c0", [4, 512, 512])
    C.d_d_n0 = din("d_n0", [128, 16])
    C.d_d_m0 = din("d_m0", [128, 4])
    C.d_dc_p = dout("d_c_p", [C.NPS, 4, 512, 512])
    C.d_dn_p = dout("d_n_p", [C.NPS, 128, 16])
    C.d_dm_p = dout("d_m_p", [C.NPS, 1, 4])
    C.d_dconv_p = dout("d_conv_p", [C.NPS, 2048, 3])
    C.d_dc_s = dout("d_c_s", [4, 512, 512])
    C.d_dn_s = dout("d_n_s", [128, 16])
    C.d_dm_s = dout("d_m_s", [1, 4])
    C.d_dconv_s = dout("d_conv_s", [2048, 3])
    C.sD_QT = dscr("sd_qt", [2048, tt], BF16)
    C.sD_KT = dscr("sd_kt", [2048, tt], BF16)
    C.sD_KTM = dscr("sd_ktm", [tt, 2048], BF16)
    C.sD_VTM = dscr("sd_vtm", [tt, 2048], BF16)
    C.sD_OS = dscr("sd_os", [tt, 2048], BF16)
    C.sD_GS = dscr("sd_gs", [tt, 8], F32)
    with es:
        setup_common(P, C)
        if phases is None:
            phases = []
            for l in range(4):
                phases += [("ffn", 2 * l), ("mix", l), ("ffn", 2 * l + 1)]
        cur = C.d_x
        for pi, ph in enumerate(phases):
            nxt = C.d_y if pi == len(phases) - 1 else C.sX[pi % 2]
            if ph[0] == "ffn":
                i = ph[1]
                l, which = divmod(i, 2)
                ffn_phase(P, C, i, cur, nxt, C.d_wg[i], C.d_wu[i], C.d_wd[i], 3 * l + (0 if which == 0 else 2), C.tiles)
            elif ph[0] == "mix":
                l = ph[1]
                [mixer_a, mixer_b, mixer_c, mixer_d][l](P, C, cur, nxt, 3 * l + 1)
            cur = nxt
        P.emit()
    return nc, P


def _fm(a):
    return np.ascontiguousarray(a.T)


def prep_core_inputs(inp, c, S=SEQ, nps=NPS):
    f = np.float32
    m = {}
    xp = inp["x_prompt"][c * nps:(c + 1) * nps].reshape(nps * S, D)
    xs = inp["x_sample"][c]
    m["x_in"] = np.ascontiguousarray(np.concatenate([xp, xs], 0).T)
    m["ln_g"] = np.ascontiguousarray(inp["ln_g"].reshape(12, KD, 128).transpose(2, 0, 1))
    m["ln_b"] = np.ascontiguousarray(inp["ln_b"].reshape(12, KD, 128).transpose(2, 0, 1))
    for l in range(4):
        for w, nm in ((0, "ffn1"), (1, "ffn2")):
            m[f"wg{2 * l + w}"] = inp[f"{nm}_wg"][l]
            m[f"wu{2 * l + w}"] = inp[f"{nm}_wu"][l]
            m[f"wd{2 * l + w}"] = inp[f"{nm}_wd"][l]
    m["b_w_in"] = inp["b_w_in"]
    m["b_w_out"] = inp["b_w_out"]
    tab = inp["b_rel_table"]
    ss = np.arange(128)[:, None]
    tt_ = np.arange(128)[None, :]
    bt = np.empty((128, 16, 2, 128), f)
    for d in range(2):
        idx = np.minimum(128 * (1 + d) + tt_ - ss, 256)
        bt[:, :, d, :] = tab[:, idx].transpose(1, 0, 2)
    m["b_bias"] = bt
    m["b_far"] = np.ascontiguousarray(np.broadcast_to(tab[:, 256][None, :], (128, 16)))
    ck = inp["cache_b_k"][c].reshape(512, 1024)
    m["cache_b_kT"] = _fm(ck)
    m["cache_b_k"] = np.ascontiguousarray(ck)
    m["cache_b_v"] = np.ascontiguousarray(inp["cache_b_v"][c].reshape(512, 1024))
    m["a_w_in"] = inp["a_w_in"]
    m["a_w_out"] = inp["a_w_out"]
    t5 = inp["t5_table"]
    ab = np.empty((128, 3, 8, 128), f)
    for d in range(2):
        rel = ss - tt_ - 128 * d
        ab[:, d, :, :] = t5[_t5_bucket(rel)].transpose(0, 2, 1)
    ab[:, 2, :, :] = t5[15][None, :, None]
    m["a_bias"] = ab
    m["ident"] = np.eye(128, dtype=f)
    m["d_w_in"] = inp["d_w_in"]
    m["d_w_out"] = inp["d_w_out"]
    m["d_cw"] = np.ascontiguousarray(inp["d_conv_w"].reshape(4, 16, 128).transpose(2, 1, 0))
    m["d_cb"] = np.ascontiguousarray(inp["d_conv_b"].reshape(16, 128).T)
    m["d_gb"] = np.ascontiguousarray(np.broadcast_to(inp["d_gate_b"][None, :], (128, 8)))
    for nm, src in (("d_wq", inp["d_wq_blk"]), ("d_wk", inp["d_wk_blk"])):
        bd = np.zeros((16, 32, 4, 32, 4), f)
        blk = src.reshape(16, 32, 4, 4)
        for b_ in range(32):
            bd[:, b_, :, b_, :] = blk[:, b_]
        m[nm] = np.ascontiguousarray(bd.reshape(16, 128, 128).transpose(1, 0, 2))
    m["d_ng"] = np.ascontiguousarray(inp["d_norm_g"].reshape(16, 128).T)
    m["identf"] = np.eye(64, dtype=f)
    m["negu"] = np.triu(np.full((64, 64), NEG, f), 1)
    s64 = np.zeros((64, 128), f); s64[63] = 1.0
    s32 = np.zeros((32, 128), f); s32[31] = 1.0
    m["sel64"], m["sel32"] = s64, s32
    m["d_conv0"] = _fm(inp["state_d_conv"][c])
    m["d_c0"] = np.ascontiguousarray(inp["state_d_c"][c])
    m["d_n0"] = np.ascontiguousarray(inp["state_d_n"][c].reshape(16, 128).T)
    m["d_m0"] = np.ascontiguousarray(np.broadcast_to(inp["state_d_m"][c][None, :], (128, 4)))
    m["c_w_in"] = inp["c_w_in"]
    m["c_w_out"] = inp["c_w_out"]
    m["c_cw"] = np.ascontiguousarray(inp["c_conv_w"].reshape(4, 32, 128).transpose(2, 1, 0))
    m["c_cb"] = np.ascontiguousarray(inp["c_conv_b"].reshape(32, 128).T)
    m["c_dtb"] = np.ascontiguousarray(np.broadcast_to(inp["c_dt_bias"][None, :], (128, 32)))
    m["c_alog"] = np.ascontiguousarray(np.broadcast_to(inp["c_a_log"][None, :], (128, 32)))
    m["c_dsk"] = np.ascontiguousarray(np.repeat(inp["c_d_skip"].reshape(16, 2), 64, axis=1).T)
    m["c_ng"] = np.ascontiguousarray(inp["c_norm_g"].reshape(16, 128).T)
    m["ut"] = np.triu(np.ones((64, 64), f))
    m["negt"] = np.tril(np.full((64, 64), NEG, f), -1)
    m["c_conv0"] = _fm(inp["state_c_conv"][c])
    m["c_ssm0"] = np.ascontiguousarray(inp["state_c_ssm"][c].reshape(2048, 128).T)
    m["cache_a_kT"] = _fm(inp["cache_a_k"][c].reshape(1024, 256))
    m["cache_a_v"] = np.ascontiguousarray(inp["cache_a_v"][c].reshape(1024, 256))
    m["cache_a_kiT"] = _fm(inp["cache_a_kidx"][c])
    return m


def _t5_bucket(rel):
    half, max_exact = 16, 8
    ret = np.where(rel > 0, half, 0)
    n = np.abs(rel)
    nf = np.maximum(n, 1).astype(np.float32)
    large = max_exact + (np.log(nf / np.float32(max_exact)) / np.float32(np.log(128 / 8)) * np.float32(half - max_exact)).astype(np.int32)
    large = np.minimum(large, half - 1)
    return ret + np.where(n < max_exact, n, large)


_OUT_ORDER = ("y_prompt", "y_sample", "a_k_p", "a_v_p", "a_kidx_p", "b_k_p", "b_v_p", "c_ssm_p", "c_conv_p",
              "d_c_p", "d_n_p", "d_m_p", "d_conv_p", "a_k_s", "a_v_s", "a_kidx_s", "b_k_s", "b_v_s",
              "c_ssm_s", "c_conv_s", "d_c_s", "d_n_s", "d_m_s", "d_conv_s")


def assemble(results, S=SEQ, nps=NPS):
    o = {k: [] for k in _OUT_ORDER}
    keep = min(512, S)
    for r in results:
        y = r["y_out"].T
        o["y_prompt"].append(y[:nps * S].reshape(nps, S, D))
        o["y_sample"].append(y[nps * S:].reshape(1, TS, D))
        o["a_k_p"].append(r["a_k_p"].reshape(nps, S, 2, 128))
        o["a_v_p"].append(r["a_v_p"].reshape(nps, S, 2, 128))
        o["a_kidx_p"].append(r["a_kidx_p"].reshape(nps, S, 64))
        o["b_k_p"].append(r["b_k_p"].reshape(nps, keep, 16, 64))
        o["b_v_p"].append(r["b_v_p"].reshape(nps, keep, 16, 64))
        o["c_ssm_p"].append(r["c_ssm_p"].reshape(nps, 128, 32, 64).transpose(0, 2, 3, 1))
        o["c_conv_p"].append(r["c_conv_p"].transpose(0, 2, 1))
        o["d_c_p"].append(r["d_c_p"])
        o["d_n_p"].append(r["d_n_p"].transpose(0, 2, 1).reshape(nps, 4, 512))
        o["d_m_p"].append(r["d_m_p"].reshape(nps, 4))
        o["d_conv_p"].append(r["d_conv_p"].transpose(0, 2, 1))
        o["a_k_s"].append(r["a_k_s"].reshape(1, TS, 2, 128))
        o["a_v_s"].append(r["a_v_s"].reshape(1, TS, 2, 128))
        o["a_kidx_s"].append(r["a_kidx_s"].reshape(1, TS, 64))
        o["b_k_s"].append(r["b_k_s"].reshape(1, 512, 16, 64))
        o["b_v_s"].append(r["b_v_s"].reshape(1, 512, 16, 64))
        o["c_ssm_s"].append(r["c_ssm_s"].reshape(1, 128, 32, 64).transpose(0, 2, 3, 1))
        o["c_conv_s"].append(r["c_conv_s"].T[None])
        o["d_c_s"].append(r["d_c_s"][None])
        o["d_n_s"].append(r["d_n_s"].T.reshape(1, 4, 512))
        o["d_m_s"].append(r["d_m_s"].reshape(1, 4))
        o["d_conv_s"].append(r["d_conv_s"].T[None])
    return tuple(np.ascontiguousarray(np.concatenate(o[k], 0), dtype=np.float32) for k in _OUT_ORDER)


def kernel(**inputs):
    inp = {k: np.asarray(v) for k, v in inputs.items()}
    nc, _ = build({})
    in_maps = [prep_core_inputs(inp, c) for c in range(NCORES)]
    res = run_bass_kernel_spmd(nc, in_maps, core_ids=list(range(NCORES)))
    return assemble(res.results)
```
